# Optimizing a Trainium2 kernel written in Bass

```python
import jax, jax.numpy as jnp
from jax import lax
import numpy as np

D_MODEL = 1024
BATCH = 8
SEQ = 4096
DEPTH = 1

PLE_DIM = 256
HG_HEADS = 4
HG_DK = 128
HG_DV = 128
HG_KW = HG_HEADS * HG_DK
HG_VW = HG_HEADS * HG_DV
HG_CHUNK = 64
FOX_HEADS = 8
FOX_DH = 64
FOX_W = FOX_HEADS * FOX_DH
FOX_BLOCK = 128
D_FF = ((-(-8 * D_MODEL // 3) + 255) // 256) * 256
EPS = 1e-6

OFF_HG_Q = 0
OFF_HG_F = OFF_HG_Q + HG_KW
OFF_HG_I = OFF_HG_F + HG_KW
OFF_HG_G = OFF_HG_I + HG_VW
OFF_FOX_Q = OFF_HG_G + HG_VW
OFF_FOX_K = OFF_FOX_Q + FOX_W
OFF_FOX_V = OFF_FOX_K + FOX_W
OFF_FOX_F = OFF_FOX_V + FOX_W
OFF_GATE = OFF_FOX_F + FOX_HEADS
IN_COLS = OFF_GATE + 2 * D_MODEL

kernel_name = "hgrn2_fox_gated_hybrid"


def rms_norm(x, g):
    xf = x.astype(jnp.float32)
    y = xf * lax.rsqrt(jnp.mean(xf * xf, axis=-1, keepdims=True) + EPS)
    return (y * g.astype(jnp.float32)).astype(x.dtype)


def hgrn2_mixer(q, f_logit, i, g, lb, o_gain):
    B, S, _ = q.shape
    f32 = jnp.float32
    qf = jax.nn.silu(q.astype(f32))
    fg = lb + (1.0 - lb) * jax.nn.sigmoid(f_logit.astype(f32))
    kf = 1.0 - fg
    logf = jnp.log(fg)
    vf = i.astype(f32)
    n = S // HG_CHUNK

    def to_chunks(t, d):
        return t.reshape(B, n, HG_CHUNK, HG_HEADS, d).transpose(1, 0, 3, 2, 4)

    qc = to_chunks(qf, HG_DK)
    kc = to_chunks(kf, HG_DK)
    vc = to_chunks(vf, HG_DV)
    bc = jnp.cumsum(to_chunks(logf, HG_DK), axis=3)
    causal = jnp.tril(jnp.ones((HG_CHUNK, HG_CHUNK), dtype=bool))[:, :, None]

    def step(state, xs):
        qt, kt, vt, bt = xs
        inter = jnp.einsum('bhtc,bhcv->bhtv', qt * jnp.exp(bt), state)
        diff = bt[:, :, :, None, :] - bt[:, :, None, :, :]
        decay = jnp.exp(jnp.where(causal, diff, -jnp.inf))
        scores = jnp.einsum('bhtc,bhsc,bhtsc->bhts', qt, kt, decay)
        intra = jnp.einsum('bhts,bhsv->bhtv', scores, vt)
        b_last = bt[:, :, -1:, :]
        new_state = (jnp.exp(b_last[:, :, 0, :])[..., None] * state
                     + jnp.einsum('bhsc,bhsv->bhcv', kt * jnp.exp(b_last - bt), vt))
        return new_state, inter + intra

    s0 = jnp.zeros((B, HG_HEADS, HG_DK, HG_DV), f32)
    _, o = lax.scan(step, s0, (qc, kc, vc, bc))
    o = o.transpose(1, 0, 3, 2, 4).reshape(B, S, HG_HEADS, HG_DV)
    o = rms_norm(o, o_gain).reshape(B, S, HG_VW)
    o = o * jax.nn.silu(g.astype(f32))
    return o.astype(q.dtype)


def forgetting_attention(q, k, v, f_logit, f_bias, q_gain, k_gain):
    B, S, _ = q.shape
    f32 = jnp.float32

    def heads(t):
        return t.reshape(B, S, FOX_HEADS, FOX_DH).transpose(0, 2, 1, 3)

    qh = rms_norm(heads(q), q_gain).astype(f32)
    kh = rms_norm(heads(k), k_gain).astype(f32)
    vh = heads(v)
    logf = jax.nn.log_sigmoid(f_logit.astype(f32) + f_bias.astype(f32))
    c = jnp.cumsum(logf, axis=1).transpose(0, 2, 1)
    scale = FOX_DH ** -0.5
    outs = []
    for blk in range(S // FOX_BLOCK):
        t0 = blk * FOX_BLOCK
        t1 = t0 + FOX_BLOCK
        s = jnp.einsum('bhtd,bhsd->bhts', qh[:, :, t0:t1], kh[:, :, :t1]) * scale
        s = s + c[:, :, t0:t1, None] - c[:, :, None, :t1]
        mask = (t0 + jnp.arange(FOX_BLOCK))[:, None] >= jnp.arange(t1)[None, :]
        s = jnp.where(mask, s, -jnp.inf)
        pr = jax.nn.softmax(s, axis=-1).astype(vh.dtype)
        outs.append(jnp.einsum('bhts,bhsd->bhtd', pr, vh[:, :, :t1]))
    o = jnp.concatenate(outs, axis=2)
    return o.transpose(0, 2, 1, 3).reshape(B, S, FOX_W)


def setup_inputs(seed: int = 0) -> dict:
    key = jax.random.key(seed)
    ks = jax.random.split(key, 24)
    f32 = jnp.float32

    def w(k, shape, fan_in):
        return jax.random.normal(k, shape, f32) * (fan_in ** -0.5)

    def gain(k, shape):
        return 1.0 + 0.02 * jax.random.normal(k, shape, f32)

    return {
        "x": jax.random.normal(ks[0], (BATCH, SEQ, D_MODEL), f32),
        "p": jax.random.normal(ks[1], (DEPTH, BATCH, SEQ, PLE_DIM), f32),
        "norm_mix_g": gain(ks[2], (DEPTH, D_MODEL)),
        "w_in": w(ks[3], (DEPTH, D_MODEL, IN_COLS), D_MODEL),
        "hg_lb_logits": 0.1 * jax.random.normal(ks[4], (DEPTH + 1, HG_KW), f32),
        "hg_onorm_g": gain(ks[5], (DEPTH, HG_DV)),
        "fox_f_bias": jax.random.uniform(ks[6], (DEPTH, FOX_HEADS), f32, minval=1.0, maxval=4.0),
        "fox_q_norm_g": gain(ks[7], (DEPTH, FOX_DH)),
        "fox_k_norm_g": gain(ks[8], (DEPTH, FOX_DH)),
        "w_branch_a": w(ks[9], (DEPTH, HG_VW, D_MODEL), HG_VW),
        "w_branch_b": w(ks[10], (DEPTH, FOX_W, D_MODEL), FOX_W),
        "w_out": w(ks[11], (DEPTH, D_MODEL, D_MODEL), D_MODEL),
        "norm_ffn_g": gain(ks[12], (DEPTH, D_MODEL)),
        "w_ffn_gate": w(ks[13], (DEPTH, D_MODEL, D_FF), D_MODEL),
        "w_ffn_up": w(ks[14], (DEPTH, D_MODEL, D_FF), D_MODEL),
        "w_ffn_down": w(ks[15], (DEPTH, D_FF, D_MODEL), D_FF),
        "norm_ple_g": gain(ks[16], (DEPTH, D_MODEL)),
        "w_ple_gate": w(ks[17], (DEPTH, D_MODEL, D_MODEL), D_MODEL),
        "w_ple_proj": w(ks[18], (DEPTH, PLE_DIM, D_MODEL), PLE_DIM),
    }


def reference(x, p, norm_mix_g, w_in, hg_lb_logits, hg_onorm_g, fox_f_bias, fox_q_norm_g,
              fox_k_norm_g, w_branch_a, w_branch_b, w_out, norm_ffn_g, w_ffn_gate, w_ffn_up,
              w_ffn_down, norm_ple_g, w_ple_gate, w_ple_proj):
    lower_bounds = jnp.cumsum(jax.nn.softmax(hg_lb_logits.astype(jnp.float32), axis=0), axis=0)
    for layer in range(DEPTH):
        h = rms_norm(x, norm_mix_g[layer])
        z = h @ w_in[layer]
        y_a = hgrn2_mixer(z[..., OFF_HG_Q:OFF_HG_F], z[..., OFF_HG_F:OFF_HG_I],
                          z[..., OFF_HG_I:OFF_HG_G], z[..., OFF_HG_G:OFF_FOX_Q],
                          lower_bounds[layer], hg_onorm_g[layer])
        y_b = forgetting_attention(z[..., OFF_FOX_Q:OFF_FOX_K], z[..., OFF_FOX_K:OFF_FOX_V],
                                   z[..., OFF_FOX_V:OFF_FOX_F], z[..., OFF_FOX_F:OFF_GATE],
                                   fox_f_bias[layer], fox_q_norm_g[layer], fox_k_norm_g[layer])
        gate_a = jax.nn.sigmoid(z[..., OFF_GATE:OFF_GATE + D_MODEL])
        gate_b = jax.nn.sigmoid(z[..., OFF_GATE + D_MODEL:IN_COLS])
        merged = gate_a * (y_a @ w_branch_a[layer]) + gate_b * (y_b @ w_branch_b[layer])
        x = x + merged @ w_out[layer]
        hf = rms_norm(x, norm_ffn_g[layer])
        x = x + (jax.nn.silu(hf @ w_ffn_gate[layer]) * (hf @ w_ffn_up[layer])) @ w_ffn_down[layer]
        hp = rms_norm(x, norm_ple_g[layer])
        x = x + jax.nn.sigmoid(hp @ w_ple_gate[layer]) * (p[layer] @ w_ple_proj[layer])
    return x
```

```python
import contextlib
import numpy as np
import concourse.bass as bass
import concourse.mybir as mybir
from concourse.bass_utils import run_bass_kernel_spmd

F32 = mybir.dt.float32
BF16 = mybir.dt.bfloat16
AF = mybir.ActivationFunctionType
ALU = mybir.AluOpType

ENGS = ("pe", "act", "dve", "pool", "sp")
EPS = 1e-6
HQ, HF, HI, HGC, FQ, FK, FV, FFC, GA, GB = 0, 512, 1024, 1536, 2048, 2560, 3072, 3584, 3592, 4616
INC = 5640
DFF = 2816
NFF = DFF // 128


class _Op:
    __slots__ = ("eng", "fn", "dma", "preds", "deps", "signal", "sig_idx", "gid", "pos", "dma_val", "cost",
                 "fin", "done")

    def __init__(self, eng, fn, dma, cost):
        self.eng = eng
        self.fn = fn
        self.dma = dma
        self.cost = cost
        self.preds = []
        self.deps = []
        self.signal = False
        self.sig_idx = None
        self.gid = None
        self.pos = None
        self.dma_val = None
        self.fin = None
        self.done = False


_DEF_COST = {"pe": 260.0, "act": 700.0, "dve": 720.0, "pool": 600.0, "sp": 100.0}


class Sched:
    SEM_ROT = 16000
    WINDOW = 256
    REORDER = True

    def __init__(self, nc):
        self.nc = nc
        self.ops = []
        self.lastw = {}
        self.readers = {}
        self.excl = set()

    def op(self, eng, fn, reads=(), writes=(), dma=None, nodep=False, c=None):
        if self.excl:
            writes = list(writes) + [k for k in reads if k in self.excl]
            reads = [k for k in reads if k not in self.excl]
        if c is None:
            c = 2600.0 if dma is not None else _DEF_COST[eng]
        o = _Op(eng, fn, dma, c)
        o.gid = len(self.ops)
        preds = {}
        if not nodep:
            for k in reads:
                w = self.lastw.get(k)
                if w is not None:
                    preds[id(w)] = (w, True)
            for k in writes:
                w = self.lastw.get(k)
                if w is not None and id(w) not in preds:
                    preds[id(w)] = (w, False)
                for r in self.readers.get(k, ()):
                    if id(r) not in preds and r is not o:
                        preds[id(r)] = (r, False)
        o.preds = list(preds.values())
        for k in writes:
            self.lastw[k] = o
            self.readers[k] = []
        for k in reads:
            self.readers.setdefault(k, []).append(o)
        self.ops.append(o)
        return o

    def _schedule(self):
        pend = {e: [o for o in self.ops if o.eng == e] for e in ENGS}
        if not self.REORDER:
            return pend
        order = {e: [] for e in ENGS}
        free = {e: 0.0 for e in ENGS}
        head = {e: 0 for e in ENGS}
        nleft = len(self.ops)
        while nleft:
            best = None
            for e in ENGS:
                L = pend[e]
                h = head[e]
                while h < len(L) and L[h].done:
                    h += 1
                head[e] = h
                if h >= len(L):
                    continue
                cand = None
                cnt = 0
                i = h
                while i < len(L) and cnt < self.WINDOW:
                    o = L[i]
                    i += 1
                    if o.done:
                        continue
                    cnt += 1
                    rdy = 0.0
                    ok = True
                    for p, _ in o.preds:
                        if not p.done:
                            ok = False
                            break
                        t = p.fin + (0.0 if p.eng == e and p.dma is None else 120.0)
                        if t > rdy:
                            rdy = t
                    if not ok:
                        continue
                    st = rdy if rdy > free[e] else free[e]
                    if cand is None or st < cand[0] - 1e-9:
                        cand = (st, o)
                    if rdy <= free[e]:
                        break
                if cand is not None and (best is None or cand[0] < best[0] - 1e-9 or
                                         (abs(cand[0] - best[0]) <= 1e-9 and cand[1].gid < best[1].gid)):
                    best = cand
            st, o = best
            e = o.eng
            occ = (1000.0 if e == "pool" else 60.0) if o.dma is not None else o.cost
            free[e] = st + occ
            o.fin = st + o.cost
            o.done = True
            o.pos = len(order[e])
            order[e].append(o)
            nleft -= 1
        self.est_ns = max(o.fin for o in self.ops) if self.ops else 0.0
        busy = {e: 0.0 for e in ENGS}
        for o in self.ops:
            busy[o.eng] += ((1000.0 if o.eng == "pool" else 60.0) if o.dma is not None else o.cost)
        print("  busy us", {e: round(v / 1e3) for e, v in busy.items()})
        return order

    def emit(self):
        nc = self.nc
        order = self._schedule()
        for e in ENGS:
            for i, o in enumerate(order[e]):
                o.pos = i
        dma_cnt = {}
        for e in ENGS:
            for o in order[e]:
                best = {}
                for p, is_raw in o.preds:
                    if p.dma is not None:
                        o.deps.append(p)
                        continue
                    if p.eng == e:
                        assert p.pos < o.pos
                        if not is_raw or e == "pe":
                            continue
                    b = best.get(p.eng)
                    if b is None or p.pos > b.pos:
                        best[p.eng] = p
                for p in best.values():
                    o.deps.append(p)
                    p.signal = True
                if o.dma is not None:
                    c = dma_cnt.get(o.dma, 0) + 1
                    dma_cnt[o.dma] = c
                    o.dma_val = 16 * c
        with contextlib.ExitStack() as st:
            eng_sems = {}
            for e in ENGS:
                n = 0
                for o in order[e]:
                    if o.signal:
                        o.sig_idx = n
                        n += 1
                nsem = max(1, (n + self.SEM_ROT - 1) // self.SEM_ROT)
                print("  sched", e, "ops", len(order[e]), "signals", n)
                eng_sems[e] = [st.enter_context(nc.semaphore(f"s_{e}{i}")) for i in range(nsem)]
            print("  est ns", getattr(self, "est_ns", None))
            dma_sems = {k: st.enter_context(nc.semaphore(f"d_{k}")) for k in dma_cnt}
            blk_st = contextlib.ExitStack()
            block = blk_st.enter_context(nc.Block())

            def target(p):
                if p.dma is not None:
                    return (dma_sems[p.dma], p.dma_val, ("d", p.dma))
                j, v = divmod(p.sig_idx, self.SEM_ROT)
                return (eng_sems[p.eng][j], v + 1, (p.eng, j))

            def run(e, handle):
                waited = {}
                for o in order[e]:
                    need = {}
                    for p in o.deps:
                        sem, val, key = target(p)
                        if waited.get(key, 0) >= val:
                            continue
                        if key not in need or need[key][1] < val:
                            need[key] = (sem, val)
                    for key, (sem, val) in need.items():
                        handle.wait_ge(sem, val)
                        waited[key] = val
                    ins = o.fn(handle)
                    if o.dma is not None:
                        ins.then_inc(dma_sems[o.dma], 16)
                    elif o.signal:
                        j, _ = divmod(o.sig_idx, self.SEM_ROT)
                        ins.then_inc(eng_sems[e][j], 1)
                if e == "sp":
                    for k, c in dma_cnt.items():
                        if waited.get(("d", k), 0) < 16 * c:
                            handle.wait_ge(dma_sems[k], 16 * c)

            block.tensor(lambda h: run("pe", h))
            block.scalar(lambda h: run("act", h))
            block.vector(lambda h: run("dve", h))
            block.gpsimd(lambda h: run("pool", h))
            block.sync(lambda h: run("sp", h))
            blk_st.close()
            nc.all_engine_barrier()


class Ring:
    def __init__(self, items):
        self.items = items
        self.i = 0

    def next(self):
        r = self.items[self.i % len(self.items)]
        self.i += 1
        return r


class NS:
    pass


class Alloc:
    def __init__(self, nc, st, tag):
        self.nc, self.st, self.tag, self.bytes = nc, st, tag, 0

    def sb(self, name, shape, dt):
        n = 1
        for s in shape[1:]:
            n *= s
        self.bytes += n * (4 if dt == F32 else 2)
        return self.st.enter_context(self.nc.sbuf_tensor(f"{self.tag}_{name}", list(shape), dt))

    def ps(self, name, shape, dt):
        return self.st.enter_context(self.nc.psum_tensor(f"{self.tag}_{name}", list(shape), dt))


def load_w(S, dst, key, src, r0, nk, c0, ncols, cstep=512):
    for k in range(nk):
        for cc in range(0, ncols, cstep):
            w = min(cstep, ncols - cc)
            S.op("pool", lambda e, k=k, cc=cc, w=w: e.dma_start(
                out=dst[:, k, cc:cc + w], in_=src[r0 + k * 128:r0 + (k + 1) * 128, c0 + cc:c0 + cc + w]),
                writes=[key], dma=key, nodep=True)


def norm_T(S, C, X, xkey, gB, dst, dkey, W, scale_eng="act"):
    ssq, sk = W.ssq.next()
    hb, hk = W.hb.next()
    pT, pk = W.pT.next()
    S.op("act", lambda e: e.activation(out=hb[:], in_=X, func=AF.Square, accum_out=ssq[:, 0:1]),
         reads=[xkey], writes=[hk, sk], c=1100.0)
    S.op("act", lambda e: e.activation(out=ssq[:, 1:2], in_=ssq[:, 0:1], func=AF.Ln, scale=1.0 / 1024.0,
                                       bias=C.eps_t[:, 0:1]), reads=[sk], writes=[sk], c=260.0)
    S.op("act", lambda e: e.activation(out=ssq[:, 2:3], in_=ssq[:, 1:2], func=AF.Exp, scale=-0.5),
         reads=[sk], writes=[sk], c=260.0)
    if scale_eng == "act":
        S.op("act", lambda e: e.activation(out=hb[:], in_=X, func=AF.Copy, scale=ssq[:, 2:3]),
             reads=[xkey, sk], writes=[hk], c=1100.0)
    else:
        S.op("dve", lambda e: e.tensor_scalar(out=hb[:], in0=X, scalar1=ssq[:, 2:3], scalar2=None, op0=ALU.mult),
             reads=[xkey, sk], writes=[hk], c=1150.0)
    for k in range(8):
        S.op("pe", lambda e, k=k: e.transpose(out=pT[:, k, :], in_=hb[:, k * 128:(k + 1) * 128], identity=C.ident[:]),
             reads=[hk], writes=[pk], c=110.0)
    S.op("dve", lambda e: e.tensor_tensor(out=dst, in0=pT[:], in1=gB[:], op=ALU.mult), reads=[pk], writes=[dkey], c=1100.0)


def sigmoid_act(S, src, skey, dst, dkey, C, np_=128):
    S.op("act", lambda e: e.activation(out=dst, in_=src, func=AF.Exp, scale=-1.0), reads=[skey], writes=[dkey])
    S.op("act", lambda e: e.activation(out=dst, in_=dst, func=AF.Ln, bias=C.one_t[0:np_, 0:1], scale=1.0),
         reads=[dkey], writes=[dkey])
    S.op("act", lambda e: e.activation(out=dst, in_=dst, func=AF.Exp, scale=-1.0), reads=[dkey], writes=[dkey])


def build_program(SEQ=4096, debug=False):
    NT = SEQ // 128
    NG = SEQ // 512
    nc = bass.Bass("TRN2", target_bir_lowering=False)

    def din(name, shape):
        return nc.dram_tensor(name, list(shape), F32, kind="ExternalInput").ap()

    x = din("x", [SEQ, 1024])
    p_in = din("p", [SEQ, 256])
    norm_mix_g = din("norm_mix_g", [1, 1024])
    w_in = din("w_in", [1024, INC])
    hg_lb = din("hg_lb_logits", [2, 512])
    hg_on = din("hg_onorm_g", [1, 128])
    fox_fb = din("fox_f_bias", [1, 8])
    fox_qg = din("fox_q_norm_g", [1, 64])
    fox_kg = din("fox_k_norm_g", [1, 64])
    w_ba = din("w_branch_a", [512, 1024])
    w_bb = din("w_branch_b", [512, 1024])
    w_out = din("w_out", [1024, 1024])
    norm_ffn_g = din("norm_ffn_g", [1, 1024])
    w_fg = din("w_ffn_gate", [1024, DFF])
    w_fu = din("w_ffn_up", [1024, DFF])
    w_fd = din("w_ffn_down", [DFF, 1024])
    norm_ple_g = din("norm_ple_g", [1, 1024])
    w_pg = din("w_ple_gate", [1024, 1024])
    w_pp = din("w_ple_proj", [256, 1024])
    out = nc.dram_tensor("out", [SEQ, 1024], F32, kind="ExternalOutput").ap()
    if debug:
        x1d = nc.dram_tensor("x1d", [SEQ, 1024], F32, kind="ExternalOutput").ap()
        dbg_yb = nc.dram_tensor("dbg_yb", [128, 4, SEQ], BF16, kind="ExternalOutput").ap()
        dbg_ya = nc.dram_tensor("dbg_ya", [SEQ // 512, 128, 4, 512], BF16, kind="ExternalOutput").ap()
    else:
        x1d = nc.dram_tensor("x1d", [SEQ, 1024], F32, kind="ExternalOutput").ap()

    with contextlib.ExitStack() as st0:
        A0 = Alloc(nc, st0, "c")
        C = NS()
        C.ident = A0.sb("ident", [128, 128], BF16)
        C.ones_bf = A0.sb("ones_bf", [128, 128], BF16)
        C.eps_t = A0.sb("eps_t", [128, 1], F32)
        C.one_t = A0.sb("one_t", [128, 1], F32)
        C.lbT = A0.sb("lbT", [128, 4], F32)
        C.omlT = A0.sb("omlT", [128, 4], F32)
        C.nomlT = A0.sb("nomlT", [128, 4], F32)
        C.gon = A0.sb("gon", [128, 1], F32)
        C.qg = A0.sb("qg", [64, 1], F32)
        C.kg = A0.sb("kg", [64, 1], F32)
        C.fbias = A0.sb("fbias", [128, 8], F32)
        st_yb = contextlib.ExitStack()
        A_yb = Alloc(nc, st_yb, "yb")
        ybT = A_yb.sb("ybT", [128, 4, SEQ], BF16)
        C.mask_bf = A_yb.sb("mask_bf", [128, 128], BF16)
        C.ones_f = A_yb.sb("ones_f", [128, 128], F32)
        C.U_f = A_yb.sb("U_f", [128, 128], F32)
        C.E127 = A_yb.sb("E127", [128, 128], F32)
        C.gmixB = A_yb.sb("gmixB", [128, 8, 128], BF16)

        with contextlib.ExitStack() as st:
            A = Alloc(nc, st, "p0")
            S = Sched(nc)
            tmpf = A.sb("tmpf", [128, 128], F32)
            gT = A.sb("gT", [128, 8], F32)
            l0 = A.sb("l0", [128, 4], F32)
            l1 = A.sb("l1", [128, 4], F32)
            S.op("pool", lambda e: e.memset(C.ones_f[:], 1.0), writes=["ones_f"])
            S.op("pool", lambda e: e.memset(C.eps_t[:], EPS), writes=["eps"])
            S.op("pool", lambda e: e.memset(C.one_t[:], 1.0), writes=["one"])
            S.op("pool", lambda e: e.memset(C.ones_bf[:], 1.0), writes=["ones_bf"])
            S.op("pool", lambda e: e.affine_select(out=tmpf[:], in_=C.ones_f[:], pattern=[[-1, 128]],
                                                   compare_op=ALU.is_equal, fill=0.0, base=0, channel_multiplier=1),
                 reads=["ones_f"], writes=["tmpf"])
            S.op("dve", lambda e: e.tensor_copy(out=C.ident[:], in_=tmpf[:]), reads=["tmpf"], writes=["ident"])
            S.op("pool", lambda e: e.affine_select(out=C.U_f[:], in_=C.ones_f[:], pattern=[[1, 128]],
                                                   compare_op=ALU.is_ge, fill=0.0, base=0, channel_multiplier=-1),
                 reads=["ones_f"], writes=["U"])
            S.op("dve", lambda e: e.tensor_copy(out=C.mask_bf[:], in_=C.U_f[:]), reads=["U"], writes=["mask"])
            S.op("pool", lambda e: e.affine_select(out=C.E127[:], in_=C.ones_f[:], pattern=[[0, 128]],
                                                   compare_op=ALU.is_equal, fill=0.0, base=-127, channel_multiplier=1),
                 reads=["ones_f"], writes=["E127"])
            S.op("sp", lambda e: e.dma_start(out=gT[:], in_=norm_mix_g.rearrange("o (k p) -> p (o k)", p=128),
                                             allow_slow_non_contiguous=True), writes=["gT"], dma="gT")
            S.op("dve", lambda e: e.tensor_copy(out=C.gmixB[:], in_=gT[:].unsqueeze(2).to_broadcast([128, 8, 128])),
                 reads=["gT"], writes=["gmixB"])
            S.op("sp", lambda e: e.dma_start(out=l0[:], in_=hg_lb[0:1, :].rearrange("o (h c) -> c (o h)", c=128),
                                             allow_slow_non_contiguous=True), writes=["l0"], dma="l0")
            S.op("sp", lambda e: e.dma_start(out=l1[:], in_=hg_lb[1:2, :].rearrange("o (h c) -> c (o h)", c=128),
                                             allow_slow_non_contiguous=True), writes=["l1"], dma="l1")
            S.op("sp", lambda e: e.dma_start(out=C.gon[:], in_=hg_on.rearrange("o v -> v o"),
                                             allow_slow_non_contiguous=True), writes=["gon"], dma="gon")
            S.op("sp", lambda e: e.dma_start(out=C.qg[:], in_=fox_qg.rearrange("o v -> v o"),
                                             allow_slow_non_contiguous=True), writes=["qg"], dma="qg")
            S.op("sp", lambda e: e.dma_start(out=C.kg[:], in_=fox_kg.rearrange("o v -> v o"),
                                             allow_slow_non_contiguous=True), writes=["kg"], dma="kg")
            S.op("sp", lambda e: e.dma_start(out=C.fbias[:], in_=fox_fb.partition_broadcast(128)),
                 writes=["fbias"], dma="fbias")
            S.op("dve", lambda e: e.tensor_scalar(out=C.qg[:], in0=C.qg[:], scalar1=0.125, scalar2=None, op0=ALU.mult),
                 reads=["qg"], writes=["qg"])
            S.op("dve", lambda e: e.tensor_tensor(out=l1[:], in0=l1[:], in1=l0[:], op=ALU.subtract),
                 reads=["l0", "l1"], writes=["l1"])
            S.op("act", lambda e: e.activation(out=l1[:], in_=l1[:], func=AF.Exp), reads=["l1"], writes=["l1"])
            S.op("dve", lambda e: e.tensor_scalar_add(out=l1[:], in0=l1[:], scalar1=1.0), reads=["l1"], writes=["l1"])
            S.op("dve", lambda e: e.reciprocal(out=C.lbT[:], in_=l1[:]), reads=["l1"], writes=["lbT"])
            S.op("dve", lambda e: e.tensor_scalar(out=C.omlT[:], in0=C.lbT[:], scalar1=-1.0, scalar2=1.0,
                                                  op0=ALU.mult, op1=ALU.add), reads=["lbT"], writes=["omlT"])
            S.op("dve", lambda e: e.tensor_scalar(out=C.nomlT[:], in0=C.lbT[:], scalar1=-1.0, scalar2=None,
                                                  op0=ALU.add), reads=["lbT"], writes=["nomlT"])
            S.emit()

        with contextlib.ExitStack() as st:
            A = Alloc(nc, st, "a1")
            S = Sched(nc)
            S.excl.update(["pT", "psV", "psQ", "psN", "psS0", "psS1", "psQ", "psO1"])
            w_fq = A.sb("w_fq", [128, 8, 512], BF16)
            w_fk = A.sb("w_fk", [128, 8, 512], BF16)
            w_fv = A.sb("w_fv", [128, 8, 512], BF16)
            w_ff = A.sb("w_ff", [128, 8, 8], BF16)
            KT = A.sb("KT", [65, 8, SEQ], BF16)
            Vc = A.sb("Vc", [128, NT, 8, 64], BF16)
            cP = A.sb("cP", [128, NT, 8], F32)
            biasG = A.sb("biasG", [128, NT, 8], F32)
            Aaug = A.sb("Aaug", [128, 4, 8], BF16)
            hT = A.sb("hT", [128, 8, 512], BF16)
            sqr = Ring([(A.sb(f"sq{i}", [64, 512], BF16), f"sq{i}") for i in range(2)])
            rsr = Ring([(A.sb(f"rs{i}", [64, 512], F32), f"rs{i}") for i in range(2)])
            rinv_sb = A.sb("rinv_sb", [64, 512], F32)
            zb = A.sb("zb", [128, 8], F32)
            cref = A.sb("cref", [128, 8], F32)
            xring = Ring([(A.sb(f"x{i}", [128, 1024], F32), f"x{i}") for i in range(3)])
            W = NS()
            W.ssq = Ring([(A.sb(f"ssq{i}", [128, 4], F32), f"ssq{i}") for i in range(2)])
            W.hb = Ring([(A.sb(f"hb{i}", [128, 1024], BF16), f"hb{i}") for i in range(2)])
            QTr = Ring([(A.sb(f"QT{i}", [65, 512], BF16), f"QT{i}") for i in range(4)])
            PTr = Ring([(A.sb(f"PT{i}", [128, 512], BF16), f"PT{i}") for i in range(3)])
            pTt = A.ps("pT", [128, 8, 128], BF16)
            W.pT = Ring([(pTt, "pT")])
            psV = A.ps("psV", [128, 512], F32)
            psQ = A.ps("psQ", [128, 512], F32)
            psN = A.ps("psN", [128, 512], F32)
            psSr = Ring([(A.ps(f"psS{i}", [128, 512], F32), f"psS{i}") for i in range(2)])
            psOr = Ring([(A.ps(f"psO{i}", [128, 512], F32), f"psO{i}") for i in range(2)])
            print("A1 sbuf bytes/partition", A.bytes + A0.bytes + A_yb.bytes)

            S.op("pool", lambda e: e.memset(KT[64:65, :, :], 1.0), writes=["KTones"])
            load_w(S, w_fv, "w_fv", w_in, 0, 8, FV, 512)
            for k in range(8):
                S.op("pool", lambda e, k=k: e.dma_start(out=w_ff[:, k, :], in_=w_in[k * 128:(k + 1) * 128, FFC:FFC + 8]),
                     writes=["w_ff"], dma="w_ff", nodep=True)
            load_w(S, w_fq, "w_fq", w_in, 0, 8, FQ, 512)
            load_w(S, w_fk, "w_fk", w_in, 0, 8, FK, 512)

            for G in range(NG):
                for i in range(4):
                    ti = 4 * G + i
                    xt, xk = xring.next()
                    S.op("sp", lambda e, xt=xt, ti=ti: e.dma_start(out=xt[:], in_=x[ti * 128:(ti + 1) * 128, :]),
                         writes=[xk], dma=xk)
                    norm_T(S, C, xt[:], xk, C.gmixB, hT[:, :, i * 128:(i + 1) * 128], ("hT", i), W, scale_eng="dve")
                    for k in range(8):
                        S.op("pe", lambda e, k=k, i=i: e.matmul(psV[:], lhsT=hT[:, k, i * 128:(i + 1) * 128], rhs=w_fv[:, k, :],
                                                               start=(k == 0), stop=(k == 7)),
                             reads=[("hT", i), "w_fv"], writes=["psV"])
                    S.op("dve", lambda e, ti=ti: e.tensor_copy(out=Vc[:, ti, :, :],
                                                               in_=psV[:].rearrange("p (h d) -> p h d", d=64)),
                         reads=["psV"], writes=[("Vc", ti)])
                    psF = psQ[:, 0:8]
                    psC = psQ[:, 8:16]
                    for k in range(8):
                        S.op("pe", lambda e, k=k, i=i, psF=psF: e.matmul(psF, lhsT=hT[:, k, i * 128:(i + 1) * 128], rhs=w_ff[:, k, :],
                                                                        start=(k == 0), stop=(k == 7)),
                             reads=[("hT", i), "w_ff"], writes=["psQ"])
                    S.op("dve", lambda e, psF=psF: e.tensor_tensor(out=zb[:], in0=psF, in1=C.fbias[:], op=ALU.add),
                         reads=["psQ"], writes=["zb"])
                    S.op("act", lambda e: e.activation(out=zb[:], in_=zb[:], func=AF.Exp, scale=-1.0),
                         reads=["zb"], writes=["zb"])
                    S.op("act", lambda e: e.activation(out=zb[:], in_=zb[:], func=AF.Ln, bias=C.one_t[:, 0:1], scale=1.0),
                         reads=["zb"], writes=["zb"])
                    S.op("pe", lambda e, ti=ti, psC=psC: e.matmul(psC, lhsT=C.U_f[:], rhs=zb[:], start=True, stop=(ti == 0)),
                         reads=["zb"], writes=["psQ"])
                    if ti > 0:
                        S.op("pe", lambda e, ti=ti, psC=psC: e.matmul(psC, lhsT=C.E127[:], rhs=cP[:, ti - 1, :], start=False, stop=True),
                             reads=[("cP", ti - 1)], writes=["psQ"])
                    S.op("dve", lambda e, ti=ti, psC=psC: e.tensor_copy(out=cP[:, ti, :], in_=psC),
                         reads=["psQ"], writes=[("cP", ti)])
                psR = psQ[:, 16:24]
                S.op("pe", lambda e, G=G, psR=psR: e.matmul(psR, lhsT=C.E127[:], rhs=cP[:, 4 * G + 1, :], start=True, stop=True),
                     reads=[("cP", 4 * G + 1)], writes=["psQ"])
                S.op("dve", lambda e, psR=psR: e.tensor_copy(out=cref[:], in_=psR), reads=["psQ"], writes=["cref"])
                for i in range(4):
                    S.op("dve", lambda e, i=i, G=G: e.tensor_tensor(out=Aaug[:, i, :], in0=cref[:], in1=cP[:, 4 * G + i, :],
                                                                   op=ALU.subtract),
                         reads=["cref", ("cP", 4 * G + i)], writes=[("Aaug", i)])
                nj = 4 * G + 4
                S.op("dve", lambda e, nj=nj: e.tensor_tensor(out=biasG[:, 0:nj, :], in0=cP[:, 0:nj, :],
                                                             in1=cref[:].unsqueeze(1).to_broadcast([128, nj, 8]),
                                                             op=ALU.subtract),
                     reads=["cref"] + [("cP", t) for t in range(nj)], writes=["biasG"])
                for h in range(8):
                    QT, qk = QTr.next()
                    for i in range(4):
                        S.op("pe", lambda e, i=i, h=h: e.matmul(psQ[64:65, i * 128:(i + 1) * 128], lhsT=Aaug[:, i, h:h + 1],
                                                               rhs=C.ident[:], start=(i == 0), stop=(i == 3)),
                             reads=[("Aaug", i)], writes=["psQ"])
                    for k in range(8):
                        S.op("pe", lambda e, k=k, h=h: e.matmul(psQ[0:64, :], lhsT=w_fq[:, k, h * 64:(h + 1) * 64], rhs=hT[:, k, :],
                                                               start=(k == 0), stop=(k == 7)),
                             reads=[("hT", 0), ("hT", 1), ("hT", 2), ("hT", 3), "w_fq"], writes=["psQ"])
                    for k in range(8):
                        S.op("pe", lambda e, k=k, h=h: e.matmul(psV[0:64, :], lhsT=w_fk[:, k, h * 64:(h + 1) * 64], rhs=hT[:, k, :],
                                                               start=(k == 0), stop=(k == 7)),
                             reads=[("hT", 0), ("hT", 1), ("hT", 2), ("hT", 3), "w_fk"], writes=["psV"])
                    for (src, skey, gain, dst, dkey) in ((psQ, "psQ", C.qg, QT[0:64, :], qk),
                                                        (psV, "psV", C.kg, KT[0:64, h, G * 512:(G + 1) * 512], ("KT", h, G))):
                        sq, sqk = sqr.next()
                        rs, rsk = rsr.next()
                        S.op("act", lambda e, src=src, sq=sq: e.activation(out=sq[:], in_=src[0:64, :], func=AF.Square),
                             reads=[skey], writes=[sqk], c=650.0)
                        S.op("pe", lambda e, sq=sq: e.matmul(psN[0:64, :], lhsT=C.ones_bf[0:64, 0:64], rhs=sq[:], start=True, stop=True),
                             reads=[sqk], writes=["psN"])
                        S.op("act", lambda e, rs=rs: e.activation(out=rs[:], in_=psN[0:64, :], func=AF.Ln, scale=1.0 / 64.0,
                                                           bias=C.eps_t[0:64, 0:1]), reads=["psN"], writes=[rsk], c=650.0)
                        S.op("act", lambda e, rs=rs: e.activation(out=rs[:], in_=rs[:], func=AF.Exp, scale=-0.5),
                             reads=[rsk], writes=[rsk], c=650.0)
                        S.op("dve", lambda e, src=src, gain=gain, dst=dst, rs=rs: e.scalar_tensor_tensor(
                            out=dst, in0=src[0:64, :], scalar=gain[:, 0:1], in1=rs[:], op0=ALU.mult, op1=ALU.mult),
                            reads=[skey, rsk], writes=[dkey], c=620.0)
                    S.op("act", lambda e, QT=QT: e.copy(out=QT[64:65, :], in_=psQ[64:65, :]), reads=["psQ"], writes=[qk])

                    psO, ok = psOr.next()
                    pend = None

                    def pv(j, PT, ptk, psO=psO, ok=ok, h=h, G=G, nj=nj):
                        c0 = max(0, j - 4 * G) * 128
                        S.op("pe", lambda e: e.matmul(psO[0:64, c0:512], lhsT=Vc[:, j, h, :], rhs=PT[:, c0:512],
                                                      start=(j == 0), stop=(j == nj - 1)),
                             reads=[ptk, ("Vc", j)], writes=[ok])
                        S.op("pe", lambda e: e.matmul(psO[64:128, c0:512], lhsT=C.ones_bf[:, 0:64], rhs=PT[:, c0:512],
                                                      start=(j == 0), stop=(j == nj - 1)),
                             reads=[ptk], writes=[ok])

                    for j in range(nj):
                        r = max(0, j - 4 * G)
                        c0 = r * 128
                        psS, sk_ = psSr.next()
                        PT, ptk = PTr.next()
                        S.op("pe", lambda e, j=j, c0=c0, psS=psS, h=h, QT=QT: e.matmul(
                            psS[:, c0:512], lhsT=KT[0:65, h, j * 128:(j + 1) * 128], rhs=QT[0:65, c0:512], start=True, stop=True),
                            reads=[("KT", h, j // 4), "KTones", qk], writes=[sk_])
                        S.op("act", lambda e, j=j, c0=c0, psS=psS, PT=PT, h=h: e.activation(
                            out=PT[:, c0:512], in_=psS[:, c0:512], func=AF.Exp, bias=biasG[:, j, h:h + 1], scale=1.0),
                            reads=[sk_, "biasG"], writes=[ptk])
                        if j >= 4 * G:
                            S.op("pool", lambda e, c0=c0, PT=PT: e.tensor_tensor(out=PT[:, c0:c0 + 128], in0=PT[:, c0:c0 + 128],
                                                                                in1=C.mask_bf[:], op=ALU.mult),
                                 reads=[ptk], writes=[ptk])
                        if pend is not None:
                            pv(*pend)
                        pend = (j, PT, ptk)
                    pv(*pend)
                    S.op("dve", lambda e, psO=psO: e.reciprocal(out=rinv_sb[:], in_=psO[64:128, :]),
                         reads=[ok], writes=["rinv_sb"])
                    S.op("dve", lambda e, psO=psO, h=h, G=G: e.tensor_tensor(
                        out=ybT[(h % 2) * 64:(h % 2) * 64 + 64, h // 2, G * 512:(G + 1) * 512],
                        in0=psO[0:64, :], in1=rinv_sb[:], op=ALU.mult),
                        reads=[ok, "rinv_sb"], writes=[("ybT", G, h)])
            if debug:
                S.op("sp", lambda e: e.dma_start(out=dbg_yb, in_=ybT[:]), reads=[("ybT", G_, h_) for G_ in range(NG) for h_ in range(8)], dma="dbgyb")
            S.emit()

        with contextlib.ExitStack() as st:
            A = Alloc(nc, st, "a2")
            S = Sched(nc)
            S.excl.update(["pT"] + [f"b{i}" for i in range(7)])
            w_hq = A.sb("w_hq", [128, 8, 512], BF16)
            w_hf = A.sb("w_hf", [128, 8, 512], BF16)
            w_hi = A.sb("w_hi", [128, 8, 512], BF16)
            w_hg = A.sb("w_hg", [128, 8, 512], BF16)
            w_ga = A.sb("w_ga", [128, 8, 1024], BF16)
            w_gb = A.sb("w_gb", [128, 8, 1024], BF16)
            wA = A.sb("wA", [128, 4, 1024], BF16)
            wB = A.sb("wB", [128, 4, 1024], BF16)
            wO = A.sb("wO", [128, 8, 1024], BF16)
            hT = A.sb("hT", [128, 8, 512], BF16)
            v_tok = A.sb("v_tok", [128, 4, 512], BF16)
            tmp = [A.sb(f"t{i}", [128, 512], F32) for i in range(5)]
            qT = A.sb("qT", [128, 512], BF16)
            ktT = A.sb("ktT", [128, 512], BF16)
            khT = A.sb("khT", [128, 512], BF16)
            khat = A.sb("khat", [128, 4, 128], BF16)
            sg = A.sb("sg", [128, 4, 512], BF16)
            yaT = A.sb("yaT", [128, 4, 512], BF16)
            mT = A.sb("mT", [128, 8, 512], BF16)
            Sf = A.sb("Sf", [128, 4, 128], F32)
            Sb = A.sb("Sb", [128, 4, 128], BF16)
            PsT = A.sb("PsT", [128, 128], BF16)
            eL = A.sb("eL", [128, 4], F32)
            sq = A.sb("sq2", [128, 512], BF16)
            xring = Ring([(A.sb(f"x{i}", [128, 1024], F32), f"x{i}") for i in range(5)])
            W = NS()
            W.ssq = Ring([(A.sb(f"ssq{i}", [128, 4], F32), f"ssq{i}") for i in range(2)])
            W.hb = Ring([(A.sb("hb0", [128, 1024], BF16), "hb0")])
            pTt = A.ps("pT", [128, 8, 128], BF16)
            W.pT = Ring([(pTt, "pT")])
            bk = [A.ps(f"b{i}", [128, 512], F32) for i in range(7)]
            print("A2 sbuf bytes/partition", A.bytes + A0.bytes + A_yb.bytes)

            S.op("pool", lambda e: e.memset(Sf[:], 0.0), writes=[("Sf", h) for h in range(4)])
            S.op("pool", lambda e: e.memset(Sb[:], 0.0), writes=[("Sb", h) for h in range(4)])
            load_w(S, w_hi, "w_hi", w_in, 0, 8, HI, 512)
            load_w(S, w_hq, "w_hq", w_in, 0, 8, HQ, 512)
            load_w(S, w_hf, "w_hf", w_in, 0, 8, HF, 512)
            load_w(S, w_hg, "w_hg", w_in, 0, 8, HGC, 512)
            load_w(S, w_ga, "w_ga", w_in, 0, 8, GA, 1024)
            load_w(S, w_gb, "w_gb", w_in, 0, 8, GB, 1024)
            load_w(S, wA, "wA", w_ba, 0, 4, 0, 1024)
            load_w(S, wB, "wB", w_bb, 0, 4, 0, 1024)
            load_w(S, wO, "wO", w_out, 0, 8, 0, 1024)

            for G in range(NG):
                xts = []
                for i in range(4):
                    ti = 4 * G + i
                    xt, xk = xring.next()
                    xts.append((xt, xk))
                    S.op("sp", lambda e, xt=xt, ti=ti: e.dma_start(out=xt[:], in_=x[ti * 128:(ti + 1) * 128, :]),
                         writes=[xk], dma=xk)
                    norm_T(S, C, xt[:], xk, C.gmixB, hT[:, :, i * 128:(i + 1) * 128], ("hT", i), W)
                    for k in range(8):
                        S.op("pe", lambda e, k=k, i=i: e.matmul(bk[0][:], lhsT=hT[:, k, i * 128:(i + 1) * 128], rhs=w_hi[:, k, :],
                                                               start=(k == 0), stop=(k == 7)),
                             reads=[("hT", i), "w_hi"], writes=["b0"])
                    S.op("act", lambda e, i=i: e.copy(out=v_tok[:, i, :], in_=bk[0][:]), reads=["b0"], writes=[("v_tok", i)])
                hTall = [("hT", 0), ("hT", 1), ("hT", 2), ("hT", 3)]
                for h in range(4):
                    pZq, pZf, pZg, pSm, pOh, pN = bk[1], bk[2], bk[3], bk[4], bk[5], bk[6]
                    for (ps_, w_, wk_, key_) in ((pZf, w_hf, "w_hf", "b2"), (pZq, w_hq, "w_hq", "b1"), (pZg, w_hg, "w_hg", "b3")):
                        for k in range(8):
                            S.op("pe", lambda e, k=k, h=h, ps_=ps_, w_=w_: e.matmul(
                                ps_[:], lhsT=w_[:, k, h * 128:(h + 1) * 128], rhs=hT[:, k, :], start=(k == 0), stop=(k == 7)),
                                reads=hTall + [wk_], writes=[key_])
                    t0, t1, t2, t3, t4 = tmp
                    sigmoid_act(S, pZf[:], "b2", t0[:], "t0", C)
                    S.op("act", lambda e, h=h: e.activation(out=t1[:], in_=t0[:], func=AF.Ln, scale=C.omlT[:, h:h + 1],
                                                            bias=C.lbT[:, h:h + 1]), reads=["t0"], writes=["t1"])
                    S.op("dve", lambda e, h=h: e.tensor_scalar(out=t2[:], in0=t0[:], scalar1=C.nomlT[:, h:h + 1],
                                                               scalar2=C.omlT[:, h:h + 1], op0=ALU.mult, op1=ALU.add),
                         reads=["t0"], writes=["t2"])
                    for i in range(4):
                        S.op("dve", lambda e, i=i: e.tensor_tensor_scan(out=t3[:, i * 128:(i + 1) * 128], data0=C.ones_f[:],
                                                                       data1=t1[:, i * 128:(i + 1) * 128], initial=0.0,
                                                                       op0=ALU.mult, op1=ALU.add),
                             reads=["t1"], writes=["t3"])
                    S.op("act", lambda e: e.activation(out=eL[:], in_=t3[:].rearrange("p (i c) -> p i c", c=128)[:, :, 127],
                                                       func=AF.Exp), reads=["t3"], writes=["eL"])
                    S.op("act", lambda e: e.activation(out=t1[:], in_=t3[:], func=AF.Exp), reads=["t3"], writes=["t1"])
                    S.op("act", lambda e: e.activation(out=t0[:], in_=t3[:], func=AF.Exp, scale=-1.0),
                         reads=["t3"], writes=["t0"])
                    sigmoid_act(S, pZq[:], "b1", t4[:], "t4", C)
                    S.op("dve", lambda e: e.tensor_tensor(out=t4[:], in0=pZq[:], in1=t4[:], op=ALU.mult),
                         reads=["b1", "t4"], writes=["t4"])
                    S.op("dve", lambda e: e.tensor_tensor(out=qT[:], in0=t4[:], in1=t1[:], op=ALU.mult),
                         reads=["t4", "t1"], writes=["qT"])
                    S.op("dve", lambda e: e.tensor_tensor(out=ktT[:], in0=t2[:], in1=t0[:], op=ALU.mult),
                         reads=["t2", "t0"], writes=["ktT"])
                    for i in range(4):
                        S.op("act", lambda e, i=i: e.activation(out=khT[:, i * 128:(i + 1) * 128], in_=ktT[:, i * 128:(i + 1) * 128],
                                                               func=AF.Copy, scale=eL[:, i:i + 1]),
                             reads=["ktT", "eL"], writes=["khT"], c=340.0)
                    sigmoid_act(S, pZg[:], "b3", t4[:], "t4", C)
                    S.op("dve", lambda e, h=h: e.tensor_tensor(out=sg[:, h, :], in0=pZg[:], in1=t4[:], op=ALU.mult),
                         reads=["b3", "t4"], writes=[("sg", h)])
                    for i in range(4):
                        S.op("pe", lambda e, i=i: e.transpose(out=pTt[:, i, :], in_=khT[:, i * 128:(i + 1) * 128], identity=C.ident[:]),
                             reads=["khT"], writes=["pT"])
                    S.op("act", lambda e: e.copy(out=khat[:], in_=pTt[:, 0:4, :]), reads=["pT"], writes=["khat"])
                    for i in range(4):
                        cs = slice(i * 128, (i + 1) * 128)
                        pSc = pSm[:, 0:128]
                        pSt = pN[:, 0:128]
                        S.op("pe", lambda e, cs=cs, pSc=pSc: e.matmul(pSc, lhsT=ktT[:, cs], rhs=qT[:, cs], start=True, stop=True),
                             reads=["ktT", "qT"], writes=["b4"])
                        S.op("dve", lambda e, pSc=pSc: e.tensor_tensor(out=PsT[:], in0=pSc, in1=C.mask_bf[:], op=ALU.mult),
                             reads=["b4"], writes=["PsT"])
                        S.op("pe", lambda e, cs=cs, i=i, h=h: e.matmul(pOh[:, cs], lhsT=v_tok[:, i, h * 128:(h + 1) * 128], rhs=PsT[:],
                                                                      start=True, stop=False),
                             reads=[("v_tok", i), "PsT"], writes=["b5"])
                        S.op("pe", lambda e, cs=cs, h=h: e.matmul(pOh[:, cs], lhsT=Sb[:, h, :], rhs=qT[:, cs], start=False, stop=True),
                             reads=[("Sb", h), "qT"], writes=["b5"])
                        S.op("pe", lambda e, i=i, h=h, pSt=pSt: e.matmul(pSt, lhsT=khat[:, i, :], rhs=v_tok[:, i, h * 128:(h + 1) * 128],
                                                                        start=True, stop=True),
                             reads=["khat", ("v_tok", i)], writes=["b6"])
                        S.op("dve", lambda e, i=i, h=h, pSt=pSt: e.scalar_tensor_tensor(
                            out=Sf[:, h, :], in0=Sf[:, h, :], scalar=eL[:, i:i + 1], in1=pSt, op0=ALU.mult, op1=ALU.add),
                            reads=[("Sf", h), "eL", "b6"], writes=[("Sf", h)])
                        S.op("act", lambda e, h=h: e.copy(out=Sb[:, h, :], in_=Sf[:, h, :]), reads=[("Sf", h)], writes=[("Sb", h)])
                    S.op("act", lambda e: e.activation(out=sq[:], in_=pOh[:], func=AF.Square), reads=["b5"], writes=["sq"])
                    S.op("pe", lambda e: e.matmul(pN[:], lhsT=C.ones_bf[:], rhs=sq[:], start=True, stop=True),
                         reads=["sq"], writes=["b6"])
                    S.op("act", lambda e: e.activation(out=t2[:], in_=pN[:], func=AF.Ln, scale=1.0 / 128.0, bias=C.eps_t[:, 0:1]),
                         reads=["b6"], writes=["t2"])
                    S.op("act", lambda e: e.activation(out=t2[:], in_=t2[:], func=AF.Exp, scale=-0.5), reads=["t2"], writes=["t2"])
                    S.op("dve", lambda e: e.scalar_tensor_tensor(out=t2[:], in0=pOh[:], scalar=C.gon[:, 0:1], in1=t2[:],
                                                                 op0=ALU.mult, op1=ALU.mult), reads=["b5", "t2"], writes=["t2"])
                    S.op("dve", lambda e, h=h: e.tensor_tensor(out=yaT[:, h, :], in0=t2[:], in1=sg[:, h, :], op=ALU.mult),
                         reads=["t2", ("sg", h)], writes=[("yaT", h)])
                if debug:
                    S.op("sp", lambda e, G=G: e.dma_start(out=dbg_ya[G], in_=yaT[:]), reads=[("yaT", h) for h in range(4)], dma="dbgya")
                t0, t1 = tmp[0], tmp[1]
                for dc in range(8):
                    ds_ = slice(dc * 128, (dc + 1) * 128)
                    pGa, pGb, pA, pB = bk[1], bk[2], bk[3], bk[4]
                    for k in range(8):
                        S.op("pe", lambda e, k=k, ds_=ds_: e.matmul(pGa[:], lhsT=w_ga[:, k, ds_], rhs=hT[:, k, :], start=(k == 0), stop=(k == 7)),
                             reads=hTall + ["w_ga"], writes=["b1"])
                    for k in range(8):
                        S.op("pe", lambda e, k=k, ds_=ds_: e.matmul(pGb[:], lhsT=w_gb[:, k, ds_], rhs=hT[:, k, :], start=(k == 0), stop=(k == 7)),
                             reads=hTall + ["w_gb"], writes=["b2"])
                    for k in range(4):
                        S.op("pe", lambda e, k=k, ds_=ds_: e.matmul(pA[:], lhsT=wA[:, k, ds_], rhs=yaT[:, k, :], start=(k == 0), stop=(k == 3)),
                             reads=[("yaT", k), "wA"], writes=["b3"])
                    for k in range(4):
                        S.op("pe", lambda e, k=k, ds_=ds_, G=G: e.matmul(pB[:], lhsT=wB[:, k, ds_], rhs=ybT[:, k, G * 512:(G + 1) * 512],
                                                                        start=(k == 0), stop=(k == 3)),
                             reads=["wB"], writes=["b4"])
                    sigmoid_act(S, pGa[:], "b1", t0[:], "t0", C)
                    sigmoid_act(S, pGb[:], "b2", t1[:], "t1", C)
                    S.op("dve", lambda e: e.tensor_tensor(out=t0[:], in0=pA[:], in1=t0[:], op=ALU.mult), reads=["b3", "t0"], writes=["t0"])
                    S.op("dve", lambda e: e.tensor_tensor(out=t1[:], in0=pB[:], in1=t1[:], op=ALU.mult), reads=["b4", "t1"], writes=["t1"])
                    S.op("dve", lambda e, dc=dc: e.tensor_tensor(out=mT[:, dc, :], in0=t0[:], in1=t1[:], op=ALU.add),
                         reads=["t0", "t1"], writes=[("mT", dc)])
                for i in range(4):
                    ti = 4 * G + i
                    xt, xk = xts[i]
                    for half in range(2):
                        pX = bk[5 + half]
                        for k in range(8):
                            S.op("pe", lambda e, k=k, i=i, half=half, pX=pX: e.matmul(
                                pX[:], lhsT=mT[:, k, i * 128:(i + 1) * 128], rhs=wO[:, k, half * 512:(half + 1) * 512],
                                start=(k == 0), stop=(k == 7)),
                                reads=[("mT", k), "wO"], writes=[f"b{5 + half}"])
                        S.op("dve", lambda e, xt=xt, half=half, pX=pX: e.tensor_tensor(
                            out=xt[:, half * 512:(half + 1) * 512], in0=pX[:], in1=xt[:, half * 512:(half + 1) * 512], op=ALU.add),
                            reads=[f"b{5 + half}", xk], writes=[xk])
                    S.op("sp", lambda e, xt=xt, ti=ti: e.dma_start(out=x1d[ti * 128:(ti + 1) * 128, :], in_=xt[:]),
                         reads=[xk], dma="st_" + xk)
            S.emit()

        st_yb.close()
        TB = 4
        NB = TB * 128
        with contextlib.ExitStack() as st:
            A = Alloc(nc, st, "b")
            S = Sched(nc)
            S.excl.update(["pT"] + [f"b{i}" for i in range(7)])
            wG = A.sb("wG", [128, 8, DFF], BF16)
            wU = A.sb("wU", [128, 8, DFF], BF16)
            wD = A.sb("wD", [128, NFF, 1024], BF16)
            wPG = A.sb("wPG", [128, 8, 1024], BF16)
            wPP = A.sb("wPP", [128, 2, 1024], BF16)
            gffnB = A.sb("gffnB", [128, 8, 128], BF16)
            gpleB = A.sb("gpleB", [128, 8, 128], BF16)
            gT2 = A.sb("gT2", [128, 8], F32)
            gT3 = A.sb("gT3", [128, 8], F32)
            h2T = A.sb("h2T", [128, 8, NB], BF16)
            actT = A.sb("actT", [128, NFF, NB], BF16)
            tr = Ring([(A.sb(f"t{i}", [128, 512], F32), f"t{i}") for i in range(2)])
            pbr = Ring([(A.sb(f"pb{i}", [128, 256], BF16), f"pb{i}") for i in range(2)])
            ppT = A.sb("ppT", [128, 2, 128], BF16)
            xaring = Ring([(A.sb("xa0", [128, 1024], F32), "xa0")])
            xring = Ring([(A.sb(f"x{i}", [128, 1024], F32), f"x{i}") for i in range(2)])
            h3T = h2T[:, :, 0:128]
            W = NS()
            W.ssq = Ring([(A.sb(f"ssq{i}", [128, 4], F32), f"ssq{i}") for i in range(2)])
            W.hb = Ring([(A.sb("hb0", [128, 1024], BF16), "hb0")])
            pTt = A.ps("pT", [128, 8, 128], BF16)
            W.pT = Ring([(pTt, "pT")])
            bk = [A.ps(f"b{i}", [128, 512], F32) for i in range(7)]
            print("B sbuf bytes/partition", A.bytes + A0.bytes)

            S.op("sp", lambda e: e.dma_start(out=gT2[:], in_=norm_ffn_g.rearrange("o (k p) -> p (o k)", p=128),
                                             allow_slow_non_contiguous=True), writes=["gT2"], dma="gT2")
            S.op("sp", lambda e: e.dma_start(out=gT3[:], in_=norm_ple_g.rearrange("o (k p) -> p (o k)", p=128),
                                             allow_slow_non_contiguous=True), writes=["gT3"], dma="gT3")
            S.op("dve", lambda e: e.tensor_copy(out=gffnB[:], in_=gT2[:].unsqueeze(2).to_broadcast([128, 8, 128])),
                 reads=["gT2"], writes=["gffnB"])
            S.op("dve", lambda e: e.tensor_copy(out=gpleB[:], in_=gT3[:].unsqueeze(2).to_broadcast([128, 8, 128])),
                 reads=["gT3"], writes=["gpleB"])
            for blk in range(4):
                for (dst_, key_, src_) in ((wG, "wG", w_fg), (wU, "wU", w_fu)):
                    for k in range(8):
                        S.op("pool", lambda e, k=k, blk=blk, dst_=dst_, src_=src_: e.dma_start(
                            out=dst_[:, k, blk * 704:(blk + 1) * 704], in_=src_[k * 128:(k + 1) * 128, blk * 704:(blk + 1) * 704]),
                            writes=[(key_, blk)], dma=f"{key_}{blk}", nodep=True)
            load_w(S, wD, "wD", w_fd, 0, NFF, 0, 1024)
            load_w(S, wPG, "wPG", w_pg, 0, 8, 0, 1024)
            load_w(S, wPP, "wPP", w_pp, 0, 2, 0, 1024)

            for g in range(NT // TB):
                for i in range(TB):
                    ti = g * TB + i
                    xt, xk = xaring.next()
                    S.op("sp", lambda e, xt=xt, ti=ti: e.dma_start(out=xt[:], in_=x1d[ti * 128:(ti + 1) * 128, :]),
                         writes=[xk], dma=xk)
                    norm_T(S, C, xt[:], xk, gffnB, h2T[:, :, i * 128:(i + 1) * 128], ("h2T", i), W)
                h2all = [("h2T", i) for i in range(TB)]
                for f in range(NFF):
                    fs = slice(f * 128, (f + 1) * 128)
                    pG = bk[0 + (f % 2)]
                    pU = bk[2 + (f % 2)]
                    kG = f"b{0 + (f % 2)}"
                    kU = f"b{2 + (f % 2)}"
                    for k in range(8):
                        S.op("pe", lambda e, k=k, fs=fs, pG=pG: e.matmul(pG[:, 0:NB], lhsT=wG[:, k, fs], rhs=h2T[:, k, :],
                                                                        start=(k == 0), stop=(k == 7)),
                             reads=h2all + [("wG", b_) for b_ in sorted({(f * 128) // 704, (f * 128 + 127) // 704})], writes=[kG])
                    for k in range(8):
                        S.op("pe", lambda e, k=k, fs=fs, pU=pU: e.matmul(pU[:, 0:NB], lhsT=wU[:, k, fs], rhs=h2T[:, k, :],
                                                                        start=(k == 0), stop=(k == 7)),
                             reads=h2all + [("wU", b_) for b_ in sorted({(f * 128) // 704, (f * 128 + 127) // 704})], writes=[kU])
                    t, tk = tr.next()
                    sigmoid_act(S, pG[:, 0:NB], kG, t[:, 0:NB], tk, C)
                    S.op("dve", lambda e, t=t, pG=pG: e.tensor_tensor(out=t[:, 0:NB], in0=pG[:, 0:NB], in1=t[:, 0:NB], op=ALU.mult),
                         reads=[kG, tk], writes=[tk])
                    S.op("dve", lambda e, t=t, pU=pU, f=f: e.tensor_tensor(out=actT[:, f, :], in0=pU[:, 0:NB], in1=t[:, 0:NB], op=ALU.mult),
                         reads=[kU, tk], writes=[("actT", f)])
                for i in range(TB):
                    ti = g * TB + i
                    xt, xk = xring.next()
                    S.op("sp", lambda e, xt=xt, ti=ti: e.dma_start(out=xt[:], in_=x1d[ti * 128:(ti + 1) * 128, :]),
                         writes=[xk], dma=xk)
                    for half in range(2):
                        pD = bk[4 + half]
                        kD = f"b{4 + half}"
                        for f in range(NFF):
                            S.op("pe", lambda e, f=f, i=i, half=half, pD=pD: e.matmul(
                                pD[:], lhsT=actT[:, f, i * 128:(i + 1) * 128], rhs=wD[:, f, half * 512:(half + 1) * 512],
                                start=(f == 0), stop=(f == NFF - 1)),
                                reads=[("actT", f), "wD"], writes=[kD])
                        S.op("dve", lambda e, xt=xt, half=half, pD=pD: e.tensor_tensor(
                            out=xt[:, half * 512:(half + 1) * 512], in0=pD[:], in1=xt[:, half * 512:(half + 1) * 512], op=ALU.add),
                            reads=[kD, xk], writes=[xk])
                    norm_T(S, C, xt[:], xk, gpleB, h3T, ("h2T", 0), W)
                    pb, pbk = pbr.next()
                    S.op("pool", lambda e, pb=pb, ti=ti: e.dma_start(out=pb[:], in_=p_in[ti * 128:(ti + 1) * 128, :]),
                         writes=[pbk], dma=pbk)
                    for c in range(2):
                        S.op("pe", lambda e, c=c, pb=pb: e.transpose(out=pTt[:, c, :], in_=pb[:, c * 128:(c + 1) * 128], identity=C.ident[:]),
                             reads=[pbk], writes=["pT"])
                    S.op("act", lambda e: e.copy(out=ppT[:], in_=pTt[:, 0:2, :]), reads=["pT"], writes=["ppT"])
                    for half in range(2):
                        hs = slice(half * 512, (half + 1) * 512)
                        pPG = bk[4 + half]
                        kPG = f"b{4 + half}"
                        pPP = bk[6]
                        for k in range(8):
                            S.op("pe", lambda e, k=k, hs=hs, pPG=pPG: e.matmul(pPG[:], lhsT=h3T[:, k, :], rhs=wPG[:, k, hs],
                                                                              start=(k == 0), stop=(k == 7)),
                                 reads=[("h2T", 0), "wPG"], writes=[kPG])
                        for k in range(2):
                            S.op("pe", lambda e, k=k, hs=hs: e.matmul(pPP[:], lhsT=ppT[:, k, :], rhs=wPP[:, k, hs],
                                                                     start=(k == 0), stop=(k == 1)),
                                 reads=["ppT", "wPP"], writes=["b6"])
                        t, tk = tr.next()
                        sigmoid_act(S, pPG[:], kPG, t[:], tk, C)
                        S.op("dve", lambda e, t=t: e.tensor_tensor(out=t[:], in0=pPP[:], in1=t[:], op=ALU.mult),
                             reads=["b6", tk], writes=[tk])
                        S.op("dve", lambda e, t=t, xt=xt, hs=hs: e.tensor_tensor(out=xt[:, hs], in0=xt[:, hs], in1=t[:], op=ALU.add),
                             reads=[tk, xk], writes=[xk])
                    S.op("sp", lambda e, xt=xt, ti=ti: e.dma_start(out=out[ti * 128:(ti + 1) * 128, :], in_=xt[:]),
                         reads=[xk], dma="ost_" + xk)
            S.emit()
    return nc


_IN_NAMES = ["x", "p", "norm_mix_g", "w_in", "hg_lb_logits", "hg_onorm_g", "fox_f_bias", "fox_q_norm_g",
             "fox_k_norm_g", "w_branch_a", "w_branch_b", "w_out", "norm_ffn_g", "w_ffn_gate", "w_ffn_up",
             "w_ffn_down", "norm_ple_g", "w_ple_gate", "w_ple_proj"]


def make_in_maps(inputs, n_cores, SEQ):
    f = lambda a: np.ascontiguousarray(np.asarray(a, dtype=np.float32))
    shared = {}
    for k in _IN_NAMES:
        if k in ("x", "p"):
            continue
        a = f(inputs[k])
        if k == "hg_lb_logits":
            shared[k] = a.reshape(2, 512)
        else:
            shared[k] = a.reshape(a.shape[-2] if a.ndim >= 2 else 1, a.shape[-1])
    xs = f(inputs["x"])
    ps = f(inputs["p"])[0]
    maps = []
    for c in range(n_cores):
        m = dict(shared)
        m["x"] = np.ascontiguousarray(xs[c])
        m["p"] = np.ascontiguousarray(ps[c])
        maps.append(m)
    return maps


def kernel(**inputs):
    x = np.asarray(inputs["x"])
    B, SEQ, _ = x.shape
    nc = build_program(SEQ)
    in_maps = make_in_maps(inputs, B, SEQ)
    res = run_bass_kernel_spmd(nc, in_maps, core_ids=list(range(B)))
    outs = [np.asarray(r["out"], dtype=np.float32) for r in res.results]
    return np.stack(outs, axis=0)
```

```python
import contextlib
import numpy as np
import concourse.bass as bass
import concourse.mybir as mybir
from concourse.bass_utils import run_bass_kernel_spmd

F32 = mybir.dt.float32
BF16 = mybir.dt.bfloat16
AF = mybir.ActivationFunctionType
ALU = mybir.AluOpType

ENGS = ("pe", "act", "dve", "pool", "sp")
EPS = 1e-6
HQ, HF, HI, HGC, FQ, FK, FV, FFC, GA, GB = 0, 512, 1024, 1536, 2048, 2560, 3072, 3584, 3592, 4616
INC = 5640
DFF = 2816
NFF = DFF // 128


class _Op:
    __slots__ = ("eng", "fn", "dma", "preds", "deps", "signal", "sig_idx", "gid", "pos", "dma_val", "cost",
                 "fin", "done")

    def __init__(self, eng, fn, dma, cost):
        self.eng = eng
        self.fn = fn
        self.dma = dma
        self.cost = cost
        self.preds = []
        self.deps = []
        self.signal = False
        self.sig_idx = None
        self.gid = None
        self.pos = None
        self.dma_val = None
        self.fin = None
        self.done = False


_DEF_COST = {"pe": 260.0, "act": 700.0, "dve": 720.0, "pool": 600.0, "sp": 100.0}


class Sched:
    SEM_ROT = 16000
    WINDOW = 256
    REORDER = True

    def __init__(self, nc):
        self.nc = nc
        self.ops = []
        self.lastw = {}
        self.readers = {}
        self.excl = set()

    def op(self, eng, fn, reads=(), writes=(), dma=None, nodep=False, c=None):
        if self.excl:
            writes = list(writes) + [k for k in reads if k in self.excl]
            reads = [k for k in reads if k not in self.excl]
        if c is None:
            c = 2600.0 if dma is not None else _DEF_COST[eng]
        o = _Op(eng, fn, dma, c)
        o.gid = len(self.ops)
        preds = {}
        if not nodep:
            for k in reads:
                w = self.lastw.get(k)
                if w is not None:
                    preds[id(w)] = (w, True)
            for k in writes:
                w = self.lastw.get(k)
                if w is not None and id(w) not in preds:
                    preds[id(w)] = (w, False)
                for r in self.readers.get(k, ()):
                    if id(r) not in preds and r is not o:
                        preds[id(r)] = (r, False)
        o.preds = list(preds.values())
        for k in writes:
            self.lastw[k] = o
            self.readers[k] = []
        for k in reads:
            self.readers.setdefault(k, []).append(o)
        self.ops.append(o)
        return o

    def _schedule(self):
        pend = {e: [o for o in self.ops if o.eng == e] for e in ENGS}
        if not self.REORDER:
            return pend
        order = {e: [] for e in ENGS}
        free = {e: 0.0 for e in ENGS}
        head = {e: 0 for e in ENGS}
        nleft = len(self.ops)
        while nleft:
            best = None
            for e in ENGS:
                L = pend[e]
                h = head[e]
                while h < len(L) and L[h].done:
                    h += 1
                head[e] = h
                if h >= len(L):
                    continue
                cand = None
                cnt = 0
                i = h
                while i < len(L) and cnt < self.WINDOW:
                    o = L[i]
                    i += 1
                    if o.done:
                        continue
                    cnt += 1
                    rdy = 0.0
                    ok = True
                    for p, _ in o.preds:
                        if not p.done:
                            ok = False
                            break
                        t = p.fin + (0.0 if p.eng == e and p.dma is None else 120.0)
                        if t > rdy:
                            rdy = t
                    if not ok:
                        continue
                    st = rdy if rdy > free[e] else free[e]
                    if cand is None or st < cand[0] - 1e-9:
                        cand = (st, o)
                    if rdy <= free[e]:
                        break
                if cand is not None and (best is None or cand[0] < best[0] - 1e-9 or
                                         (abs(cand[0] - best[0]) <= 1e-9 and cand[1].gid < best[1].gid)):
                    best = cand
            st, o = best
            e = o.eng
            occ = (1000.0 if e == "pool" else 60.0) if o.dma is not None else o.cost
            free[e] = st + occ
            o.fin = st + o.cost
            o.done = True
            o.pos = len(order[e])
            order[e].append(o)
            nleft -= 1
        self.est_ns = max(o.fin for o in self.ops) if self.ops else 0.0
        busy = {e: 0.0 for e in ENGS}
        for o in self.ops:
            busy[o.eng] += ((1000.0 if o.eng == "pool" else 60.0) if o.dma is not None else o.cost)
        print("  busy us", {e: round(v / 1e3) for e, v in busy.items()})
        return order

    def emit(self):
        nc = self.nc
        order = self._schedule()
        for e in ENGS:
            for i, o in enumerate(order[e]):
                o.pos = i
        dma_cnt = {}
        for e in ENGS:
            for o in order[e]:
                best = {}
                for p, is_raw in o.preds:
                    if p.dma is not None:
                        o.deps.append(p)
                        continue
                    if p.eng == e:
                        assert p.pos < o.pos
                        if not is_raw or e == "pe":
                            continue
                    b = best.get(p.eng)
                    if b is None or p.pos > b.pos:
                        best[p.eng] = p
                for p in best.values():
                    o.deps.append(p)
                    p.signal = True
                if o.dma is not None:
                    c = dma_cnt.get(o.dma, 0) + 1
                    dma_cnt[o.dma] = c
                    o.dma_val = 16 * c
        with contextlib.ExitStack() as st:
            eng_sems = {}
            for e in ENGS:
                n = 0
                for o in order[e]:
                    if o.signal:
                        o.sig_idx = n
                        n += 1
                nsem = max(1, (n + self.SEM_ROT - 1) // self.SEM_ROT)
                print("  sched", e, "ops", len(order[e]), "signals", n)
                eng_sems[e] = [st.enter_context(nc.semaphore(f"s_{e}{i}")) for i in range(nsem)]
            print("  est ns", getattr(self, "est_ns", None))
            dma_sems = {k: st.enter_context(nc.semaphore(f"d_{k}")) for k in dma_cnt}
            blk_st = contextlib.ExitStack()
            block = blk_st.enter_context(nc.Block())

            def target(p):
                if p.dma is not None:
                    return (dma_sems[p.dma], p.dma_val, ("d", p.dma))
                j, v = divmod(p.sig_idx, self.SEM_ROT)
                return (eng_sems[p.eng][j], v + 1, (p.eng, j))

            def run(e, handle):
                waited = {}
                for o in order[e]:
                    need = {}
                    for p in o.deps:
                        sem, val, key = target(p)
                        if waited.get(key, 0) >= val:
                            continue
                        if key not in need or need[key][1] < val:
                            need[key] = (sem, val)
                    for key, (sem, val) in need.items():
                        handle.wait_ge(sem, val)
                        waited[key] = val
                    ins = o.fn(handle)
                    if o.dma is not None:
                        ins.then_inc(dma_sems[o.dma], 16)
                    elif o.signal:
                        j, _ = divmod(o.sig_idx, self.SEM_ROT)
                        ins.then_inc(eng_sems[e][j], 1)
                if e == "sp":
                    for k, c in dma_cnt.items():
                        if waited.get(("d", k), 0) < 16 * c:
                            handle.wait_ge(dma_sems[k], 16 * c)

            block.tensor(lambda h: run("pe", h))
            block.scalar(lambda h: run("act", h))
            block.vector(lambda h: run("dve", h))
            block.gpsimd(lambda h: run("pool", h))
            block.sync(lambda h: run("sp", h))
            blk_st.close()
            nc.all_engine_barrier()


class Ring:
    def __init__(self, items):
        self.items = items
        self.i = 0

    def next(self):
        r = self.items[self.i % len(self.items)]
        self.i += 1
        return r


class NS:
    pass


class Alloc:
    def __init__(self, nc, st, tag):
        self.nc, self.st, self.tag, self.bytes = nc, st, tag, 0

    def sb(self, name, shape, dt):
        n = 1
        for s in shape[1:]:
            n *= s
        self.bytes += n * (4 if dt == F32 else 2)
        return self.st.enter_context(self.nc.sbuf_tensor(f"{self.tag}_{name}", list(shape), dt))

    def ps(self, name, shape, dt):
        return self.st.enter_context(self.nc.psum_tensor(f"{self.tag}_{name}", list(shape), dt))


def load_w(S, dst, key, src, r0, nk, c0, ncols, cstep=512):
    for k in range(nk):
        for cc in range(0, ncols, cstep):
            w = min(cstep, ncols - cc)
            S.op("pool", lambda e, k=k, cc=cc, w=w: e.dma_start(
                out=dst[:, k, cc:cc + w], in_=src[r0 + k * 128:r0 + (k + 1) * 128, c0 + cc:c0 + cc + w]),
                writes=[key], dma=key, nodep=True)


def norm_T(S, C, X, xkey, gB, dst, dkey, W, scale_eng="act"):
    ssq, sk = W.ssq.next()
    hb, hk = W.hb.next()
    pT, pk = W.pT.next()
    S.op("act", lambda e: e.activation(out=hb[:], in_=X, func=AF.Square, accum_out=ssq[:, 0:1]),
         reads=[xkey], writes=[hk, sk], c=1100.0)
    S.op("act", lambda e: e.activation(out=ssq[:, 1:2], in_=ssq[:, 0:1], func=AF.Ln, scale=1.0 / 1024.0,
                                       bias=C.eps_t[:, 0:1]), reads=[sk], writes=[sk], c=260.0)
    S.op("act", lambda e: e.activation(out=ssq[:, 2:3], in_=ssq[:, 1:2], func=AF.Exp, scale=-0.5),
         reads=[sk], writes=[sk], c=260.0)
    if scale_eng == "act":
        S.op("act", lambda e: e.activation(out=hb[:], in_=X, func=AF.Copy, scale=ssq[:, 2:3]),
             reads=[xkey, sk], writes=[hk], c=1100.0)
    else:
        S.op("dve", lambda e: e.tensor_scalar(out=hb[:], in0=X, scalar1=ssq[:, 2:3], scalar2=None, op0=ALU.mult),
             reads=[xkey, sk], writes=[hk], c=1150.0)
    for k in range(8):
        S.op("pe", lambda e, k=k: e.transpose(out=pT[:, k, :], in_=hb[:, k * 128:(k + 1) * 128], identity=C.ident[:]),
             reads=[hk], writes=[pk], c=110.0)
    S.op("dve", lambda e: e.tensor_tensor(out=dst, in0=pT[:], in1=gB[:], op=ALU.mult), reads=[pk], writes=[dkey], c=1100.0)


def sigmoid_act(S, src, skey, dst, dkey, C, np_=128):
    S.op("act", lambda e: e.activation(out=dst, in_=src, func=AF.Exp, scale=-1.0), reads=[skey], writes=[dkey])
    S.op("act", lambda e: e.activation(out=dst, in_=dst, func=AF.Ln, bias=C.one_t[0:np_, 0:1], scale=1.0),
         reads=[dkey], writes=[dkey])
    S.op("act", lambda e: e.activation(out=dst, in_=dst, func=AF.Exp, scale=-1.0), reads=[dkey], writes=[dkey])


def build_program(SEQ=4096, debug=False):
    NT = SEQ // 128
    NG = SEQ // 512
    nc = bass.Bass("TRN2", target_bir_lowering=False)

    def din(name, shape):
        return nc.dram_tensor(name, list(shape), F32, kind="ExternalInput").ap()

    x = din("x", [SEQ, 1024])
    p_in = din("p", [SEQ, 256])
    norm_mix_g = din("norm_mix_g", [1, 1024])
    w_in = din("w_in", [1024, INC])
    hg_lb = din("hg_lb_logits", [2, 512])
    hg_on = din("hg_onorm_g", [1, 128])
    fox_fb = din("fox_f_bias", [1, 8])
    fox_qg = din("fox_q_norm_g", [1, 64])
    fox_kg = din("fox_k_norm_g", [1, 64])
    w_ba = din("w_branch_a", [512, 1024])
    w_bb = din("w_branch_b", [512, 1024])
    w_out = din("w_out", [1024, 1024])
    norm_ffn_g = din("norm_ffn_g", [1, 1024])
    w_fg = din("w_ffn_gate", [1024, DFF])
    w_fu = din("w_ffn_up", [1024, DFF])
    w_fd = din("w_ffn_down", [DFF, 1024])
    norm_ple_g = din("norm_ple_g", [1, 1024])
    w_pg = din("w_ple_gate", [1024, 1024])
    w_pp = din("w_ple_proj", [256, 1024])
    out = nc.dram_tensor("out", [SEQ, 1024], F32, kind="ExternalOutput").ap()
    if debug:
        x1d = nc.dram_tensor("x1d", [SEQ, 1024], F32, kind="ExternalOutput").ap()
        dbg_yb = nc.dram_tensor("dbg_yb", [128, 4, SEQ], BF16, kind="ExternalOutput").ap()
        dbg_ya = nc.dram_tensor("dbg_ya", [SEQ // 512, 128, 4, 512], BF16, kind="ExternalOutput").ap()
    else:
        x1d = nc.dram_tensor("x1d", [SEQ, 1024], F32, kind="ExternalOutput").ap()

    with contextlib.ExitStack() as st0:
        A0 = Alloc(nc, st0, "c")
        C = NS()
        C.ident = A0.sb("ident", [128, 128], BF16)
        C.ones_bf = A0.sb("ones_bf", [128, 128], BF16)
        C.eps_t = A0.sb("eps_t", [128, 1], F32)
        C.one_t = A0.sb("one_t", [128, 1], F32)
        C.lbT = A0.sb("lbT", [128, 4], F32)
        C.omlT = A0.sb("omlT", [128, 4], F32)
        C.nomlT = A0.sb("nomlT", [128, 4], F32)
        C.gon = A0.sb("gon", [128, 1], F32)
        C.qg = A0.sb("qg", [128, 1], F32)
        C.kg = A0.sb("kg", [128, 1], F32)
        C.fbias = A0.sb("fbias", [128, 8], F32)
        st_yb = contextlib.ExitStack()
        A_yb = Alloc(nc, st_yb, "yb")
        ybT = A_yb.sb("ybT", [128, 4, SEQ], BF16)
        C.mask_bf = A_yb.sb("mask_bf", [128, 128], BF16)
        C.ones_f = A_yb.sb("ones_f", [128, 128], F32)
        C.U_f = A_yb.sb("U_f", [128, 128], F32)
        C.E127 = A_yb.sb("E127", [128, 128], F32)
        C.gmixB = A_yb.sb("gmixB", [128, 8, 128], BF16)
        C.bd_bf = A_yb.sb("bd_bf", [128, 128], BF16)

        with contextlib.ExitStack() as st:
            A = Alloc(nc, st, "p0")
            S = Sched(nc)
            tmpf = A.sb("tmpf", [128, 128], F32)
            gT = A.sb("gT", [128, 8], F32)
            l0 = A.sb("l0", [128, 4], F32)
            l1 = A.sb("l1", [128, 4], F32)
            S.op("pool", lambda e: e.memset(C.ones_f[:], 1.0), writes=["ones_f"])
            S.op("pool", lambda e: e.memset(C.eps_t[:], EPS), writes=["eps"])
            S.op("pool", lambda e: e.memset(C.one_t[:], 1.0), writes=["one"])
            S.op("pool", lambda e: e.memset(C.ones_bf[:], 1.0), writes=["ones_bf"])
            S.op("pool", lambda e: e.affine_select(out=tmpf[:], in_=C.ones_f[:], pattern=[[-1, 128]],
                                                   compare_op=ALU.is_equal, fill=0.0, base=0, channel_multiplier=1),
                 reads=["ones_f"], writes=["tmpf"])
            S.op("dve", lambda e: e.tensor_copy(out=C.ident[:], in_=tmpf[:]), reads=["tmpf"], writes=["ident"])
            S.op("pool", lambda e: e.affine_select(out=C.U_f[:], in_=C.ones_f[:], pattern=[[1, 128]],
                                                   compare_op=ALU.is_ge, fill=0.0, base=0, channel_multiplier=-1),
                 reads=["ones_f"], writes=["U"])
            S.op("dve", lambda e: e.tensor_copy(out=C.mask_bf[:], in_=C.U_f[:]), reads=["U"], writes=["mask"])
            S.op("pool", lambda e: e.affine_select(out=C.E127[:], in_=C.ones_f[:], pattern=[[0, 128]],
                                                   compare_op=ALU.is_equal, fill=0.0, base=-127, channel_multiplier=1),
                 reads=["ones_f"], writes=["E127"])
            S.op("sp", lambda e: e.dma_start(out=gT[:], in_=norm_mix_g.rearrange("o (k p) -> p (o k)", p=128),
                                             allow_slow_non_contiguous=True), writes=["gT"], dma="gT")
            S.op("dve", lambda e: e.tensor_copy(out=C.gmixB[:], in_=gT[:].unsqueeze(2).to_broadcast([128, 8, 128])),
                 reads=["gT"], writes=["gmixB"])
            S.op("sp", lambda e: e.dma_start(out=l0[:], in_=hg_lb[0:1, :].rearrange("o (h c) -> c (o h)", c=128),
                                             allow_slow_non_contiguous=True), writes=["l0"], dma="l0")
            S.op("sp", lambda e: e.dma_start(out=l1[:], in_=hg_lb[1:2, :].rearrange("o (h c) -> c (o h)", c=128),
                                             allow_slow_non_contiguous=True), writes=["l1"], dma="l1")
            S.op("sp", lambda e: e.dma_start(out=C.gon[:], in_=hg_on.rearrange("o v -> v o"),
                                             allow_slow_non_contiguous=True), writes=["gon"], dma="gon")
            for hh in range(2):
                S.op("sp", lambda e, hh=hh: e.dma_start(out=C.qg[hh * 64:(hh + 1) * 64, :], in_=fox_qg.rearrange("o v -> v o"),
                                                        allow_slow_non_contiguous=True), writes=[("qg", hh)], dma=f"qg{hh}")
                S.op("sp", lambda e, hh=hh: e.dma_start(out=C.kg[hh * 64:(hh + 1) * 64, :], in_=fox_kg.rearrange("o v -> v o"),
                                                        allow_slow_non_contiguous=True), writes=[("kg", hh)], dma=f"kg{hh}")
            S.op("pool", lambda e: e.memset(C.bd_bf[:], 0.0), writes=["bd"])
            S.op("pool", lambda e: e.memset(C.bd_bf[0:64, 0:64], 1.0), reads=["bd"], writes=["bd"])
            S.op("pool", lambda e: e.memset(C.bd_bf[64:128, 64:128], 1.0), reads=["bd"], writes=["bd"])
            S.op("sp", lambda e: e.dma_start(out=C.fbias[:], in_=fox_fb.partition_broadcast(128)),
                 writes=["fbias"], dma="fbias")
            S.op("dve", lambda e: e.tensor_scalar(out=C.qg[:], in0=C.qg[:], scalar1=0.125, scalar2=None, op0=ALU.mult),
                 reads=[("qg", 0), ("qg", 1)], writes=["qg"])
            S.op("dve", lambda e: e.tensor_tensor(out=l1[:], in0=l1[:], in1=l0[:], op=ALU.subtract),
                 reads=["l0", "l1"], writes=["l1"])
            S.op("act", lambda e: e.activation(out=l1[:], in_=l1[:], func=AF.Exp), reads=["l1"], writes=["l1"])
            S.op("dve", lambda e: e.tensor_scalar_add(out=l1[:], in0=l1[:], scalar1=1.0), reads=["l1"], writes=["l1"])
            S.op("dve", lambda e: e.reciprocal(out=C.lbT[:], in_=l1[:]), reads=["l1"], writes=["lbT"])
            S.op("dve", lambda e: e.tensor_scalar(out=C.omlT[:], in0=C.lbT[:], scalar1=-1.0, scalar2=1.0,
                                                  op0=ALU.mult, op1=ALU.add), reads=["lbT"], writes=["omlT"])
            S.op("dve", lambda e: e.tensor_scalar(out=C.nomlT[:], in0=C.lbT[:], scalar1=-1.0, scalar2=None,
                                                  op0=ALU.add), reads=["lbT"], writes=["nomlT"])
            S.emit()

        with contextlib.ExitStack() as st:
            A = Alloc(nc, st, "a1")
            S = Sched(nc)
            S.excl.update(["pT", "psV", "psQ", "psN", "psS0", "psS1", "psQ", "psO1"])
            w_fq = A.sb("w_fq", [128, 8, 512], BF16)
            w_fk = A.sb("w_fk", [128, 8, 512], BF16)
            w_fv = A.sb("w_fv", [128, 8, 512], BF16)
            w_ff = A.sb("w_ff", [128, 8, 8], BF16)
            KT = A.sb("KT", [65, 8, SEQ], BF16)
            Vc = A.sb("Vc", [128, NT, 8, 64], BF16)
            cP = A.sb("cP", [128, NT, 8], F32)
            biasG = A.sb("biasG", [128, NT, 8], F32)
            Aaug = A.sb("Aaug", [128, 4, 8], BF16)
            hT = A.sb("hT", [128, 8, 512], BF16)
            sqr = Ring([(A.sb(f"sq{i}", [128, 512], BF16), f"sq{i}") for i in range(2)])
            rsr = Ring([(A.sb(f"rs{i}", [128, 512], F32), f"rs{i}") for i in range(2)])
            rinv_sb = A.sb("rinv_sb", [64, 512], F32)
            zb = A.sb("zb", [128, 8], F32)
            cref = A.sb("cref", [128, 8], F32)
            xring = Ring([(A.sb(f"x{i}", [128, 1024], F32), f"x{i}") for i in range(3)])
            W = NS()
            W.ssq = Ring([(A.sb(f"ssq{i}", [128, 4], F32), f"ssq{i}") for i in range(2)])
            W.hb = Ring([(A.sb(f"hb{i}", [128, 1024], BF16), f"hb{i}") for i in range(2)])
            QTs = [(A.sb(f"QT{i}", [65, 512], BF16), f"QT{i}") for i in range(8)]
            PTr = Ring([(A.sb(f"PT{i}", [128, 512], BF16), f"PT{i}") for i in range(3)])
            pTt = A.ps("pT", [128, 8, 128], BF16)
            W.pT = Ring([(pTt, "pT")])
            psV = A.ps("psV", [128, 512], F32)
            psQ = A.ps("psQ", [128, 512], F32)
            psN = A.ps("psN", [128, 512], F32)
            psSr = Ring([(A.ps(f"psS{i}", [128, 512], F32), f"psS{i}") for i in range(2)])
            psOr = Ring([(A.ps(f"psO{i}", [128, 512], F32), f"psO{i}") for i in range(2)])
            print("A1 sbuf bytes/partition", A.bytes + A0.bytes + A_yb.bytes)

            S.op("pool", lambda e: e.memset(KT[64:65, :, :], 1.0), writes=["KTones"])
            load_w(S, w_fv, "w_fv", w_in, 0, 8, FV, 512)
            for k in range(8):
                S.op("pool", lambda e, k=k: e.dma_start(out=w_ff[:, k, :], in_=w_in[k * 128:(k + 1) * 128, FFC:FFC + 8]),
                     writes=["w_ff"], dma="w_ff", nodep=True)
            load_w(S, w_fq, "w_fq", w_in, 0, 8, FQ, 512)
            load_w(S, w_fk, "w_fk", w_in, 0, 8, FK, 512)

            for G in range(NG):
                for i in range(4):
                    ti = 4 * G + i
                    xt, xk = xring.next()
                    S.op("sp", lambda e, xt=xt, ti=ti: e.dma_start(out=xt[:], in_=x[ti * 128:(ti + 1) * 128, :]),
                         writes=[xk], dma=xk)
                    norm_T(S, C, xt[:], xk, C.gmixB, hT[:, :, i * 128:(i + 1) * 128], ("hT", i), W, scale_eng="dve")
                    for k in range(8):
                        S.op("pe", lambda e, k=k, i=i: e.matmul(psV[:], lhsT=hT[:, k, i * 128:(i + 1) * 128], rhs=w_fv[:, k, :],
                                                               start=(k == 0), stop=(k == 7)),
                             reads=[("hT", i), "w_fv"], writes=["psV"])
                    S.op("dve", lambda e, ti=ti: e.tensor_copy(out=Vc[:, ti, :, :],
                                                               in_=psV[:].rearrange("p (h d) -> p h d", d=64)),
                         reads=["psV"], writes=[("Vc", ti)])
                    psF = psQ[:, 0:8]
                    psC = psQ[:, 8:16]
                    for k in range(8):
                        S.op("pe", lambda e, k=k, i=i, psF=psF: e.matmul(psF, lhsT=hT[:, k, i * 128:(i + 1) * 128], rhs=w_ff[:, k, :],
                                                                        start=(k == 0), stop=(k == 7)),
                             reads=[("hT", i), "w_ff"], writes=["psQ"])
                    S.op("dve", lambda e, psF=psF: e.tensor_tensor(out=zb[:], in0=psF, in1=C.fbias[:], op=ALU.add),
                         reads=["psQ"], writes=["zb"])
                    S.op("act", lambda e: e.activation(out=zb[:], in_=zb[:], func=AF.Exp, scale=-1.0),
                         reads=["zb"], writes=["zb"])
                    S.op("act", lambda e: e.activation(out=zb[:], in_=zb[:], func=AF.Ln, bias=C.one_t[:, 0:1], scale=1.0),
                         reads=["zb"], writes=["zb"])
                    S.op("pe", lambda e, ti=ti, psC=psC: e.matmul(psC, lhsT=C.U_f[:], rhs=zb[:], start=True, stop=(ti == 0)),
                         reads=["zb"], writes=["psQ"])
                    if ti > 0:
                        S.op("pe", lambda e, ti=ti, psC=psC: e.matmul(psC, lhsT=C.E127[:], rhs=cP[:, ti - 1, :], start=False, stop=True),
                             reads=[("cP", ti - 1)], writes=["psQ"])
                    S.op("dve", lambda e, ti=ti, psC=psC: e.tensor_copy(out=cP[:, ti, :], in_=psC),
                         reads=["psQ"], writes=[("cP", ti)])
                psR = psQ[:, 16:24]
                S.op("pe", lambda e, G=G, psR=psR: e.matmul(psR, lhsT=C.E127[:], rhs=cP[:, 4 * G + 1, :], start=True, stop=True),
                     reads=[("cP", 4 * G + 1)], writes=["psQ"])
                S.op("dve", lambda e, psR=psR: e.tensor_copy(out=cref[:], in_=psR), reads=["psQ"], writes=["cref"])
                for i in range(4):
                    S.op("dve", lambda e, i=i, G=G: e.tensor_tensor(out=Aaug[:, i, :], in0=cref[:], in1=cP[:, 4 * G + i, :],
                                                                   op=ALU.subtract),
                         reads=["cref", ("cP", 4 * G + i)], writes=[("Aaug", i)])
                nj = 4 * G + 4
                S.op("dve", lambda e, nj=nj: e.tensor_tensor(out=biasG[:, 0:nj, :], in0=cP[:, 0:nj, :],
                                                             in1=cref[:].unsqueeze(1).to_broadcast([128, nj, 8]),
                                                             op=ALU.subtract),
                     reads=["cref"] + [("cP", t) for t in range(nj)], writes=["biasG"])
                for h in range(8):
                    QT, qk = QTs[h]
                    pa, pak = (psQ, "psQ") if h % 2 == 0 else (psN, "psN")
                    for i in range(4):
                        S.op("pe", lambda e, i=i, h=h, pa=pa: e.matmul(pa[64:65, i * 128:(i + 1) * 128], lhsT=Aaug[:, i, h:h + 1],
                                                                      rhs=C.ident[:], start=(i == 0), stop=(i == 3)),
                             reads=[("Aaug", i)], writes=[pak], c=110.0)
                    S.op("act", lambda e, QT=QT, pa=pa: e.copy(out=QT[64:65, :], in_=pa[64:65, :]), reads=[pak], writes=[(qk, "r")], c=500.0)
                for p in range(4):
                    he, ho = 2 * p, 2 * p + 1
                    for k in range(8):
                        S.op("pe", lambda e, k=k, p=p: e.matmul(psQ[:], lhsT=w_fq[:, k, p * 128:(p + 1) * 128], rhs=hT[:, k, :],
                                                               start=(k == 0), stop=(k == 7)),
                             reads=[("hT", 0), ("hT", 1), ("hT", 2), ("hT", 3), "w_fq"], writes=["psQ"])
                    for k in range(8):
                        S.op("pe", lambda e, k=k, p=p: e.matmul(psV[:], lhsT=w_fk[:, k, p * 128:(p + 1) * 128], rhs=hT[:, k, :],
                                                               start=(k == 0), stop=(k == 7)),
                             reads=[("hT", 0), ("hT", 1), ("hT", 2), ("hT", 3), "w_fk"], writes=["psV"])
                    for (src, skey, gain, dsts) in (
                            (psQ, "psQ", C.qg, [(QTs[he][0][0:64, :], (QTs[he][1], "q")), (QTs[ho][0][0:64, :], (QTs[ho][1], "q"))]),
                            (psV, "psV", C.kg, [(KT[0:64, he, G * 512:(G + 1) * 512], ("KT", he, G)),
                                                (KT[0:64, ho, G * 512:(G + 1) * 512], ("KT", ho, G))])):
                        sq, sqk = sqr.next()
                        rs, rsk = rsr.next()
                        S.op("act", lambda e, src=src, sq=sq: e.activation(out=sq[:], in_=src[:], func=AF.Square),
                             reads=[skey], writes=[sqk], c=650.0)
                        S.op("pe", lambda e, sq=sq: e.matmul(psN[:], lhsT=C.bd_bf[:], rhs=sq[:], start=True, stop=True),
                             reads=[sqk], writes=["psN"])
                        S.op("act", lambda e, rs=rs: e.activation(out=rs[:], in_=psN[:], func=AF.Ln, scale=1.0 / 64.0,
                                                           bias=C.eps_t[:, 0:1]), reads=["psN"], writes=[rsk], c=650.0)
                        S.op("act", lambda e, rs=rs: e.activation(out=rs[:], in_=rs[:], func=AF.Exp, scale=-0.5),
                             reads=[rsk], writes=[rsk], c=650.0)
                        for hh, (dst, dkey) in enumerate(dsts):
                            ps_ = slice(hh * 64, (hh + 1) * 64)
                            S.op("dve", lambda e, src=src, gain=gain, dst=dst, rs=rs, ps_=ps_: e.scalar_tensor_tensor(
                                out=dst, in0=src[ps_, :], scalar=gain[ps_, 0:1], in1=rs[ps_, :], op0=ALU.mult, op1=ALU.mult),
                                reads=[skey, rsk], writes=[dkey], c=620.0)
                for h in range(8):
                    QT, qk0 = QTs[h]
                    qk = (qk0, "q")
                    qkr = (qk0, "r")
                    psO, ok = psOr.next()
                    pend = None

                    def pv(j, PT, ptk, psO=psO, ok=ok, h=h, G=G, nj=nj):
                        c0 = max(0, j - 4 * G) * 128
                        S.op("pe", lambda e: e.matmul(psO[0:64, c0:512], lhsT=Vc[:, j, h, :], rhs=PT[:, c0:512],
                                                      start=(j == 0), stop=(j == nj - 1)),
                             reads=[ptk, ("Vc", j)], writes=[ok])
                        S.op("pe", lambda e: e.matmul(psO[64:128, c0:512], lhsT=C.ones_bf[:, 0:64], rhs=PT[:, c0:512],
                                                      start=(j == 0), stop=(j == nj - 1)),
                             reads=[ptk], writes=[ok])

                    for j in range(nj):
                        r = max(0, j - 4 * G)
                        c0 = r * 128
                        psS, sk_ = psSr.next()
                        PT, ptk = PTr.next()
                        S.op("pe", lambda e, j=j, c0=c0, psS=psS, h=h, QT=QT: e.matmul(
                            psS[:, c0:512], lhsT=KT[0:65, h, j * 128:(j + 1) * 128], rhs=QT[0:65, c0:512], start=True, stop=True),
                            reads=[("KT", h, j // 4), "KTones", qk, qkr], writes=[sk_])
                        S.op("act", lambda e, j=j, c0=c0, psS=psS, PT=PT, h=h: e.activation(
                            out=PT[:, c0:512], in_=psS[:, c0:512], func=AF.Exp, bias=biasG[:, j, h:h + 1], scale=1.0),
                            reads=[sk_, "biasG"], writes=[ptk])
                        if j >= 4 * G:
                            S.op("pool", lambda e, c0=c0, PT=PT: e.tensor_tensor(out=PT[:, c0:c0 + 128], in0=PT[:, c0:c0 + 128],
                                                                                in1=C.mask_bf[:], op=ALU.mult),
                                 reads=[ptk], writes=[ptk])
                        if pend is not None:
                            pv(*pend)
                        pend = (j, PT, ptk)
                    pv(*pend)
                    S.op("dve", lambda e, psO=psO: e.reciprocal(out=rinv_sb[:], in_=psO[64:128, :]),
                         reads=[ok], writes=["rinv_sb"])
                    S.op("dve", lambda e, psO=psO, h=h, G=G: e.tensor_tensor(
                        out=ybT[(h % 2) * 64:(h % 2) * 64 + 64, h // 2, G * 512:(G + 1) * 512],
                        in0=psO[0:64, :], in1=rinv_sb[:], op=ALU.mult),
                        reads=[ok, "rinv_sb"], writes=[("ybT", G, h)])
            if debug:
                S.op("sp", lambda e: e.dma_start(out=dbg_yb, in_=ybT[:]), reads=[("ybT", G_, h_) for G_ in range(NG) for h_ in range(8)], dma="dbgyb")
            S.emit()

        with contextlib.ExitStack() as st:
            A = Alloc(nc, st, "a2")
            S = Sched(nc)
            S.excl.update(["pT"] + [f"b{i}" for i in range(7)])
            w_hq = A.sb("w_hq", [128, 8, 512], BF16)
            w_hf = A.sb("w_hf", [128, 8, 512], BF16)
            w_hi = A.sb("w_hi", [128, 8, 512], BF16)
            w_hg = A.sb("w_hg", [128, 8, 512], BF16)
            w_ga = A.sb("w_ga", [128, 8, 1024], BF16)
            w_gb = A.sb("w_gb", [128, 8, 1024], BF16)
            wA = A.sb("wA", [128, 4, 1024], BF16)
            wB = A.sb("wB", [128, 4, 1024], BF16)
            wO = A.sb("wO", [128, 8, 1024], BF16)
            hT = A.sb("hT", [128, 8, 512], BF16)
            v_tok = A.sb("v_tok", [128, 4, 512], BF16)
            tmp = [A.sb(f"t{i}", [128, 512], F32) for i in range(5)]
            qT = A.sb("qT", [128, 512], BF16)
            ktT = A.sb("ktT", [128, 512], BF16)
            khT = A.sb("khT", [128, 512], BF16)
            khat = A.sb("khat", [128, 4, 128], BF16)
            sg = A.sb("sg", [128, 4, 512], BF16)
            yaT = A.sb("yaT", [128, 4, 512], BF16)
            mT = A.sb("mT", [128, 8, 512], BF16)
            Sf = A.sb("Sf", [128, 4, 128], F32)
            Sb = A.sb("Sb", [128, 4, 128], BF16)
            PsT = A.sb("PsT", [128, 128], BF16)
            eL = A.sb("eL", [128, 4], F32)
            sq = A.sb("sq2", [128, 512], BF16)
            xring = Ring([(A.sb(f"x{i}", [128, 1024], F32), f"x{i}") for i in range(5)])
            W = NS()
            W.ssq = Ring([(A.sb(f"ssq{i}", [128, 4], F32), f"ssq{i}") for i in range(2)])
            W.hb = Ring([(A.sb("hb0", [128, 1024], BF16), "hb0")])
            pTt = A.ps("pT", [128, 8, 128], BF16)
            W.pT = Ring([(pTt, "pT")])
            bk = [A.ps(f"b{i}", [128, 512], F32) for i in range(7)]
            print("A2 sbuf bytes/partition", A.bytes + A0.bytes + A_yb.bytes)

            S.op("pool", lambda e: e.memset(Sf[:], 0.0), writes=[("Sf", h) for h in range(4)])
            S.op("pool", lambda e: e.memset(Sb[:], 0.0), writes=[("Sb", h) for h in range(4)])
            load_w(S, w_hi, "w_hi", w_in, 0, 8, HI, 512)
            load_w(S, w_hq, "w_hq", w_in, 0, 8, HQ, 512)
            load_w(S, w_hf, "w_hf", w_in, 0, 8, HF, 512)
            load_w(S, w_hg, "w_hg", w_in, 0, 8, HGC, 512)
            load_w(S, w_ga, "w_ga", w_in, 0, 8, GA, 1024)
            load_w(S, w_gb, "w_gb", w_in, 0, 8, GB, 1024)
            load_w(S, wA, "wA", w_ba, 0, 4, 0, 1024)
            load_w(S, wB, "wB", w_bb, 0, 4, 0, 1024)
            load_w(S, wO, "wO", w_out, 0, 8, 0, 1024)

            for G in range(NG):
                xts = []
                for i in range(4):
                    ti = 4 * G + i
                    xt, xk = xring.next()
                    xts.append((xt, xk))
                    S.op("sp", lambda e, xt=xt, ti=ti: e.dma_start(out=xt[:], in_=x[ti * 128:(ti + 1) * 128, :]),
                         writes=[xk], dma=xk)
                    norm_T(S, C, xt[:], xk, C.gmixB, hT[:, :, i * 128:(i + 1) * 128], ("hT", i), W)
                    for k in range(8):
                        S.op("pe", lambda e, k=k, i=i: e.matmul(bk[0][:], lhsT=hT[:, k, i * 128:(i + 1) * 128], rhs=w_hi[:, k, :],
                                                               start=(k == 0), stop=(k == 7)),
                             reads=[("hT", i), "w_hi"], writes=["b0"])
                    S.op("act", lambda e, i=i: e.copy(out=v_tok[:, i, :], in_=bk[0][:]), reads=["b0"], writes=[("v_tok", i)])
                hTall = [("hT", 0), ("hT", 1), ("hT", 2), ("hT", 3)]
                for h in range(4):
                    pZq, pZf, pZg, pSm, pOh, pN = bk[1], bk[2], bk[3], bk[4], bk[5], bk[6]
                    for (ps_, w_, wk_, key_) in ((pZf, w_hf, "w_hf", "b2"), (pZq, w_hq, "w_hq", "b1"), (pZg, w_hg, "w_hg", "b3")):
                        for k in range(8):
                            S.op("pe", lambda e, k=k, h=h, ps_=ps_, w_=w_: e.matmul(
                                ps_[:], lhsT=w_[:, k, h * 128:(h + 1) * 128], rhs=hT[:, k, :], start=(k == 0), stop=(k == 7)),
                                reads=hTall + [wk_], writes=[key_])
                    t0, t1, t2, t3, t4 = tmp
                    sigmoid_act(S, pZf[:], "b2", t0[:], "t0", C)
                    S.op("act", lambda e, h=h: e.activation(out=t1[:], in_=t0[:], func=AF.Ln, scale=C.omlT[:, h:h + 1],
                                                            bias=C.lbT[:, h:h + 1]), reads=["t0"], writes=["t1"])
                    S.op("dve", lambda e, h=h: e.tensor_scalar(out=t2[:], in0=t0[:], scalar1=C.nomlT[:, h:h + 1],
                                                               scalar2=C.omlT[:, h:h + 1], op0=ALU.mult, op1=ALU.add),
                         reads=["t0"], writes=["t2"])
                    for i in range(4):
                        S.op("dve", lambda e, i=i: e.tensor_tensor_scan(out=t3[:, i * 128:(i + 1) * 128], data0=C.ones_f[:],
                                                                       data1=t1[:, i * 128:(i + 1) * 128], initial=0.0,
                                                                       op0=ALU.mult, op1=ALU.add),
                             reads=["t1"], writes=["t3"])
                    S.op("act", lambda e: e.activation(out=eL[:], in_=t3[:].rearrange("p (i c) -> p i c", c=128)[:, :, 127],
                                                       func=AF.Exp), reads=["t3"], writes=["eL"])
                    S.op("act", lambda e: e.activation(out=t1[:], in_=t3[:], func=AF.Exp), reads=["t3"], writes=["t1"])
                    S.op("act", lambda e: e.activation(out=t0[:], in_=t3[:], func=AF.Exp, scale=-1.0),
                         reads=["t3"], writes=["t0"])
                    sigmoid_act(S, pZq[:], "b1", t4[:], "t4", C)
                    S.op("dve", lambda e: e.tensor_tensor(out=t4[:], in0=pZq[:], in1=t4[:], op=ALU.mult),
                         reads=["b1", "t4"], writes=["t4"])
                    S.op("dve", lambda e: e.tensor_tensor(out=qT[:], in0=t4[:], in1=t1[:], op=ALU.mult),
                         reads=["t4", "t1"], writes=["qT"])
                    S.op("dve", lambda e: e.tensor_tensor(out=ktT[:], in0=t2[:], in1=t0[:], op=ALU.mult),
                         reads=["t2", "t0"], writes=["ktT"])
                    for i in range(4):
                        S.op("act", lambda e, i=i: e.activation(out=khT[:, i * 128:(i + 1) * 128], in_=ktT[:, i * 128:(i + 1) * 128],
                                                               func=AF.Copy, scale=eL[:, i:i + 1]),
                             reads=["ktT", "eL"], writes=["khT"], c=340.0)
                    sigmoid_act(S, pZg[:], "b3", t4[:], "t4", C)
                    S.op("dve", lambda e, h=h: e.tensor_tensor(out=sg[:, h, :], in0=pZg[:], in1=t4[:], op=ALU.mult),
                         reads=["b3", "t4"], writes=[("sg", h)])
                    for i in range(4):
                        S.op("pe", lambda e, i=i: e.transpose(out=pTt[:, i, :], in_=khT[:, i * 128:(i + 1) * 128], identity=C.ident[:]),
                             reads=["khT"], writes=["pT"])
                    S.op("act", lambda e: e.copy(out=khat[:], in_=pTt[:, 0:4, :]), reads=["pT"], writes=["khat"])
                    for i in range(4):
                        cs = slice(i * 128, (i + 1) * 128)
                        pSc = pSm[:, 0:128]
                        pSt = pN[:, 0:128]
                        S.op("pe", lambda e, cs=cs, pSc=pSc: e.matmul(pSc, lhsT=ktT[:, cs], rhs=qT[:, cs], start=True, stop=True),
                             reads=["ktT", "qT"], writes=["b4"])
                        S.op("dve", lambda e, pSc=pSc: e.tensor_tensor(out=PsT[:], in0=pSc, in1=C.mask_bf[:], op=ALU.mult),
                             reads=["b4"], writes=["PsT"])
                        S.op("pe", lambda e, cs=cs, i=i, h=h: e.matmul(pOh[:, cs], lhsT=v_tok[:, i, h * 128:(h + 1) * 128], rhs=PsT[:],
                                                                      start=True, stop=False),
                             reads=[("v_tok", i), "PsT"], writes=["b5"])
                        S.op("pe", lambda e, cs=cs, h=h: e.matmul(pOh[:, cs], lhsT=Sb[:, h, :], rhs=qT[:, cs], start=False, stop=True),
                             reads=[("Sb", h), "qT"], writes=["b5"])
                        S.op("pe", lambda e, i=i, h=h, pSt=pSt: e.matmul(pSt, lhsT=khat[:, i, :], rhs=v_tok[:, i, h * 128:(h + 1) * 128],
                                                                        start=True, stop=True),
                             reads=["khat", ("v_tok", i)], writes=["b6"])
                        S.op("dve", lambda e, i=i, h=h, pSt=pSt: e.scalar_tensor_tensor(
                            out=Sf[:, h, :], in0=Sf[:, h, :], scalar=eL[:, i:i + 1], in1=pSt, op0=ALU.mult, op1=ALU.add),
                            reads=[("Sf", h), "eL", "b6"], writes=[("Sf", h)])
                        S.op("act", lambda e, h=h: e.copy(out=Sb[:, h, :], in_=Sf[:, h, :]), reads=[("Sf", h)], writes=[("Sb", h)])
                    S.op("act", lambda e: e.activation(out=sq[:], in_=pOh[:], func=AF.Square), reads=["b5"], writes=["sq"])
                    S.op("pe", lambda e: e.matmul(pN[:], lhsT=C.ones_bf[:], rhs=sq[:], start=True, stop=True),
                         reads=["sq"], writes=["b6"])
                    S.op("act", lambda e: e.activation(out=t2[:], in_=pN[:], func=AF.Ln, scale=1.0 / 128.0, bias=C.eps_t[:, 0:1]),
                         reads=["b6"], writes=["t2"])
                    S.op("act", lambda e: e.activation(out=t2[:], in_=t2[:], func=AF.Exp, scale=-0.5), reads=["t2"], writes=["t2"])
                    S.op("dve", lambda e: e.scalar_tensor_tensor(out=t2[:], in0=pOh[:], scalar=C.gon[:, 0:1], in1=t2[:],
                                                                 op0=ALU.mult, op1=ALU.mult), reads=["b5", "t2"], writes=["t2"])
                    S.op("dve", lambda e, h=h: e.tensor_tensor(out=yaT[:, h, :], in0=t2[:], in1=sg[:, h, :], op=ALU.mult),
                         reads=["t2", ("sg", h)], writes=[("yaT", h)])
                if debug:
                    S.op("sp", lambda e, G=G: e.dma_start(out=dbg_ya[G], in_=yaT[:]), reads=[("yaT", h) for h in range(4)], dma="dbgya")
                t0, t1 = tmp[0], tmp[1]
                for dc in range(8):
                    ds_ = slice(dc * 128, (dc + 1) * 128)
                    pGa, pGb, pA, pB = bk[1], bk[2], bk[3], bk[4]
                    for k in range(8):
                        S.op("pe", lambda e, k=k, ds_=ds_: e.matmul(pGa[:], lhsT=w_ga[:, k, ds_], rhs=hT[:, k, :], start=(k == 0), stop=(k == 7)),
                             reads=hTall + ["w_ga"], writes=["b1"])
                    for k in range(8):
                        S.op("pe", lambda e, k=k, ds_=ds_: e.matmul(pGb[:], lhsT=w_gb[:, k, ds_], rhs=hT[:, k, :], start=(k == 0), stop=(k == 7)),
                             reads=hTall + ["w_gb"], writes=["b2"])
                    for k in range(4):
                        S.op("pe", lambda e, k=k, ds_=ds_: e.matmul(pA[:], lhsT=wA[:, k, ds_], rhs=yaT[:, k, :], start=(k == 0), stop=(k == 3)),
                             reads=[("yaT", k), "wA"], writes=["b3"])
                    for k in range(4):
                        S.op("pe", lambda e, k=k, ds_=ds_, G=G: e.matmul(pB[:], lhsT=wB[:, k, ds_], rhs=ybT[:, k, G * 512:(G + 1) * 512],
                                                                        start=(k == 0), stop=(k == 3)),
                             reads=["wB"], writes=["b4"])
                    sigmoid_act(S, pGa[:], "b1", t0[:], "t0", C)
                    sigmoid_act(S, pGb[:], "b2", t1[:], "t1", C)
                    S.op("dve", lambda e: e.tensor_tensor(out=t0[:], in0=pA[:], in1=t0[:], op=ALU.mult), reads=["b3", "t0"], writes=["t0"])
                    S.op("dve", lambda e: e.tensor_tensor(out=t1[:], in0=pB[:], in1=t1[:], op=ALU.mult), reads=["b4", "t1"], writes=["t1"])
                    S.op("dve", lambda e, dc=dc: e.tensor_tensor(out=mT[:, dc, :], in0=t0[:], in1=t1[:], op=ALU.add),
                         reads=["t0", "t1"], writes=[("mT", dc)])
                for i in range(4):
                    ti = 4 * G + i
                    xt, xk = xts[i]
                    for half in range(2):
                        pX = bk[5 + half]
                        for k in range(8):
                            S.op("pe", lambda e, k=k, i=i, half=half, pX=pX: e.matmul(
                                pX[:], lhsT=mT[:, k, i * 128:(i + 1) * 128], rhs=wO[:, k, half * 512:(half + 1) * 512],
                                start=(k == 0), stop=(k == 7)),
                                reads=[("mT", k), "wO"], writes=[f"b{5 + half}"])
                        S.op("dve", lambda e, xt=xt, half=half, pX=pX: e.tensor_tensor(
                            out=xt[:, half * 512:(half + 1) * 512], in0=pX[:], in1=xt[:, half * 512:(half + 1) * 512], op=ALU.add),
                            reads=[f"b{5 + half}", xk], writes=[xk])
                    S.op("sp", lambda e, xt=xt, ti=ti: e.dma_start(out=x1d[ti * 128:(ti + 1) * 128, :], in_=xt[:]),
                         reads=[xk], dma="st_" + xk)
            S.emit()

        st_yb.close()
        TB = 4
        NB = TB * 128
        with contextlib.ExitStack() as st:
            A = Alloc(nc, st, "b")
            S = Sched(nc)
            S.excl.update(["pT"] + [f"b{i}" for i in range(7)])
            wG = A.sb("wG", [128, 8, DFF], BF16)
            wU = A.sb("wU", [128, 8, DFF], BF16)
            wD = A.sb("wD", [128, NFF, 1024], BF16)
            wPG = A.sb("wPG", [128, 8, 1024], BF16)
            wPP = A.sb("wPP", [128, 2, 1024], BF16)
            gffnB = A.sb("gffnB", [128, 8, 128], BF16)
            gpleB = A.sb("gpleB", [128, 8, 128], BF16)
            gT2 = A.sb("gT2", [128, 8], F32)
            gT3 = A.sb("gT3", [128, 8], F32)
            h2T = A.sb("h2T", [128, 8, NB], BF16)
            actT = A.sb("actT", [128, NFF, NB], BF16)
            tr = Ring([(A.sb(f"t{i}", [128, 512], F32), f"t{i}") for i in range(2)])
            pbr = Ring([(A.sb(f"pb{i}", [128, 256], BF16), f"pb{i}") for i in range(2)])
            ppT = A.sb("ppT", [128, 2, 128], BF16)
            xaring = Ring([(A.sb("xa0", [128, 1024], F32), "xa0")])
            xring = Ring([(A.sb(f"x{i}", [128, 1024], F32), f"x{i}") for i in range(2)])
            h3T = h2T[:, :, 384:512]
            W = NS()
            W.ssq = Ring([(A.sb(f"ssq{i}", [128, 4], F32), f"ssq{i}") for i in range(2)])
            W.hb = Ring([(A.sb("hb0", [128, 1024], BF16), "hb0")])
            pTt = A.ps("pT", [128, 8, 128], BF16)
            W.pT = Ring([(pTt, "pT")])
            bk = [A.ps(f"b{i}", [128, 512], F32) for i in range(7)]
            print("B sbuf bytes/partition", A.bytes + A0.bytes)

            S.op("sp", lambda e: e.dma_start(out=gT2[:], in_=norm_ffn_g.rearrange("o (k p) -> p (o k)", p=128),
                                             allow_slow_non_contiguous=True), writes=["gT2"], dma="gT2")
            S.op("sp", lambda e: e.dma_start(out=gT3[:], in_=norm_ple_g.rearrange("o (k p) -> p (o k)", p=128),
                                             allow_slow_non_contiguous=True), writes=["gT3"], dma="gT3")
            S.op("dve", lambda e: e.tensor_copy(out=gffnB[:], in_=gT2[:].unsqueeze(2).to_broadcast([128, 8, 128])),
                 reads=["gT2"], writes=["gffnB"])
            S.op("dve", lambda e: e.tensor_copy(out=gpleB[:], in_=gT3[:].unsqueeze(2).to_broadcast([128, 8, 128])),
                 reads=["gT3"], writes=["gpleB"])
            for blk in range(4):
                for (dst_, key_, src_) in ((wG, "wG", w_fg), (wU, "wU", w_fu)):
                    for k in range(8):
                        S.op("pool", lambda e, k=k, blk=blk, dst_=dst_, src_=src_: e.dma_start(
                            out=dst_[:, k, blk * 704:(blk + 1) * 704], in_=src_[k * 128:(k + 1) * 128, blk * 704:(blk + 1) * 704]),
                            writes=[(key_, blk)], dma=f"{key_}{blk}", nodep=True)
            load_w(S, wD, "wD", w_fd, 0, NFF, 0, 1024)
            load_w(S, wPG, "wPG", w_pg, 0, 8, 0, 1024)
            load_w(S, wPP, "wPP", w_pp, 0, 2, 0, 1024)

            for g in range(NT // TB):
                for i in range(TB):
                    ti = g * TB + i
                    xt, xk = xaring.next()
                    S.op("sp", lambda e, xt=xt, ti=ti: e.dma_start(out=xt[:], in_=x1d[ti * 128:(ti + 1) * 128, :]),
                         writes=[xk], dma=xk)
                    norm_T(S, C, xt[:], xk, gffnB, h2T[:, :, i * 128:(i + 1) * 128], ("h2T", i), W)
                h2all = [("h2T", i) for i in range(TB)]
                for f in range(NFF):
                    fs = slice(f * 128, (f + 1) * 128)
                    pG = bk[0 + (f % 2)]
                    pU = bk[2 + (f % 2)]
                    kG = f"b{0 + (f % 2)}"
                    kU = f"b{2 + (f % 2)}"
                    for k in range(8):
                        S.op("pe", lambda e, k=k, fs=fs, pG=pG: e.matmul(pG[:, 0:NB], lhsT=wG[:, k, fs], rhs=h2T[:, k, :],
                                                                        start=(k == 0), stop=(k == 7)),
                             reads=h2all + [("wG", b_) for b_ in sorted({(f * 128) // 704, (f * 128 + 127) // 704})], writes=[kG])
                    for k in range(8):
                        S.op("pe", lambda e, k=k, fs=fs, pU=pU: e.matmul(pU[:, 0:NB], lhsT=wU[:, k, fs], rhs=h2T[:, k, :],
                                                                        start=(k == 0), stop=(k == 7)),
                             reads=h2all + [("wU", b_) for b_ in sorted({(f * 128) // 704, (f * 128 + 127) // 704})], writes=[kU])
                    t, tk = tr.next()
                    sigmoid_act(S, pG[:, 0:NB], kG, t[:, 0:NB], tk, C)
                    S.op("dve", lambda e, t=t, pG=pG: e.tensor_tensor(out=t[:, 0:NB], in0=pG[:, 0:NB], in1=t[:, 0:NB], op=ALU.mult),
                         reads=[kG, tk], writes=[tk])
                    S.op("dve", lambda e, t=t, pU=pU, f=f: e.tensor_tensor(out=actT[:, f, :], in0=pU[:, 0:NB], in1=t[:, 0:NB], op=ALU.mult),
                         reads=[kU, tk], writes=[("actT", f)])
                for i in range(TB):
                    ti = g * TB + i
                    xt, xk = xring.next()
                    S.op("sp", lambda e, xt=xt, ti=ti: e.dma_start(out=xt[:], in_=x1d[ti * 128:(ti + 1) * 128, :]),
                         writes=[xk], dma=xk)
                    for half in range(2):
                        pD = bk[4 + half]
                        kD = f"b{4 + half}"
                        for f in range(NFF):
                            S.op("pe", lambda e, f=f, i=i, half=half, pD=pD: e.matmul(
                                pD[:], lhsT=actT[:, f, i * 128:(i + 1) * 128], rhs=wD[:, f, half * 512:(half + 1) * 512],
                                start=(f == 0), stop=(f == NFF - 1)),
                                reads=[("actT", f), "wD"], writes=[kD])
                        S.op("dve", lambda e, xt=xt, half=half, pD=pD: e.tensor_tensor(
                            out=xt[:, half * 512:(half + 1) * 512], in0=pD[:], in1=xt[:, half * 512:(half + 1) * 512], op=ALU.add),
                            reads=[kD, xk], writes=[xk])
                    norm_T(S, C, xt[:], xk, gpleB, h3T, ("h2T", 3), W)
                    pb, pbk = pbr.next()
                    S.op("pool", lambda e, pb=pb, ti=ti: e.dma_start(out=pb[:], in_=p_in[ti * 128:(ti + 1) * 128, :]),
                         writes=[pbk], dma=pbk)
                    for c in range(2):
                        S.op("pe", lambda e, c=c, pb=pb: e.transpose(out=pTt[:, c, :], in_=pb[:, c * 128:(c + 1) * 128], identity=C.ident[:]),
                             reads=[pbk], writes=["pT"])
                    S.op("act", lambda e: e.copy(out=ppT[:], in_=pTt[:, 0:2, :]), reads=["pT"], writes=["ppT"])
                    for half in range(2):
                        hs = slice(half * 512, (half + 1) * 512)
                        pPG = bk[4 + half]
                        kPG = f"b{4 + half}"
                        pPP = bk[6]
                        for k in range(8):
                            S.op("pe", lambda e, k=k, hs=hs, pPG=pPG: e.matmul(pPG[:], lhsT=h3T[:, k, :], rhs=wPG[:, k, hs],
                                                                              start=(k == 0), stop=(k == 7)),
                                 reads=[("h2T", 3), "wPG"], writes=[kPG])
                        for k in range(2):
                            S.op("pe", lambda e, k=k, hs=hs: e.matmul(pPP[:], lhsT=ppT[:, k, :], rhs=wPP[:, k, hs],
                                                                     start=(k == 0), stop=(k == 1)),
                                 reads=["ppT", "wPP"], writes=["b6"])
                        t, tk = tr.next()
                        sigmoid_act(S, pPG[:], kPG, t[:], tk, C)
                        S.op("dve", lambda e, t=t: e.tensor_tensor(out=t[:], in0=pPP[:], in1=t[:], op=ALU.mult),
                             reads=["b6", tk], writes=[tk])
                        S.op("dve", lambda e, t=t, xt=xt, hs=hs: e.tensor_tensor(out=xt[:, hs], in0=xt[:, hs], in1=t[:], op=ALU.add),
                             reads=[tk, xk], writes=[xk])
                    S.op("sp", lambda e, xt=xt, ti=ti: e.dma_start(out=out[ti * 128:(ti + 1) * 128, :], in_=xt[:]),
                         reads=[xk], dma="ost_" + xk)
            S.emit()
    return nc


_IN_NAMES = ["x", "p", "norm_mix_g", "w_in", "hg_lb_logits", "hg_onorm_g", "fox_f_bias", "fox_q_norm_g",
             "fox_k_norm_g", "w_branch_a", "w_branch_b", "w_out", "norm_ffn_g", "w_ffn_gate", "w_ffn_up",
             "w_ffn_down", "norm_ple_g", "w_ple_gate", "w_ple_proj"]


def make_in_maps(inputs, n_cores, SEQ):
    f = lambda a: np.ascontiguousarray(np.asarray(a, dtype=np.float32))
    shared = {}
    for k in _IN_NAMES:
        if k in ("x", "p"):
            continue
        a = f(inputs[k])
        if k == "hg_lb_logits":
            shared[k] = a.reshape(2, 512)
        else:
            shared[k] = a.reshape(a.shape[-2] if a.ndim >= 2 else 1, a.shape[-1])
    xs = f(inputs["x"])
    ps = f(inputs["p"])[0]
    maps = []
    for c in range(n_cores):
        m = dict(shared)
        m["x"] = np.ascontiguousarray(xs[c])
        m["p"] = np.ascontiguousarray(ps[c])
        maps.append(m)
    return maps


def kernel(**inputs):
    x = np.asarray(inputs["x"])
    B, SEQ, _ = x.shape
    nc = build_program(SEQ)
    in_maps = make_in_maps(inputs, B, SEQ)
    res = run_bass_kernel_spmd(nc, in_maps, core_ids=list(range(B)))
    outs = [np.asarray(r["out"], dtype=np.float32) for r in res.results]
    return np.stack(outs, axis=0)
```

```python
import contextlib
import numpy as np
import concourse.bass as bass
import concourse.mybir as mybir
from concourse.bass_utils import run_bass_kernel_spmd

F32 = mybir.dt.float32
BF16 = mybir.dt.bfloat16
AF = mybir.ActivationFunctionType
ALU = mybir.AluOpType

ENGS = ("pe", "act", "dve", "pool", "sp")
EPS = 1e-6
HQ, HF, HI, HGC, FQ, FK, FV, FFC, GA, GB = 0, 512, 1024, 1536, 2048, 2560, 3072, 3584, 3592, 4616
INC = 5640
DFF = 2816
NFF = DFF // 128


class _Op:
    __slots__ = ("eng", "fn", "dma", "preds", "deps", "signal", "sig_idx", "gid", "pos", "dma_val", "cost",
                 "fin", "done", "tail")

    def __init__(self, eng, fn, dma, cost):
        self.eng = eng
        self.fn = fn
        self.dma = dma
        self.cost = cost
        self.preds = []
        self.deps = []
        self.signal = False
        self.sig_idx = None
        self.gid = None
        self.pos = None
        self.dma_val = None
        self.fin = None
        self.done = False


_DEF_COST = {"pe": 260.0, "act": 700.0, "dve": 720.0, "pool": 600.0, "sp": 100.0}


class Sched:
    SEM_ROT = 16000
    WINDOW = 256
    REORDER = True
    PRIO = "cp"

    def __init__(self, nc):
        self.nc = nc
        self.ops = []
        self.lastw = {}
        self.readers = {}
        self.excl = set()
        self.wgroup = {}

    def op(self, eng, fn, reads=(), writes=(), dma=None, nodep=False, c=None):
        if self.excl:
            writes = list(writes) + [k for k in reads if k in self.excl]
            reads = [k for k in reads if k not in self.excl]
        if c is None:
            c = 2600.0 if dma is not None else _DEF_COST[eng]
        o = _Op(eng, fn, dma, c)
        o.gid = len(self.ops)
        preds = {}
        if not nodep:
            for k in reads:
                w = self.lastw.get(k)
                if w is not None:
                    preds[id(w)] = (w, True)
                for g_ in self.wgroup.get(k, ()):
                    preds[id(g_)] = (g_, True)
            for k in writes:
                w = self.lastw.get(k)
                if w is not None and id(w) not in preds:
                    preds[id(w)] = (w, False)
                for g_ in self.wgroup.get(k, ()):
                    if id(g_) not in preds:
                        preds[id(g_)] = (g_, False)
                for r in self.readers.get(k, ()):
                    if id(r) not in preds and r is not o:
                        preds[id(r)] = (r, False)
        o.preds = list(preds.values())
        for k in writes:
            if nodep and dma is not None:
                self.wgroup.setdefault(k, []).append(o)
            else:
                self.wgroup.pop(k, None)
            self.lastw[k] = o
            self.readers[k] = []
        for k in reads:
            self.readers.setdefault(k, []).append(o)
        self.ops.append(o)
        return o

    def _schedule(self):
        pend = {e: [o for o in self.ops if o.eng == e] for e in ENGS}
        if not self.REORDER:
            return pend
        if self.PRIO == "cp":
            succ_max = [0.0] * len(self.ops)
            for o in reversed(self.ops):
                t = succ_max[o.gid] + o.cost
                o.tail = t
                for p, _ in o.preds:
                    if t > succ_max[p.gid]:
                        succ_max[p.gid] = t
        order = {e: [] for e in ENGS}
        free = {e: 0.0 for e in ENGS}
        head = {e: 0 for e in ENGS}
        nleft = len(self.ops)
        while nleft:
            best = None
            for e in ENGS:
                L = pend[e]
                h = head[e]
                while h < len(L) and L[h].done:
                    h += 1
                head[e] = h
                if h >= len(L):
                    continue
                cand = None
                cnt = 0
                i = h
                while i < len(L) and cnt < self.WINDOW:
                    o = L[i]
                    i += 1
                    if o.done:
                        continue
                    cnt += 1
                    rdy = 0.0
                    ok = True
                    for p, _ in o.preds:
                        if not p.done:
                            ok = False
                            break
                        t = p.fin + (0.0 if p.eng == e and p.dma is None else 120.0)
                        if t > rdy:
                            rdy = t
                    if not ok:
                        continue
                    st = rdy if rdy > free[e] else free[e]
                    if self.PRIO == "cp":
                        if cand is None or st < cand[0] - 1e-9 or (abs(st - cand[0]) <= 1e-9 and o.tail > cand[1].tail):
                            cand = (st, o)
                    else:
                        if cand is None or st < cand[0] - 1e-9:
                            cand = (st, o)
                        if rdy <= free[e]:
                            break
                if cand is not None and (best is None or cand[0] < best[0] - 1e-9 or
                                         (abs(cand[0] - best[0]) <= 1e-9 and cand[1].gid < best[1].gid)):
                    best = cand
            st, o = best
            e = o.eng
            occ = (1000.0 if e == "pool" else 60.0) if o.dma is not None else o.cost
            free[e] = st + occ
            o.fin = st + o.cost
            o.done = True
            o.pos = len(order[e])
            order[e].append(o)
            nleft -= 1
        self.est_ns = max(o.fin for o in self.ops) if self.ops else 0.0
        busy = {e: 0.0 for e in ENGS}
        for o in self.ops:
            busy[o.eng] += ((1000.0 if o.eng == "pool" else 60.0) if o.dma is not None else o.cost)
        print("  busy us", {e: round(v / 1e3) for e, v in busy.items()})
        return order

    def emit(self):
        nc = self.nc
        order = self._schedule()
        for e in ENGS:
            for i, o in enumerate(order[e]):
                o.pos = i
        dma_cnt = {}
        for e in ENGS:
            for o in order[e]:
                best = {}
                for p, is_raw in o.preds:
                    if p.dma is not None:
                        o.deps.append(p)
                        continue
                    if p.eng == e:
                        assert p.pos < o.pos
                        if not is_raw or e == "pe":
                            continue
                    b = best.get(p.eng)
                    if b is None or p.pos > b.pos:
                        best[p.eng] = p
                for p in best.values():
                    o.deps.append(p)
                    p.signal = True
                if o.dma is not None:
                    c = dma_cnt.get(o.dma, 0) + 1
                    dma_cnt[o.dma] = c
                    o.dma_val = 16 * c
        with contextlib.ExitStack() as st:
            eng_sems = {}
            for e in ENGS:
                n = 0
                for o in order[e]:
                    if o.signal:
                        o.sig_idx = n
                        n += 1
                nsem = max(1, (n + self.SEM_ROT - 1) // self.SEM_ROT)
                print("  sched", e, "ops", len(order[e]), "signals", n)
                eng_sems[e] = [st.enter_context(nc.semaphore(f"s_{e}{i}")) for i in range(nsem)]
            print("  est ns", getattr(self, "est_ns", None))
            dma_sems = {k: st.enter_context(nc.semaphore(f"d_{k}")) for k in dma_cnt}
            blk_st = contextlib.ExitStack()
            block = blk_st.enter_context(nc.Block())

            def target(p):
                if p.dma is not None:
                    return (dma_sems[p.dma], p.dma_val, ("d", p.dma))
                j, v = divmod(p.sig_idx, self.SEM_ROT)
                return (eng_sems[p.eng][j], v + 1, (p.eng, j))

            def run(e, handle):
                waited = {}
                for o in order[e]:
                    need = {}
                    for p in o.deps:
                        sem, val, key = target(p)
                        if waited.get(key, 0) >= val:
                            continue
                        if key not in need or need[key][1] < val:
                            need[key] = (sem, val)
                    for key, (sem, val) in need.items():
                        handle.wait_ge(sem, val)
                        waited[key] = val
                    ins = o.fn(handle)
                    if o.dma is not None:
                        ins.then_inc(dma_sems[o.dma], 16)
                    elif o.signal:
                        j, _ = divmod(o.sig_idx, self.SEM_ROT)
                        ins.then_inc(eng_sems[e][j], 1)
                if e == "sp":
                    for k, c in dma_cnt.items():
                        if waited.get(("d", k), 0) < 16 * c:
                            handle.wait_ge(dma_sems[k], 16 * c)

            block.tensor(lambda h: run("pe", h))
            block.scalar(lambda h: run("act", h))
            block.vector(lambda h: run("dve", h))
            block.gpsimd(lambda h: run("pool", h))
            block.sync(lambda h: run("sp", h))
            blk_st.close()
            nc.all_engine_barrier()


class Ring:
    def __init__(self, items):
        self.items = items
        self.i = 0

    def next(self):
        r = self.items[self.i % len(self.items)]
        self.i += 1
        return r


class NS:
    pass


class Alloc:
    def __init__(self, nc, st, tag):
        self.nc, self.st, self.tag, self.bytes = nc, st, tag, 0

    def sb(self, name, shape, dt):
        n = 1
        for s in shape[1:]:
            n *= s
        self.bytes += n * (4 if dt == F32 else 2)
        return self.st.enter_context(self.nc.sbuf_tensor(f"{self.tag}_{name}", list(shape), dt))

    def ps(self, name, shape, dt):
        return self.st.enter_context(self.nc.psum_tensor(f"{self.tag}_{name}", list(shape), dt))


def load_w(S, dst, key, src, r0, nk, c0, ncols, cstep=512):
    for k in range(nk):
        for cc in range(0, ncols, cstep):
            w = min(cstep, ncols - cc)
            S.op("pool", lambda e, k=k, cc=cc, w=w: e.dma_start(
                out=dst[:, k, cc:cc + w], in_=src[r0 + k * 128:r0 + (k + 1) * 128, c0 + cc:c0 + cc + w]),
                writes=[key], dma=key, nodep=True)


def norm_T(S, C, X, xkey, gB, dst, dkey, W, scale_eng="act"):
    ssq, sk = W.ssq.next()
    hb, hk = W.hb.next()
    pT, pk = W.pT.next()
    S.op("act", lambda e: e.activation(out=hb[:], in_=X, func=AF.Square, accum_out=ssq[:, 0:1]),
         reads=[xkey], writes=[hk, sk], c=1100.0)
    S.op("act", lambda e: e.activation(out=ssq[:, 1:2], in_=ssq[:, 0:1], func=AF.Ln, scale=1.0 / 1024.0,
                                       bias=C.eps_t[:, 0:1]), reads=[sk], writes=[sk], c=260.0)
    S.op("act", lambda e: e.activation(out=ssq[:, 2:3], in_=ssq[:, 1:2], func=AF.Exp, scale=-0.5),
         reads=[sk], writes=[sk], c=260.0)
    if scale_eng == "act":
        S.op("act", lambda e: e.activation(out=hb[:], in_=X, func=AF.Copy, scale=ssq[:, 2:3]),
             reads=[xkey, sk], writes=[hk], c=1100.0)
    else:
        S.op("dve", lambda e: e.tensor_scalar(out=hb[:], in0=X, scalar1=ssq[:, 2:3], scalar2=None, op0=ALU.mult),
             reads=[xkey, sk], writes=[hk], c=1150.0)
    for k in range(8):
        S.op("pe", lambda e, k=k: e.transpose(out=pT[:, k, :], in_=hb[:, k * 128:(k + 1) * 128], identity=C.ident[:]),
             reads=[hk], writes=[pk], c=110.0)
    S.op("dve", lambda e: e.tensor_tensor(out=dst, in0=pT[:], in1=gB[:], op=ALU.mult), reads=[pk], writes=[dkey], c=1100.0)


def sigmoid_act(S, src, skey, dst, dkey, C, np_=128):
    S.op("act", lambda e: e.activation(out=dst, in_=src, func=AF.Exp, scale=-1.0), reads=[skey], writes=[dkey])
    S.op("act", lambda e: e.activation(out=dst, in_=dst, func=AF.Ln, bias=C.one_t[0:np_, 0:1], scale=1.0),
         reads=[dkey], writes=[dkey])
    S.op("act", lambda e: e.activation(out=dst, in_=dst, func=AF.Exp, scale=-1.0), reads=[dkey], writes=[dkey])


def build_program(SEQ=4096, debug=False):
    NT = SEQ // 128
    NG = SEQ // 512
    nc = bass.Bass("TRN2", target_bir_lowering=False)

    def din(name, shape):
        return nc.dram_tensor(name, list(shape), F32, kind="ExternalInput").ap()

    x = din("x", [SEQ, 1024])
    p_in = din("p", [SEQ, 256])
    norm_mix_g = din("norm_mix_g", [1, 1024])
    w_in = din("w_in", [1024, INC])
    hg_lb = din("hg_lb_logits", [2, 512])
    hg_on = din("hg_onorm_g", [1, 128])
    fox_fb = din("fox_f_bias", [1, 8])
    fox_qg = din("fox_q_norm_g", [1, 64])
    fox_kg = din("fox_k_norm_g", [1, 64])
    w_ba = din("w_branch_a", [512, 1024])
    w_bb = din("w_branch_b", [512, 1024])
    w_out = din("w_out", [1024, 1024])
    norm_ffn_g = din("norm_ffn_g", [1, 1024])
    w_fg = din("w_ffn_gate", [1024, DFF])
    w_fu = din("w_ffn_up", [1024, DFF])
    w_fd = din("w_ffn_down", [DFF, 1024])
    norm_ple_g = din("norm_ple_g", [1, 1024])
    w_pg = din("w_ple_gate", [1024, 1024])
    w_pp = din("w_ple_proj", [256, 1024])
    out = nc.dram_tensor("out", [SEQ, 1024], F32, kind="ExternalOutput").ap()
    if debug:
        x1d = nc.dram_tensor("x1d", [SEQ, 1024], F32, kind="ExternalOutput").ap()
        dbg_yb = nc.dram_tensor("dbg_yb", [128, 4, SEQ], BF16, kind="ExternalOutput").ap()
        dbg_ya = nc.dram_tensor("dbg_ya", [SEQ // 512, 128, 4, 512], BF16, kind="ExternalOutput").ap()
    else:
        x1d = nc.dram_tensor("x1d", [SEQ, 1024], F32, kind="ExternalOutput").ap()

    with contextlib.ExitStack() as st0:
        A0 = Alloc(nc, st0, "c")
        C = NS()
        C.ident = A0.sb("ident", [128, 128], BF16)
        C.ones_bf = A0.sb("ones_bf", [128, 128], BF16)
        C.eps_t = A0.sb("eps_t", [128, 1], F32)
        C.one_t = A0.sb("one_t", [128, 1], F32)
        C.lbT = A0.sb("lbT", [128, 4], F32)
        C.omlT = A0.sb("omlT", [128, 4], F32)
        C.nomlT = A0.sb("nomlT", [128, 4], F32)
        C.gon = A0.sb("gon", [128, 1], F32)
        C.qg = A0.sb("qg", [128, 1], F32)
        C.kg = A0.sb("kg", [128, 1], F32)
        C.fbias = A0.sb("fbias", [128, 8], F32)
        st_yb = contextlib.ExitStack()
        A_yb = Alloc(nc, st_yb, "yb")
        ybT = A_yb.sb("ybT", [128, 4, SEQ], BF16)
        C.mask_bf = A_yb.sb("mask_bf", [128, 128], BF16)
        C.ones_f = A_yb.sb("ones_f", [128, 128], F32)
        C.U_f = A_yb.sb("U_f", [128, 128], F32)
        C.E127 = A_yb.sb("E127", [128, 128], F32)
        C.gmixB = A_yb.sb("gmixB", [128, 8, 128], BF16)
        C.bd_bf = A_yb.sb("bd_bf", [128, 128], BF16)

        with contextlib.ExitStack() as st:
            A = Alloc(nc, st, "p0")
            S = Sched(nc)
            tmpf = A.sb("tmpf", [128, 128], F32)
            gT = A.sb("gT", [128, 8], F32)
            l0 = A.sb("l0", [128, 4], F32)
            l1 = A.sb("l1", [128, 4], F32)
            S.op("pool", lambda e: e.memset(C.ones_f[:], 1.0), writes=["ones_f"])
            S.op("pool", lambda e: e.memset(C.eps_t[:], EPS), writes=["eps"])
            S.op("pool", lambda e: e.memset(C.one_t[:], 1.0), writes=["one"])
            S.op("pool", lambda e: e.memset(C.ones_bf[:], 1.0), writes=["ones_bf"])
            S.op("pool", lambda e: e.affine_select(out=tmpf[:], in_=C.ones_f[:], pattern=[[-1, 128]],
                                                   compare_op=ALU.is_equal, fill=0.0, base=0, channel_multiplier=1),
                 reads=["ones_f"], writes=["tmpf"])
            S.op("dve", lambda e: e.tensor_copy(out=C.ident[:], in_=tmpf[:]), reads=["tmpf"], writes=["ident"])
            S.op("pool", lambda e: e.affine_select(out=C.U_f[:], in_=C.ones_f[:], pattern=[[1, 128]],
                                                   compare_op=ALU.is_ge, fill=0.0, base=0, channel_multiplier=-1),
                 reads=["ones_f"], writes=["U"])
            S.op("dve", lambda e: e.tensor_copy(out=C.mask_bf[:], in_=C.U_f[:]), reads=["U"], writes=["mask"])
            S.op("pool", lambda e: e.affine_select(out=C.E127[:], in_=C.ones_f[:], pattern=[[0, 128]],
                                                   compare_op=ALU.is_equal, fill=0.0, base=-127, channel_multiplier=1),
                 reads=["ones_f"], writes=["E127"])
            S.op("sp", lambda e: e.dma_start(out=gT[:], in_=norm_mix_g.rearrange("o (k p) -> p (o k)", p=128),
                                             allow_slow_non_contiguous=True), writes=["gT"], dma="gT")
            S.op("dve", lambda e: e.tensor_copy(out=C.gmixB[:], in_=gT[:].unsqueeze(2).to_broadcast([128, 8, 128])),
                 reads=["gT"], writes=["gmixB"])
            S.op("sp", lambda e: e.dma_start(out=l0[:], in_=hg_lb[0:1, :].rearrange("o (h c) -> c (o h)", c=128),
                                             allow_slow_non_contiguous=True), writes=["l0"], dma="l0")
            S.op("sp", lambda e: e.dma_start(out=l1[:], in_=hg_lb[1:2, :].rearrange("o (h c) -> c (o h)", c=128),
                                             allow_slow_non_contiguous=True), writes=["l1"], dma="l1")
            S.op("sp", lambda e: e.dma_start(out=C.gon[:], in_=hg_on.rearrange("o v -> v o"),
                                             allow_slow_non_contiguous=True), writes=["gon"], dma="gon")
            for hh in range(2):
                S.op("sp", lambda e, hh=hh: e.dma_start(out=C.qg[hh * 64:(hh + 1) * 64, :], in_=fox_qg.rearrange("o v -> v o"),
                                                        allow_slow_non_contiguous=True), writes=[("qg", hh)], dma=f"qg{hh}")
                S.op("sp", lambda e, hh=hh: e.dma_start(out=C.kg[hh * 64:(hh + 1) * 64, :], in_=fox_kg.rearrange("o v -> v o"),
                                                        allow_slow_non_contiguous=True), writes=[("kg", hh)], dma=f"kg{hh}")
            S.op("pool", lambda e: e.memset(C.bd_bf[:], 0.0), writes=["bd"])
            S.op("pool", lambda e: e.memset(C.bd_bf[0:64, 0:64], 1.0), reads=["bd"], writes=["bd"])
            S.op("pool", lambda e: e.memset(C.bd_bf[64:128, 64:128], 1.0), reads=["bd"], writes=["bd"])
            S.op("sp", lambda e: e.dma_start(out=C.fbias[:], in_=fox_fb.partition_broadcast(128)),
                 writes=["fbias"], dma="fbias")
            S.op("dve", lambda e: e.tensor_scalar(out=C.qg[:], in0=C.qg[:], scalar1=0.125, scalar2=None, op0=ALU.mult),
                 reads=[("qg", 0), ("qg", 1)], writes=["qg"])
            S.op("dve", lambda e: e.tensor_tensor(out=l1[:], in0=l1[:], in1=l0[:], op=ALU.subtract),
                 reads=["l0", "l1"], writes=["l1"])
            S.op("act", lambda e: e.activation(out=l1[:], in_=l1[:], func=AF.Exp), reads=["l1"], writes=["l1"])
            S.op("dve", lambda e: e.tensor_scalar_add(out=l1[:], in0=l1[:], scalar1=1.0), reads=["l1"], writes=["l1"])
            S.op("dve", lambda e: e.reciprocal(out=C.lbT[:], in_=l1[:]), reads=["l1"], writes=["lbT"])
            S.op("dve", lambda e: e.tensor_scalar(out=C.omlT[:], in0=C.lbT[:], scalar1=-1.0, scalar2=1.0,
                                                  op0=ALU.mult, op1=ALU.add), reads=["lbT"], writes=["omlT"])
            S.op("dve", lambda e: e.tensor_scalar(out=C.nomlT[:], in0=C.lbT[:], scalar1=-1.0, scalar2=None,
                                                  op0=ALU.add), reads=["lbT"], writes=["nomlT"])
            S.emit()

        with contextlib.ExitStack() as st:
            A = Alloc(nc, st, "a1")
            S = Sched(nc)
            S.excl.update(["pT", "psV", "psQ", "psN", "psS0", "psS1", "psQ", "psO1"])
            w_fq = A.sb("w_fq", [128, 8, 512], BF16)
            w_fk = A.sb("w_fk", [128, 8, 512], BF16)
            w_fv = A.sb("w_fv", [128, 8, 512], BF16)
            w_ff = A.sb("w_ff", [128, 8, 8], BF16)
            KT = A.sb("KT", [65, 8, SEQ], BF16)
            Vc = A.sb("Vc", [128, NT, 8, 64], BF16)
            cP = A.sb("cP", [128, NT, 8], F32)
            biasG = A.sb("biasG", [128, NT, 8], F32)
            Aaug = A.sb("Aaug", [128, 4, 8], BF16)
            hT = A.sb("hT", [128, 8, 512], BF16)
            sqr = Ring([(A.sb(f"sq{i}", [128, 512], BF16), f"sq{i}") for i in range(2)])
            rsr = Ring([(A.sb(f"rs{i}", [128, 512], F32), f"rs{i}") for i in range(2)])
            rinv_sb = A.sb("rinv_sb", [64, 512], F32)
            zb = A.sb("zb", [128, 8], F32)
            cref = A.sb("cref", [128, 8], F32)
            xring = Ring([(A.sb(f"x{i}", [128, 1024], F32), f"x{i}") for i in range(3)])
            W = NS()
            W.ssq = Ring([(A.sb(f"ssq{i}", [128, 4], F32), f"ssq{i}") for i in range(2)])
            W.hb = Ring([(A.sb(f"hb{i}", [128, 1024], BF16), f"hb{i}") for i in range(2)])
            QTs = [(A.sb(f"QT{i}", [65, 512], BF16), f"QT{i}") for i in range(8)]
            PTr = Ring([(A.sb(f"PT{i}", [128, 512], BF16), f"PT{i}") for i in range(3)])
            pTt = A.ps("pT", [128, 8, 128], BF16)
            W.pT = Ring([(pTt, "pT")])
            psV = A.ps("psV", [128, 512], F32)
            psQ = A.ps("psQ", [128, 512], F32)
            psN = A.ps("psN", [128, 512], F32)
            psSr = Ring([(A.ps(f"psS{i}", [128, 512], F32), f"psS{i}") for i in range(2)])
            psOr = Ring([(A.ps(f"psO{i}", [128, 512], F32), f"psO{i}") for i in range(2)])
            print("A1 sbuf bytes/partition", A.bytes + A0.bytes + A_yb.bytes)

            S.op("pool", lambda e: e.memset(KT[64:65, :, :], 1.0), writes=["KTones"])
            load_w(S, w_fv, "w_fv", w_in, 0, 8, FV, 512)
            for k in range(8):
                S.op("pool", lambda e, k=k: e.dma_start(out=w_ff[:, k, :], in_=w_in[k * 128:(k + 1) * 128, FFC:FFC + 8]),
                     writes=["w_ff"], dma="w_ff", nodep=True)
            load_w(S, w_fq, "w_fq", w_in, 0, 8, FQ, 512)
            load_w(S, w_fk, "w_fk", w_in, 0, 8, FK, 512)

            for G in range(NG):
                for i in range(4):
                    ti = 4 * G + i
                    xt, xk = xring.next()
                    S.op("sp", lambda e, xt=xt, ti=ti: e.dma_start(out=xt[:], in_=x[ti * 128:(ti + 1) * 128, :]),
                         writes=[xk], dma=xk)
                    norm_T(S, C, xt[:], xk, C.gmixB, hT[:, :, i * 128:(i + 1) * 128], ("hT", i), W, scale_eng="dve")
                    for k in range(8):
                        S.op("pe", lambda e, k=k, i=i: e.matmul(psV[:], lhsT=hT[:, k, i * 128:(i + 1) * 128], rhs=w_fv[:, k, :],
                                                               start=(k == 0), stop=(k == 7)),
                             reads=[("hT", i), "w_fv"], writes=["psV"])
                    S.op("dve", lambda e, ti=ti: e.tensor_copy(out=Vc[:, ti, :, :],
                                                               in_=psV[:].rearrange("p (h d) -> p h d", d=64)),
                         reads=["psV"], writes=[("Vc", ti)])
                    psF = psQ[:, 0:8]
                    psC = psQ[:, 8:16]
                    for k in range(8):
                        S.op("pe", lambda e, k=k, i=i, psF=psF: e.matmul(psF, lhsT=hT[:, k, i * 128:(i + 1) * 128], rhs=w_ff[:, k, :],
                                                                        start=(k == 0), stop=(k == 7)),
                             reads=[("hT", i), "w_ff"], writes=["psQ"])
                    S.op("dve", lambda e, psF=psF: e.tensor_tensor(out=zb[:], in0=psF, in1=C.fbias[:], op=ALU.add),
                         reads=["psQ"], writes=["zb"])
                    S.op("act", lambda e: e.activation(out=zb[:], in_=zb[:], func=AF.Exp, scale=-1.0),
                         reads=["zb"], writes=["zb"])
                    S.op("act", lambda e: e.activation(out=zb[:], in_=zb[:], func=AF.Ln, bias=C.one_t[:, 0:1], scale=1.0),
                         reads=["zb"], writes=["zb"])
                    S.op("pe", lambda e, ti=ti, psC=psC: e.matmul(psC, lhsT=C.U_f[:], rhs=zb[:], start=True, stop=(ti == 0)),
                         reads=["zb"], writes=["psQ"])
                    if ti > 0:
                        S.op("pe", lambda e, ti=ti, psC=psC: e.matmul(psC, lhsT=C.E127[:], rhs=cP[:, ti - 1, :], start=False, stop=True),
                             reads=[("cP", ti - 1)], writes=["psQ"])
                    S.op("dve", lambda e, ti=ti, psC=psC: e.tensor_copy(out=cP[:, ti, :], in_=psC),
                         reads=["psQ"], writes=[("cP", ti)])
                psR = psQ[:, 16:24]
                S.op("pe", lambda e, G=G, psR=psR: e.matmul(psR, lhsT=C.E127[:], rhs=cP[:, 4 * G + 1, :], start=True, stop=True),
                     reads=[("cP", 4 * G + 1)], writes=["psQ"])
                S.op("dve", lambda e, psR=psR: e.tensor_copy(out=cref[:], in_=psR), reads=["psQ"], writes=["cref"])
                for i in range(4):
                    S.op("dve", lambda e, i=i, G=G: e.tensor_tensor(out=Aaug[:, i, :], in0=cref[:], in1=cP[:, 4 * G + i, :],
                                                                   op=ALU.subtract),
                         reads=["cref", ("cP", 4 * G + i)], writes=[("Aaug", i)])
                nj = 4 * G + 4
                S.op("dve", lambda e, nj=nj: e.tensor_tensor(out=biasG[:, 0:nj, :], in0=cP[:, 0:nj, :],
                                                             in1=cref[:].unsqueeze(1).to_broadcast([128, nj, 8]),
                                                             op=ALU.subtract),
                     reads=["cref"] + [("cP", t) for t in range(nj)], writes=["biasG"])
                for h in range(8):
                    QT, qk = QTs[h]
                    pa, pak = (psQ, "psQ") if h % 2 == 0 else (psN, "psN")
                    for i in range(4):
                        S.op("pe", lambda e, i=i, h=h, pa=pa: e.matmul(pa[64:65, i * 128:(i + 1) * 128], lhsT=Aaug[:, i, h:h + 1],
                                                                      rhs=C.ident[:], start=(i == 0), stop=(i == 3)),
                             reads=[("Aaug", i)], writes=[pak], c=110.0)
                    S.op("act", lambda e, QT=QT, pa=pa: e.copy(out=QT[64:65, :], in_=pa[64:65, :]), reads=[pak], writes=[(qk, "r")], c=500.0)
                for p in range(4):
                    he, ho = 2 * p, 2 * p + 1
                    for k in range(8):
                        S.op("pe", lambda e, k=k, p=p: e.matmul(psQ[:], lhsT=w_fq[:, k, p * 128:(p + 1) * 128], rhs=hT[:, k, :],
                                                               start=(k == 0), stop=(k == 7)),
                             reads=[("hT", 0), ("hT", 1), ("hT", 2), ("hT", 3), "w_fq"], writes=["psQ"])
                    for k in range(8):
                        S.op("pe", lambda e, k=k, p=p: e.matmul(psV[:], lhsT=w_fk[:, k, p * 128:(p + 1) * 128], rhs=hT[:, k, :],
                                                               start=(k == 0), stop=(k == 7)),
                             reads=[("hT", 0), ("hT", 1), ("hT", 2), ("hT", 3), "w_fk"], writes=["psV"])
                    for (src, skey, gain, dsts) in (
                            (psQ, "psQ", C.qg, [(QTs[he][0][0:64, :], (QTs[he][1], "q")), (QTs[ho][0][0:64, :], (QTs[ho][1], "q"))]),
                            (psV, "psV", C.kg, [(KT[0:64, he, G * 512:(G + 1) * 512], ("KT", he, G)),
                                                (KT[0:64, ho, G * 512:(G + 1) * 512], ("KT", ho, G))])):
                        sq, sqk = sqr.next()
                        rs, rsk = rsr.next()
                        S.op("act", lambda e, src=src, sq=sq: e.activation(out=sq[:], in_=src[:], func=AF.Square),
                             reads=[skey], writes=[sqk], c=650.0)
                        S.op("pe", lambda e, sq=sq: e.matmul(psN[:], lhsT=C.bd_bf[:], rhs=sq[:], start=True, stop=True),
                             reads=[sqk], writes=["psN"])
                        S.op("act", lambda e, rs=rs: e.activation(out=rs[:], in_=psN[:], func=AF.Ln, scale=1.0 / 64.0,
                                                           bias=C.eps_t[:, 0:1]), reads=["psN"], writes=[rsk], c=650.0)
                        S.op("act", lambda e, rs=rs: e.activation(out=rs[:], in_=rs[:], func=AF.Exp, scale=-0.5),
                             reads=[rsk], writes=[rsk], c=650.0)
                        for hh, (dst, dkey) in enumerate(dsts):
                            ps_ = slice(hh * 64, (hh + 1) * 64)
                            S.op("dve", lambda e, src=src, gain=gain, dst=dst, rs=rs, ps_=ps_: e.scalar_tensor_tensor(
                                out=dst, in0=src[ps_, :], scalar=gain[ps_, 0:1], in1=rs[ps_, :], op0=ALU.mult, op1=ALU.mult),
                                reads=[skey, rsk], writes=[dkey], c=620.0)
                for h in range(8):
                    QT, qk0 = QTs[h]
                    qk = (qk0, "q")
                    qkr = (qk0, "r")
                    psO, ok = psOr.next()
                    pend = None

                    def pv(j, PT, ptk, psO=psO, ok=ok, h=h, G=G, nj=nj):
                        c0 = max(0, j - 4 * G) * 128
                        S.op("pe", lambda e: e.matmul(psO[0:64, c0:512], lhsT=Vc[:, j, h, :], rhs=PT[:, c0:512],
                                                      start=(j == 0), stop=(j == nj - 1)),
                             reads=[ptk, ("Vc", j)], writes=[ok])
                        S.op("pe", lambda e: e.matmul(psO[64:128, c0:512], lhsT=C.ones_bf[:, 0:64], rhs=PT[:, c0:512],
                                                      start=(j == 0), stop=(j == nj - 1)),
                             reads=[ptk], writes=[ok])

                    for j in range(nj):
                        r = max(0, j - 4 * G)
                        c0 = r * 128
                        psS, sk_ = psSr.next()
                        PT, ptk = PTr.next()
                        S.op("pe", lambda e, j=j, c0=c0, psS=psS, h=h, QT=QT: e.matmul(
                            psS[:, c0:512], lhsT=KT[0:65, h, j * 128:(j + 1) * 128], rhs=QT[0:65, c0:512], start=True, stop=True),
                            reads=[("KT", h, j // 4), "KTones", qk, qkr], writes=[sk_])
                        S.op("act", lambda e, j=j, c0=c0, psS=psS, PT=PT, h=h: e.activation(
                            out=PT[:, c0:512], in_=psS[:, c0:512], func=AF.Exp, bias=biasG[:, j, h:h + 1], scale=1.0),
                            reads=[sk_, "biasG"], writes=[ptk])
                        if j >= 4 * G:
                            S.op("pool", lambda e, c0=c0, PT=PT: e.tensor_tensor(out=PT[:, c0:c0 + 128], in0=PT[:, c0:c0 + 128],
                                                                                in1=C.mask_bf[:], op=ALU.mult),
                                 reads=[ptk], writes=[ptk])
                        if pend is not None:
                            pv(*pend)
                        pend = (j, PT, ptk)
                    pv(*pend)
                    S.op("dve", lambda e, psO=psO: e.reciprocal(out=rinv_sb[:], in_=psO[64:128, :]),
                         reads=[ok], writes=["rinv_sb"])
                    S.op("dve", lambda e, psO=psO, h=h, G=G: e.tensor_tensor(
                        out=ybT[(h % 2) * 64:(h % 2) * 64 + 64, h // 2, G * 512:(G + 1) * 512],
                        in0=psO[0:64, :], in1=rinv_sb[:], op=ALU.mult),
                        reads=[ok, "rinv_sb"], writes=[("ybT", G, h)])
            if debug:
                S.op("sp", lambda e: e.dma_start(out=dbg_yb, in_=ybT[:]), reads=[("ybT", G_, h_) for G_ in range(NG) for h_ in range(8)], dma="dbgyb")
            S.emit()

        with contextlib.ExitStack() as st:
            A = Alloc(nc, st, "a2")
            S = Sched(nc)
            S.excl.update(["pT"] + [f"b{i}" for i in range(7)])
            w_hq = A.sb("w_hq", [128, 8, 512], BF16)
            w_hf = A.sb("w_hf", [128, 8, 512], BF16)
            w_hi = A.sb("w_hi", [128, 8, 512], BF16)
            w_hg = A.sb("w_hg", [128, 8, 512], BF16)
            w_ga = A.sb("w_ga", [128, 8, 1024], BF16)
            w_gb = A.sb("w_gb", [128, 8, 1024], BF16)
            wA = A.sb("wA", [128, 4, 1024], BF16)
            wB = A.sb("wB", [128, 4, 1024], BF16)
            wO = A.sb("wO", [128, 8, 1024], BF16)
            hT = A.sb("hT", [128, 8, 512], BF16)
            v_tok = A.sb("v_tok", [128, 4, 512], BF16)
            tmp = [A.sb(f"t{i}", [128, 512], F32) for i in range(5)]
            qT = A.sb("qT", [128, 512], BF16)
            ktT = A.sb("ktT", [128, 512], BF16)
            khT = A.sb("khT", [128, 512], BF16)
            khat = A.sb("khat", [128, 4, 128], BF16)
            sg = A.sb("sg", [128, 4, 512], BF16)
            yaT = A.sb("yaT", [128, 4, 512], BF16)
            mT = A.sb("mT", [128, 8, 512], BF16)
            Sf = A.sb("Sf", [128, 4, 128], F32)
            Sb = A.sb("Sb", [128, 4, 128], BF16)
            PsT = A.sb("PsT", [128, 128], BF16)
            eL = A.sb("eL", [128, 4], F32)
            sq = A.sb("sq2", [128, 512], BF16)
            xring = Ring([(A.sb(f"x{i}", [128, 1024], F32), f"x{i}") for i in range(5)])
            W = NS()
            W.ssq = Ring([(A.sb(f"ssq{i}", [128, 4], F32), f"ssq{i}") for i in range(2)])
            W.hb = Ring([(A.sb("hb0", [128, 1024], BF16), "hb0")])
            pTt = A.ps("pT", [128, 8, 128], BF16)
            W.pT = Ring([(pTt, "pT")])
            bk = [A.ps(f"b{i}", [128, 512], F32) for i in range(7)]
            print("A2 sbuf bytes/partition", A.bytes + A0.bytes + A_yb.bytes)

            S.op("pool", lambda e: e.memset(Sf[:], 0.0), writes=[("Sf", h) for h in range(4)])
            S.op("pool", lambda e: e.memset(Sb[:], 0.0), writes=[("Sb", h) for h in range(4)])
            load_w(S, w_hi, "w_hi", w_in, 0, 8, HI, 512)
            load_w(S, w_hq, "w_hq", w_in, 0, 8, HQ, 512)
            load_w(S, w_hf, "w_hf", w_in, 0, 8, HF, 512)
            load_w(S, w_hg, "w_hg", w_in, 0, 8, HGC, 512)
            load_w(S, w_ga, "w_ga", w_in, 0, 8, GA, 1024)
            load_w(S, w_gb, "w_gb", w_in, 0, 8, GB, 1024)
            load_w(S, wA, "wA", w_ba, 0, 4, 0, 1024)
            load_w(S, wB, "wB", w_bb, 0, 4, 0, 1024)
            load_w(S, wO, "wO", w_out, 0, 8, 0, 1024)

            for G in range(NG):
                xts = []
                for i in range(4):
                    ti = 4 * G + i
                    xt, xk = xring.next()
                    xts.append((xt, xk))
                    S.op("sp", lambda e, xt=xt, ti=ti: e.dma_start(out=xt[:], in_=x[ti * 128:(ti + 1) * 128, :]),
                         writes=[xk], dma=xk)
                    norm_T(S, C, xt[:], xk, C.gmixB, hT[:, :, i * 128:(i + 1) * 128], ("hT", i), W)
                    for k in range(8):
                        S.op("pe", lambda e, k=k, i=i: e.matmul(bk[0][:], lhsT=hT[:, k, i * 128:(i + 1) * 128], rhs=w_hi[:, k, :],
                                                               start=(k == 0), stop=(k == 7)),
                             reads=[("hT", i), "w_hi"], writes=["b0"])
                    S.op("act", lambda e, i=i: e.copy(out=v_tok[:, i, :], in_=bk[0][:]), reads=["b0"], writes=[("v_tok", i)])
                hTall = [("hT", 0), ("hT", 1), ("hT", 2), ("hT", 3)]
                for h in range(4):
                    pZq, pZf, pZg, pSm, pOh, pN = bk[1], bk[2], bk[3], bk[4], bk[5], bk[6]
                    for (ps_, w_, wk_, key_) in ((pZf, w_hf, "w_hf", "b2"), (pZq, w_hq, "w_hq", "b1"), (pZg, w_hg, "w_hg", "b3")):
                        for k in range(8):
                            S.op("pe", lambda e, k=k, h=h, ps_=ps_, w_=w_: e.matmul(
                                ps_[:], lhsT=w_[:, k, h * 128:(h + 1) * 128], rhs=hT[:, k, :], start=(k == 0), stop=(k == 7)),
                                reads=hTall + [wk_], writes=[key_])
                    t0, t1, t2, t3, t4 = tmp
                    sigmoid_act(S, pZf[:], "b2", t0[:], "t0", C)
                    S.op("act", lambda e, h=h: e.activation(out=t1[:], in_=t0[:], func=AF.Ln, scale=C.omlT[:, h:h + 1],
                                                            bias=C.lbT[:, h:h + 1]), reads=["t0"], writes=["t1"])
                    S.op("dve", lambda e, h=h: e.tensor_scalar(out=t2[:], in0=t0[:], scalar1=C.nomlT[:, h:h + 1],
                                                               scalar2=C.omlT[:, h:h + 1], op0=ALU.mult, op1=ALU.add),
                         reads=["t0"], writes=["t2"])
                    for i in range(4):
                        S.op("dve", lambda e, i=i: e.tensor_tensor_scan(out=t3[:, i * 128:(i + 1) * 128], data0=C.ones_f[:],
                                                                       data1=t1[:, i * 128:(i + 1) * 128], initial=0.0,
                                                                       op0=ALU.mult, op1=ALU.add),
                             reads=["t1"], writes=["t3"])
                    S.op("act", lambda e: e.activation(out=eL[:], in_=t3[:].rearrange("p (i c) -> p i c", c=128)[:, :, 127],
                                                       func=AF.Exp), reads=["t3"], writes=["eL"])
                    S.op("act", lambda e: e.activation(out=t1[:], in_=t3[:], func=AF.Exp), reads=["t3"], writes=["t1"])
                    S.op("act", lambda e: e.activation(out=t0[:], in_=t3[:], func=AF.Exp, scale=-1.0),
                         reads=["t3"], writes=["t0"])
                    sigmoid_act(S, pZq[:], "b1", t4[:], "t4", C)
                    S.op("dve", lambda e: e.tensor_tensor(out=t4[:], in0=pZq[:], in1=t4[:], op=ALU.mult),
                         reads=["b1", "t4"], writes=["t4"])
                    S.op("dve", lambda e: e.tensor_tensor(out=qT[:], in0=t4[:], in1=t1[:], op=ALU.mult),
                         reads=["t4", "t1"], writes=["qT"])
                    S.op("dve", lambda e: e.tensor_tensor(out=ktT[:], in0=t2[:], in1=t0[:], op=ALU.mult),
                         reads=["t2", "t0"], writes=["ktT"])
                    for i in range(4):
                        S.op("act", lambda e, i=i: e.activation(out=khT[:, i * 128:(i + 1) * 128], in_=ktT[:, i * 128:(i + 1) * 128],
                                                               func=AF.Copy, scale=eL[:, i:i + 1]),
                             reads=["ktT", "eL"], writes=["khT"], c=340.0)
                    sigmoid_act(S, pZg[:], "b3", t4[:], "t4", C)
                    S.op("dve", lambda e, h=h: e.tensor_tensor(out=sg[:, h, :], in0=pZg[:], in1=t4[:], op=ALU.mult),
                         reads=["b3", "t4"], writes=[("sg", h)])
                    for i in range(4):
                        S.op("pe", lambda e, i=i: e.transpose(out=pTt[:, i, :], in_=khT[:, i * 128:(i + 1) * 128], identity=C.ident[:]),
                             reads=["khT"], writes=["pT"])
                    S.op("act", lambda e: e.copy(out=khat[:], in_=pTt[:, 0:4, :]), reads=["pT"], writes=["khat"])
                    for i in range(4):
                        cs = slice(i * 128, (i + 1) * 128)
                        pSc = pSm[:, 0:128]
                        pSt = pN[:, 0:128]
                        S.op("pe", lambda e, cs=cs, pSc=pSc: e.matmul(pSc, lhsT=ktT[:, cs], rhs=qT[:, cs], start=True, stop=True),
                             reads=["ktT", "qT"], writes=["b4"])
                        S.op("dve", lambda e, pSc=pSc: e.tensor_tensor(out=PsT[:], in0=pSc, in1=C.mask_bf[:], op=ALU.mult),
                             reads=["b4"], writes=["PsT"])
                        S.op("pe", lambda e, cs=cs, i=i, h=h: e.matmul(pOh[:, cs], lhsT=v_tok[:, i, h * 128:(h + 1) * 128], rhs=PsT[:],
                                                                      start=True, stop=False),
                             reads=[("v_tok", i), "PsT"], writes=["b5"])
                        S.op("pe", lambda e, cs=cs, h=h: e.matmul(pOh[:, cs], lhsT=Sb[:, h, :], rhs=qT[:, cs], start=False, stop=True),
                             reads=[("Sb", h), "qT"], writes=["b5"])
                        S.op("pe", lambda e, i=i, h=h, pSt=pSt: e.matmul(pSt, lhsT=khat[:, i, :], rhs=v_tok[:, i, h * 128:(h + 1) * 128],
                                                                        start=True, stop=True),
                             reads=["khat", ("v_tok", i)], writes=["b6"])
                        S.op("dve", lambda e, i=i, h=h, pSt=pSt: e.scalar_tensor_tensor(
                            out=Sf[:, h, :], in0=Sf[:, h, :], scalar=eL[:, i:i + 1], in1=pSt, op0=ALU.mult, op1=ALU.add),
                            reads=[("Sf", h), "eL", "b6"], writes=[("Sf", h)])
                        S.op("act", lambda e, h=h: e.copy(out=Sb[:, h, :], in_=Sf[:, h, :]), reads=[("Sf", h)], writes=[("Sb", h)])
                    S.op("act", lambda e: e.activation(out=sq[:], in_=pOh[:], func=AF.Square), reads=["b5"], writes=["sq"])
                    S.op("pe", lambda e: e.matmul(pN[:], lhsT=C.ones_bf[:], rhs=sq[:], start=True, stop=True),
                         reads=["sq"], writes=["b6"])
                    S.op("act", lambda e: e.activation(out=t2[:], in_=pN[:], func=AF.Ln, scale=1.0 / 128.0, bias=C.eps_t[:, 0:1]),
                         reads=["b6"], writes=["t2"])
                    S.op("act", lambda e: e.activation(out=t2[:], in_=t2[:], func=AF.Exp, scale=-0.5), reads=["t2"], writes=["t2"])
                    S.op("dve", lambda e: e.scalar_tensor_tensor(out=t2[:], in0=pOh[:], scalar=C.gon[:, 0:1], in1=t2[:],
                                                                 op0=ALU.mult, op1=ALU.mult), reads=["b5", "t2"], writes=["t2"])
                    S.op("dve", lambda e, h=h: e.tensor_tensor(out=yaT[:, h, :], in0=t2[:], in1=sg[:, h, :], op=ALU.mult),
                         reads=["t2", ("sg", h)], writes=[("yaT", h)])
                if debug:
                    S.op("sp", lambda e, G=G: e.dma_start(out=dbg_ya[G], in_=yaT[:]), reads=[("yaT", h) for h in range(4)], dma="dbgya")
                t0, t1 = tmp[0], tmp[1]
                for dc in range(8):
                    ds_ = slice(dc * 128, (dc + 1) * 128)
                    pGa, pGb, pA, pB = bk[1], bk[2], bk[3], bk[4]
                    for k in range(8):
                        S.op("pe", lambda e, k=k, ds_=ds_: e.matmul(pGa[:], lhsT=w_ga[:, k, ds_], rhs=hT[:, k, :], start=(k == 0), stop=(k == 7)),
                             reads=hTall + ["w_ga"], writes=["b1"])
                    for k in range(8):
                        S.op("pe", lambda e, k=k, ds_=ds_: e.matmul(pGb[:], lhsT=w_gb[:, k, ds_], rhs=hT[:, k, :], start=(k == 0), stop=(k == 7)),
                             reads=hTall + ["w_gb"], writes=["b2"])
                    for k in range(4):
                        S.op("pe", lambda e, k=k, ds_=ds_: e.matmul(pA[:], lhsT=wA[:, k, ds_], rhs=yaT[:, k, :], start=(k == 0), stop=(k == 3)),
                             reads=[("yaT", k), "wA"], writes=["b3"])
                    for k in range(4):
                        S.op("pe", lambda e, k=k, ds_=ds_, G=G: e.matmul(pB[:], lhsT=wB[:, k, ds_], rhs=ybT[:, k, G * 512:(G + 1) * 512],
                                                                        start=(k == 0), stop=(k == 3)),
                             reads=["wB"], writes=["b4"])
                    sigmoid_act(S, pGa[:], "b1", t0[:], "t0", C)
                    sigmoid_act(S, pGb[:], "b2", t1[:], "t1", C)
                    S.op("dve", lambda e: e.tensor_tensor(out=t0[:], in0=pA[:], in1=t0[:], op=ALU.mult), reads=["b3", "t0"], writes=["t0"])
                    S.op("dve", lambda e: e.tensor_tensor(out=t1[:], in0=pB[:], in1=t1[:], op=ALU.mult), reads=["b4", "t1"], writes=["t1"])
                    S.op("dve", lambda e, dc=dc: e.tensor_tensor(out=mT[:, dc, :], in0=t0[:], in1=t1[:], op=ALU.add),
                         reads=["t0", "t1"], writes=[("mT", dc)])
                for i in range(4):
                    ti = 4 * G + i
                    xt, xk = xts[i]
                    for half in range(2):
                        pX = bk[5 + half]
                        for k in range(8):
                            S.op("pe", lambda e, k=k, i=i, half=half, pX=pX: e.matmul(
                                pX[:], lhsT=mT[:, k, i * 128:(i + 1) * 128], rhs=wO[:, k, half * 512:(half + 1) * 512],
                                start=(k == 0), stop=(k == 7)),
                                reads=[("mT", k), "wO"], writes=[f"b{5 + half}"])
                        S.op("dve", lambda e, xt=xt, half=half, pX=pX: e.tensor_tensor(
                            out=xt[:, half * 512:(half + 1) * 512], in0=pX[:], in1=xt[:, half * 512:(half + 1) * 512], op=ALU.add),
                            reads=[f"b{5 + half}", xk], writes=[xk])
                    S.op("sp", lambda e, xt=xt, ti=ti: e.dma_start(out=x1d[ti * 128:(ti + 1) * 128, :], in_=xt[:]),
                         reads=[xk], dma="st_" + xk)
            S.emit()

        st_yb.close()
        TB = 4
        NB = TB * 128
        with contextlib.ExitStack() as st:
            A = Alloc(nc, st, "b")
            S = Sched(nc)
            S.excl.update(["pT"] + [f"b{i}" for i in range(7)])
            wG = A.sb("wG", [128, 8, DFF], BF16)
            wU = A.sb("wU", [128, 8, DFF], BF16)
            wD = A.sb("wD", [128, NFF, 1024], BF16)
            wPG = A.sb("wPG", [128, 8, 1024], BF16)
            wPP = A.sb("wPP", [128, 2, 1024], BF16)
            gffnB = A.sb("gffnB", [128, 8, 128], BF16)
            gpleB = A.sb("gpleB", [128, 8, 128], BF16)
            gT2 = A.sb("gT2", [128, 8], F32)
            gT3 = A.sb("gT3", [128, 8], F32)
            h2T = A.sb("h2T", [128, 8, NB], BF16)
            actT = A.sb("actT", [128, NFF, NB], BF16)
            tr = Ring([(A.sb(f"t{i}", [128, 512], F32), f"t{i}") for i in range(2)])
            pbr = Ring([(A.sb(f"pb{i}", [128, 256], BF16), f"pb{i}") for i in range(2)])
            ppT = A.sb("ppT", [128, 2, 128], BF16)
            xaring = Ring([(A.sb("xa0", [128, 1024], F32), "xa0")])
            xring = Ring([(A.sb(f"x{i}", [128, 1024], F32), f"x{i}") for i in range(2)])
            h3T = h2T[:, :, 384:512]
            W = NS()
            W.ssq = Ring([(A.sb(f"ssq{i}", [128, 4], F32), f"ssq{i}") for i in range(2)])
            W.hb = Ring([(A.sb("hb0", [128, 1024], BF16), "hb0")])
            pTt = A.ps("pT", [128, 8, 128], BF16)
            W.pT = Ring([(pTt, "pT")])
            bk = [A.ps(f"b{i}", [128, 512], F32) for i in range(7)]
            print("B sbuf bytes/partition", A.bytes + A0.bytes)

            S.op("sp", lambda e: e.dma_start(out=gT2[:], in_=norm_ffn_g.rearrange("o (k p) -> p (o k)", p=128),
                                             allow_slow_non_contiguous=True), writes=["gT2"], dma="gT2")
            S.op("sp", lambda e: e.dma_start(out=gT3[:], in_=norm_ple_g.rearrange("o (k p) -> p (o k)", p=128),
                                             allow_slow_non_contiguous=True), writes=["gT3"], dma="gT3")
            S.op("dve", lambda e: e.tensor_copy(out=gffnB[:], in_=gT2[:].unsqueeze(2).to_broadcast([128, 8, 128])),
                 reads=["gT2"], writes=["gffnB"])
            S.op("dve", lambda e: e.tensor_copy(out=gpleB[:], in_=gT3[:].unsqueeze(2).to_broadcast([128, 8, 128])),
                 reads=["gT3"], writes=["gpleB"])
            for blk in range(4):
                for (dst_, key_, src_) in ((wG, "wG", w_fg), (wU, "wU", w_fu)):
                    for k in range(8):
                        S.op("pool", lambda e, k=k, blk=blk, dst_=dst_, src_=src_: e.dma_start(
                            out=dst_[:, k, blk * 704:(blk + 1) * 704], in_=src_[k * 128:(k + 1) * 128, blk * 704:(blk + 1) * 704]),
                            writes=[(key_, blk)], dma=f"{key_}{blk}", nodep=True)
            load_w(S, wD, "wD", w_fd, 0, NFF, 0, 1024)
            load_w(S, wPG, "wPG", w_pg, 0, 8, 0, 1024)
            load_w(S, wPP, "wPP", w_pp, 0, 2, 0, 1024)

            for g in range(NT // TB):
                for i in range(TB):
                    ti = g * TB + i
                    xt, xk = xaring.next()
                    S.op("sp", lambda e, xt=xt, ti=ti: e.dma_start(out=xt[:], in_=x1d[ti * 128:(ti + 1) * 128, :]),
                         writes=[xk], dma=xk)
                    norm_T(S, C, xt[:], xk, gffnB, h2T[:, :, i * 128:(i + 1) * 128], ("h2T", i), W)
                h2all = [("h2T", i) for i in range(TB)]
                for f in range(NFF):
                    fs = slice(f * 128, (f + 1) * 128)
                    pG = bk[0 + (f % 2)]
                    pU = bk[2 + (f % 2)]
                    kG = f"b{0 + (f % 2)}"
                    kU = f"b{2 + (f % 2)}"
                    for k in range(8):
                        S.op("pe", lambda e, k=k, fs=fs, pG=pG: e.matmul(pG[:, 0:NB], lhsT=wG[:, k, fs], rhs=h2T[:, k, :],
                                                                        start=(k == 0), stop=(k == 7)),
                             reads=h2all + [("wG", b_) for b_ in sorted({(f * 128) // 704, (f * 128 + 127) // 704})], writes=[kG])
                    for k in range(8):
                        S.op("pe", lambda e, k=k, fs=fs, pU=pU: e.matmul(pU[:, 0:NB], lhsT=wU[:, k, fs], rhs=h2T[:, k, :],
                                                                        start=(k == 0), stop=(k == 7)),
                             reads=h2all + [("wU", b_) for b_ in sorted({(f * 128) // 704, (f * 128 + 127) // 704})], writes=[kU])
                    t, tk = tr.next()
                    sigmoid_act(S, pG[:, 0:NB], kG, t[:, 0:NB], tk, C)
                    S.op("dve", lambda e, t=t, pG=pG: e.tensor_tensor(out=t[:, 0:NB], in0=pG[:, 0:NB], in1=t[:, 0:NB], op=ALU.mult),
                         reads=[kG, tk], writes=[tk])
                    S.op("dve", lambda e, t=t, pU=pU, f=f: e.tensor_tensor(out=actT[:, f, :], in0=pU[:, 0:NB], in1=t[:, 0:NB], op=ALU.mult),
                         reads=[kU, tk], writes=[("actT", f)])
                for i in range(TB):
                    ti = g * TB + i
                    xt, xk = xring.next()
                    S.op("sp", lambda e, xt=xt, ti=ti: e.dma_start(out=xt[:], in_=x1d[ti * 128:(ti + 1) * 128, :]),
                         writes=[xk], dma=xk)
                    for half in range(2):
                        pD = bk[4 + half]
                        kD = f"b{4 + half}"
                        for f in range(NFF):
                            S.op("pe", lambda e, f=f, i=i, half=half, pD=pD: e.matmul(
                                pD[:], lhsT=actT[:, f, i * 128:(i + 1) * 128], rhs=wD[:, f, half * 512:(half + 1) * 512],
                                start=(f == 0), stop=(f == NFF - 1)),
                                reads=[("actT", f), "wD"], writes=[kD])
                        S.op("dve", lambda e, xt=xt, half=half, pD=pD: e.tensor_tensor(
                            out=xt[:, half * 512:(half + 1) * 512], in0=pD[:], in1=xt[:, half * 512:(half + 1) * 512], op=ALU.add),
                            reads=[kD, xk], writes=[xk])
                    norm_T(S, C, xt[:], xk, gpleB, h3T, ("h2T", 3), W)
                    pb, pbk = pbr.next()
                    S.op("pool", lambda e, pb=pb, ti=ti: e.dma_start(out=pb[:], in_=p_in[ti * 128:(ti + 1) * 128, :]),
                         writes=[pbk], dma=pbk)
                    for c in range(2):
                        S.op("pe", lambda e, c=c, pb=pb: e.transpose(out=pTt[:, c, :], in_=pb[:, c * 128:(c + 1) * 128], identity=C.ident[:]),
                             reads=[pbk], writes=["pT"])
                    S.op("act", lambda e: e.copy(out=ppT[:], in_=pTt[:, 0:2, :]), reads=["pT"], writes=["ppT"])
                    for half in range(2):
                        hs = slice(half * 512, (half + 1) * 512)
                        pPG = bk[4 + half]
                        kPG = f"b{4 + half}"
                        pPP = bk[6]
                        for k in range(8):
                            S.op("pe", lambda e, k=k, hs=hs, pPG=pPG: e.matmul(pPG[:], lhsT=h3T[:, k, :], rhs=wPG[:, k, hs],
                                                                              start=(k == 0), stop=(k == 7)),
                                 reads=[("h2T", 3), "wPG"], writes=[kPG])
                        for k in range(2):
                            S.op("pe", lambda e, k=k, hs=hs: e.matmul(pPP[:], lhsT=ppT[:, k, :], rhs=wPP[:, k, hs],
                                                                     start=(k == 0), stop=(k == 1)),
                                 reads=["ppT", "wPP"], writes=["b6"])
                        t, tk = tr.next()
                        sigmoid_act(S, pPG[:], kPG, t[:], tk, C)
                        S.op("dve", lambda e, t=t: e.tensor_tensor(out=t[:], in0=pPP[:], in1=t[:], op=ALU.mult),
                             reads=["b6", tk], writes=[tk])
                        S.op("dve", lambda e, t=t, xt=xt, hs=hs: e.tensor_tensor(out=xt[:, hs], in0=xt[:, hs], in1=t[:], op=ALU.add),
                             reads=[tk, xk], writes=[xk])
                    S.op("sp", lambda e, xt=xt, ti=ti: e.dma_start(out=out[ti * 128:(ti + 1) * 128, :], in_=xt[:]),
                         reads=[xk], dma="ost_" + xk)
            S.emit()
    return nc


_IN_NAMES = ["x", "p", "norm_mix_g", "w_in", "hg_lb_logits", "hg_onorm_g", "fox_f_bias", "fox_q_norm_g",
             "fox_k_norm_g", "w_branch_a", "w_branch_b", "w_out", "norm_ffn_g", "w_ffn_gate", "w_ffn_up",
             "w_ffn_down", "norm_ple_g", "w_ple_gate", "w_ple_proj"]


def make_in_maps(inputs, n_cores, SEQ):
    f = lambda a: np.ascontiguousarray(np.asarray(a, dtype=np.float32))
    shared = {}
    for k in _IN_NAMES:
        if k in ("x", "p"):
            continue
        a = f(inputs[k])
        if k == "hg_lb_logits":
            shared[k] = a.reshape(2, 512)
        else:
            shared[k] = a.reshape(a.shape[-2] if a.ndim >= 2 else 1, a.shape[-1])
    xs = f(inputs["x"])
    ps = f(inputs["p"])[0]
    maps = []
    for c in range(n_cores):
        m = dict(shared)
        m["x"] = np.ascontiguousarray(xs[c])
        m["p"] = np.ascontiguousarray(ps[c])
        maps.append(m)
    return maps


def kernel(**inputs):
    x = np.asarray(inputs["x"])
    B, SEQ, _ = x.shape
    nc = build_program(SEQ)
    in_maps = make_in_maps(inputs, B, SEQ)
    res = run_bass_kernel_spmd(nc, in_maps, core_ids=list(range(B)))
    outs = [np.asarray(r["out"], dtype=np.float32) for r in res.results]
    return np.stack(outs, axis=0)
```

```python
import contextlib
import numpy as np
import concourse.bass as bass
import concourse.mybir as mybir
from concourse.bass_utils import run_bass_kernel_spmd

F32 = mybir.dt.float32
BF16 = mybir.dt.bfloat16
AF = mybir.ActivationFunctionType
ALU = mybir.AluOpType

ENGS = ("pe", "act", "dve", "pool", "sp")
EPS = 1e-6
HQ, HF, HI, HGC, FQ, FK, FV, FFC, GA, GB = 0, 512, 1024, 1536, 2048, 2560, 3072, 3584, 3592, 4616
INC = 5640
DFF = 2816
NFF = DFF // 128


class _Op:
    __slots__ = ("eng", "fn", "dma", "preds", "deps", "signal", "sig_idx", "gid", "pos", "dma_val", "cost",
                 "fin", "done", "tail")

    def __init__(self, eng, fn, dma, cost):
        self.eng = eng
        self.fn = fn
        self.dma = dma
        self.cost = cost
        self.preds = []
        self.deps = []
        self.signal = False
        self.sig_idx = None
        self.gid = None
        self.pos = None
        self.dma_val = None
        self.fin = None
        self.done = False


_DEF_COST = {"pe": 260.0, "act": 700.0, "dve": 720.0, "pool": 600.0, "sp": 100.0}


class Sched:
    SEM_ROT = 16000
    WINDOW = 256
    REORDER = True
    PRIO = "cp"

    def __init__(self, nc):
        self.nc = nc
        self.ops = []
        self.lastw = {}
        self.readers = {}
        self.excl = set()
        self.wgroup = {}

    def op(self, eng, fn, reads=(), writes=(), dma=None, nodep=False, c=None):
        if self.excl:
            writes = list(writes) + [k for k in reads if k in self.excl]
            reads = [k for k in reads if k not in self.excl]
        if c is None:
            c = 2600.0 if dma is not None else _DEF_COST[eng]
        o = _Op(eng, fn, dma, c)
        o.gid = len(self.ops)
        preds = {}
        if not nodep:
            for k in reads:
                w = self.lastw.get(k)
                if w is not None:
                    preds[id(w)] = (w, True)
                for g_ in self.wgroup.get(k, ()):
                    preds[id(g_)] = (g_, True)
            for k in writes:
                w = self.lastw.get(k)
                if w is not None and id(w) not in preds:
                    preds[id(w)] = (w, False)
                for g_ in self.wgroup.get(k, ()):
                    if id(g_) not in preds:
                        preds[id(g_)] = (g_, False)
                for r in self.readers.get(k, ()):
                    if id(r) not in preds and r is not o:
                        preds[id(r)] = (r, False)
        o.preds = list(preds.values())
        for k in writes:
            if nodep and dma is not None:
                self.wgroup.setdefault(k, []).append(o)
            else:
                self.wgroup.pop(k, None)
            self.lastw[k] = o
            self.readers[k] = []
        for k in reads:
            self.readers.setdefault(k, []).append(o)
        self.ops.append(o)
        return o

    def _schedule(self):
        pend = {e: [o for o in self.ops if o.eng == e] for e in ENGS}
        if not self.REORDER:
            return pend
        if self.PRIO == "cp":
            succ_max = [0.0] * len(self.ops)
            for o in reversed(self.ops):
                t = succ_max[o.gid] + o.cost
                o.tail = t
                for p, _ in o.preds:
                    if t > succ_max[p.gid]:
                        succ_max[p.gid] = t
        order = {e: [] for e in ENGS}
        free = {e: 0.0 for e in ENGS}
        head = {e: 0 for e in ENGS}
        nleft = len(self.ops)
        while nleft:
            best = None
            for e in ENGS:
                L = pend[e]
                h = head[e]
                while h < len(L) and L[h].done:
                    h += 1
                head[e] = h
                if h >= len(L):
                    continue
                cand = None
                cnt = 0
                i = h
                while i < len(L) and cnt < self.WINDOW:
                    o = L[i]
                    i += 1
                    if o.done:
                        continue
                    cnt += 1
                    rdy = 0.0
                    ok = True
                    for p, _ in o.preds:
                        if not p.done:
                            ok = False
                            break
                        t = p.fin + (0.0 if p.eng == e and p.dma is None else 120.0)
                        if t > rdy:
                            rdy = t
                    if not ok:
                        continue
                    st = rdy if rdy > free[e] else free[e]
                    if self.PRIO == "cp":
                        if cand is None or st < cand[0] - 1e-9 or (abs(st - cand[0]) <= 1e-9 and o.tail > cand[1].tail):
                            cand = (st, o)
                    else:
                        if cand is None or st < cand[0] - 1e-9:
                            cand = (st, o)
                        if rdy <= free[e]:
                            break
                if cand is not None and (best is None or cand[0] < best[0] - 1e-9 or
                                         (abs(cand[0] - best[0]) <= 1e-9 and cand[1].gid < best[1].gid)):
                    best = cand
            st, o = best
            e = o.eng
            occ = (1000.0 if e == "pool" else 60.0) if o.dma is not None else o.cost
            free[e] = st + occ
            o.fin = st + o.cost
            o.done = True
            o.pos = len(order[e])
            order[e].append(o)
            nleft -= 1
        self.est_ns = max(o.fin for o in self.ops) if self.ops else 0.0
        busy = {e: 0.0 for e in ENGS}
        for o in self.ops:
            busy[o.eng] += ((1000.0 if o.eng == "pool" else 60.0) if o.dma is not None else o.cost)
        print("  busy us", {e: round(v / 1e3) for e, v in busy.items()})
        return order

    def emit(self):
        nc = self.nc
        order = self._schedule()
        for e in ENGS:
            for i, o in enumerate(order[e]):
                o.pos = i
        dma_cnt = {}
        for e in ENGS:
            for o in order[e]:
                best = {}
                for p, is_raw in o.preds:
                    if p.dma is not None:
                        o.deps.append(p)
                        continue
                    if p.eng == e:
                        assert p.pos < o.pos
                        if not is_raw or e == "pe":
                            continue
                    b = best.get(p.eng)
                    if b is None or p.pos > b.pos:
                        best[p.eng] = p
                for p in best.values():
                    o.deps.append(p)
                    p.signal = True
                if o.dma is not None:
                    c = dma_cnt.get(o.dma, 0) + 1
                    dma_cnt[o.dma] = c
                    o.dma_val = 16 * c
        with contextlib.ExitStack() as st:
            eng_sems = {}
            for e in ENGS:
                n = 0
                for o in order[e]:
                    if o.signal:
                        o.sig_idx = n
                        n += 1
                nsem = max(1, (n + self.SEM_ROT - 1) // self.SEM_ROT)
                print("  sched", e, "ops", len(order[e]), "signals", n)
                eng_sems[e] = [st.enter_context(nc.semaphore(f"s_{e}{i}")) for i in range(nsem)]
            print("  est ns", getattr(self, "est_ns", None))
            dma_sems = {k: st.enter_context(nc.semaphore(f"d_{k}")) for k in dma_cnt}
            blk_st = contextlib.ExitStack()
            block = blk_st.enter_context(nc.Block())

            def target(p):
                if p.dma is not None:
                    return (dma_sems[p.dma], p.dma_val, ("d", p.dma))
                j, v = divmod(p.sig_idx, self.SEM_ROT)
                return (eng_sems[p.eng][j], v + 1, (p.eng, j))

            def run(e, handle):
                waited = {}
                for o in order[e]:
                    need = {}
                    for p in o.deps:
                        sem, val, key = target(p)
                        if waited.get(key, 0) >= val:
                            continue
                        if key not in need or need[key][1] < val:
                            need[key] = (sem, val)
                    for key, (sem, val) in need.items():
                        handle.wait_ge(sem, val)
                        waited[key] = val
                    ins = o.fn(handle)
                    if o.dma is not None:
                        ins.then_inc(dma_sems[o.dma], 16)
                    elif o.signal:
                        j, _ = divmod(o.sig_idx, self.SEM_ROT)
                        ins.then_inc(eng_sems[e][j], 1)
                if e == "sp":
                    for k, c in dma_cnt.items():
                        if waited.get(("d", k), 0) < 16 * c:
                            handle.wait_ge(dma_sems[k], 16 * c)

            block.tensor(lambda h: run("pe", h))
            block.scalar(lambda h: run("act", h))
            block.vector(lambda h: run("dve", h))
            block.gpsimd(lambda h: run("pool", h))
            block.sync(lambda h: run("sp", h))
            blk_st.close()
            nc.all_engine_barrier()


class Ring:
    def __init__(self, items):
        self.items = items
        self.i = 0

    def next(self):
        r = self.items[self.i % len(self.items)]
        self.i += 1
        return r


class NS:
    pass


class Alloc:
    def __init__(self, nc, st, tag):
        self.nc, self.st, self.tag, self.bytes = nc, st, tag, 0

    def sb(self, name, shape, dt):
        n = 1
        for s in shape[1:]:
            n *= s
        self.bytes += n * (4 if dt == F32 else 2)
        return self.st.enter_context(self.nc.sbuf_tensor(f"{self.tag}_{name}", list(shape), dt))

    def ps(self, name, shape, dt):
        return self.st.enter_context(self.nc.psum_tensor(f"{self.tag}_{name}", list(shape), dt))


def load_w(S, dst, key, src, r0, nk, c0, ncols, cstep=512):
    for k in range(nk):
        for cc in range(0, ncols, cstep):
            w = min(cstep, ncols - cc)
            S.op("pool", lambda e, k=k, cc=cc, w=w: e.dma_start(
                out=dst[:, k, cc:cc + w], in_=src[r0 + k * 128:r0 + (k + 1) * 128, c0 + cc:c0 + cc + w]),
                writes=[key], dma=key, nodep=True)


def norm_T(S, C, X, xkey, gB, dst, dkey, W, scale_eng="act"):
    ssq, sk = W.ssq.next()
    hb, hk = W.hb.next()
    pT, pk = W.pT.next()
    S.op("act", lambda e: e.activation(out=hb[:], in_=X, func=AF.Square, accum_out=ssq[:, 0:1]),
         reads=[xkey], writes=[hk, sk], c=1100.0)
    S.op("act", lambda e: e.activation(out=ssq[:, 1:2], in_=ssq[:, 0:1], func=AF.Ln, scale=1.0 / 1024.0,
                                       bias=C.eps_t[:, 0:1]), reads=[sk], writes=[sk], c=260.0)
    S.op("act", lambda e: e.activation(out=ssq[:, 2:3], in_=ssq[:, 1:2], func=AF.Exp, scale=-0.5),
         reads=[sk], writes=[sk], c=260.0)
    if scale_eng == "act":
        S.op("act", lambda e: e.activation(out=hb[:], in_=X, func=AF.Copy, scale=ssq[:, 2:3]),
             reads=[xkey, sk], writes=[hk], c=1100.0)
    else:
        S.op("dve", lambda e: e.tensor_scalar(out=hb[:], in0=X, scalar1=ssq[:, 2:3], scalar2=None, op0=ALU.mult),
             reads=[xkey, sk], writes=[hk], c=1150.0)
    for k in range(8):
        S.op("pe", lambda e, k=k: e.transpose(out=pT[:, k, :], in_=hb[:, k * 128:(k + 1) * 128], identity=C.ident[:]),
             reads=[hk], writes=[pk], c=110.0)
    S.op("dve", lambda e: e.tensor_tensor(out=dst, in0=pT[:], in1=gB[:], op=ALU.mult), reads=[pk], writes=[dkey], c=1100.0)


def sigmoid_act(S, src, skey, dst, dkey, C, np_=128):
    S.op("act", lambda e: e.activation(out=dst, in_=src, func=AF.Exp, scale=-1.0), reads=[skey], writes=[dkey])
    S.op("act", lambda e: e.activation(out=dst, in_=dst, func=AF.Ln, bias=C.one_t[0:np_, 0:1], scale=1.0),
         reads=[dkey], writes=[dkey])
    S.op("act", lambda e: e.activation(out=dst, in_=dst, func=AF.Exp, scale=-1.0), reads=[dkey], writes=[dkey])


def build_program(SEQ=4096, debug=False):
    NT = SEQ // 128
    NG = SEQ // 512
    nc = bass.Bass("TRN2", target_bir_lowering=False)

    def din(name, shape):
        return nc.dram_tensor(name, list(shape), F32, kind="ExternalInput").ap()

    x = din("x", [SEQ, 1024])
    p_in = din("p", [SEQ, 256])
    norm_mix_g = din("norm_mix_g", [1, 1024])
    w_in = din("w_in", [1024, INC])
    hg_lb = din("hg_lb_logits", [2, 512])
    hg_on = din("hg_onorm_g", [1, 128])
    fox_fb = din("fox_f_bias", [1, 8])
    fox_qg = din("fox_q_norm_g", [1, 64])
    fox_kg = din("fox_k_norm_g", [1, 64])
    w_ba = din("w_branch_a", [512, 1024])
    w_bb = din("w_branch_b", [512, 1024])
    w_out = din("w_out", [1024, 1024])
    norm_ffn_g = din("norm_ffn_g", [1, 1024])
    w_fg = din("w_ffn_gate", [1024, DFF])
    w_fu = din("w_ffn_up", [1024, DFF])
    w_fd = din("w_ffn_down", [DFF, 1024])
    norm_ple_g = din("norm_ple_g", [1, 1024])
    w_pg = din("w_ple_gate", [1024, 1024])
    w_pp = din("w_ple_proj", [256, 1024])
    out = nc.dram_tensor("out", [SEQ, 1024], F32, kind="ExternalOutput").ap()
    if debug:
        x1d = nc.dram_tensor("x1d", [SEQ, 1024], F32, kind="ExternalOutput").ap()
        dbg_yb = nc.dram_tensor("dbg_yb", [128, 4, SEQ], BF16, kind="ExternalOutput").ap()
        dbg_ya = nc.dram_tensor("dbg_ya", [SEQ // 512, 128, 4, 512], BF16, kind="ExternalOutput").ap()
    else:
        x1d = nc.dram_tensor("x1d", [SEQ, 1024], F32, kind="ExternalOutput").ap()

    with contextlib.ExitStack() as st0:
        A0 = Alloc(nc, st0, "c")
        C = NS()
        C.ident = A0.sb("ident", [128, 128], BF16)
        C.ones_bf = A0.sb("ones_bf", [128, 128], BF16)
        C.eps_t = A0.sb("eps_t", [128, 1], F32)
        C.one_t = A0.sb("one_t", [128, 1], F32)
        C.lbT = A0.sb("lbT", [128, 4], F32)
        C.omlT = A0.sb("omlT", [128, 4], F32)
        C.nomlT = A0.sb("nomlT", [128, 4], F32)
        C.gon = A0.sb("gon", [128, 1], F32)
        C.qg = A0.sb("qg", [128, 1], F32)
        C.kg = A0.sb("kg", [128, 1], F32)
        C.fbias = A0.sb("fbias", [128, 8], F32)
        st_yb = contextlib.ExitStack()
        A_yb = Alloc(nc, st_yb, "yb")
        ybT = A_yb.sb("ybT", [128, 4, SEQ], BF16)
        C.mask_bf = A_yb.sb("mask_bf", [128, 128], BF16)
        C.ones_f = A_yb.sb("ones_f", [128, 128], F32)
        C.U_f = A_yb.sb("U_f", [128, 128], F32)
        C.E127 = A_yb.sb("E127", [128, 128], F32)
        C.gmixB = A_yb.sb("gmixB", [128, 8, 128], BF16)
        C.bd_bf = A_yb.sb("bd_bf", [128, 128], BF16)

        with contextlib.ExitStack() as st:
            A = Alloc(nc, st, "p0")
            S = Sched(nc)
            tmpf = A.sb("tmpf", [128, 128], F32)
            gT = A.sb("gT", [128, 8], F32)
            l0 = A.sb("l0", [128, 4], F32)
            l1 = A.sb("l1", [128, 4], F32)
            S.op("pool", lambda e: e.memset(C.ones_f[:], 1.0), writes=["ones_f"])
            S.op("pool", lambda e: e.memset(C.eps_t[:], EPS), writes=["eps"])
            S.op("pool", lambda e: e.memset(C.one_t[:], 1.0), writes=["one"])
            S.op("pool", lambda e: e.memset(C.ones_bf[:], 1.0), writes=["ones_bf"])
            S.op("pool", lambda e: e.affine_select(out=tmpf[:], in_=C.ones_f[:], pattern=[[-1, 128]],
                                                   compare_op=ALU.is_equal, fill=0.0, base=0, channel_multiplier=1),
                 reads=["ones_f"], writes=["tmpf"])
            S.op("dve", lambda e: e.tensor_copy(out=C.ident[:], in_=tmpf[:]), reads=["tmpf"], writes=["ident"])
            S.op("pool", lambda e: e.affine_select(out=C.U_f[:], in_=C.ones_f[:], pattern=[[1, 128]],
                                                   compare_op=ALU.is_ge, fill=0.0, base=0, channel_multiplier=-1),
                 reads=["ones_f"], writes=["U"])
            S.op("dve", lambda e: e.tensor_copy(out=C.mask_bf[:], in_=C.U_f[:]), reads=["U"], writes=["mask"])
            S.op("pool", lambda e: e.affine_select(out=C.E127[:], in_=C.ones_f[:], pattern=[[0, 128]],
                                                   compare_op=ALU.is_equal, fill=0.0, base=-127, channel_multiplier=1),
                 reads=["ones_f"], writes=["E127"])
            S.op("sp", lambda e: e.dma_start(out=gT[:], in_=norm_mix_g.rearrange("o (k p) -> p (o k)", p=128),
                                             allow_slow_non_contiguous=True), writes=["gT"], dma="gT")
            S.op("dve", lambda e: e.tensor_copy(out=C.gmixB[:], in_=gT[:].unsqueeze(2).to_broadcast([128, 8, 128])),
                 reads=["gT"], writes=["gmixB"])
            S.op("sp", lambda e: e.dma_start(out=l0[:], in_=hg_lb[0:1, :].rearrange("o (h c) -> c (o h)", c=128),
                                             allow_slow_non_contiguous=True), writes=["l0"], dma="l0")
            S.op("sp", lambda e: e.dma_start(out=l1[:], in_=hg_lb[1:2, :].rearrange("o (h c) -> c (o h)", c=128),
                                             allow_slow_non_contiguous=True), writes=["l1"], dma="l1")
            S.op("sp", lambda e: e.dma_start(out=C.gon[:], in_=hg_on.rearrange("o v -> v o"),
                                             allow_slow_non_contiguous=True), writes=["gon"], dma="gon")
            for hh in range(2):
                S.op("sp", lambda e, hh=hh: e.dma_start(out=C.qg[hh * 64:(hh + 1) * 64, :], in_=fox_qg.rearrange("o v -> v o"),
                                                        allow_slow_non_contiguous=True), writes=[("qg", hh)], dma=f"qg{hh}")
                S.op("sp", lambda e, hh=hh: e.dma_start(out=C.kg[hh * 64:(hh + 1) * 64, :], in_=fox_kg.rearrange("o v -> v o"),
                                                        allow_slow_non_contiguous=True), writes=[("kg", hh)], dma=f"kg{hh}")
            S.op("pool", lambda e: e.memset(C.bd_bf[:], 0.0), writes=["bd"])
            S.op("pool", lambda e: e.memset(C.bd_bf[0:64, 0:64], 1.0), reads=["bd"], writes=["bd"])
            S.op("pool", lambda e: e.memset(C.bd_bf[64:128, 64:128], 1.0), reads=["bd"], writes=["bd"])
            S.op("sp", lambda e: e.dma_start(out=C.fbias[:], in_=fox_fb.partition_broadcast(128)),
                 writes=["fbias"], dma="fbias")
            S.op("dve", lambda e: e.tensor_scalar(out=C.qg[:], in0=C.qg[:], scalar1=0.125, scalar2=None, op0=ALU.mult),
                 reads=[("qg", 0), ("qg", 1)], writes=["qg"])
            S.op("dve", lambda e: e.tensor_tensor(out=l1[:], in0=l1[:], in1=l0[:], op=ALU.subtract),
                 reads=["l0", "l1"], writes=["l1"])
            S.op("act", lambda e: e.activation(out=l1[:], in_=l1[:], func=AF.Exp), reads=["l1"], writes=["l1"])
            S.op("dve", lambda e: e.tensor_scalar_add(out=l1[:], in0=l1[:], scalar1=1.0), reads=["l1"], writes=["l1"])
            S.op("dve", lambda e: e.reciprocal(out=C.lbT[:], in_=l1[:]), reads=["l1"], writes=["lbT"])
            S.op("dve", lambda e: e.tensor_scalar(out=C.omlT[:], in0=C.lbT[:], scalar1=-1.0, scalar2=1.0,
                                                  op0=ALU.mult, op1=ALU.add), reads=["lbT"], writes=["omlT"])
            S.op("dve", lambda e: e.tensor_scalar(out=C.nomlT[:], in0=C.lbT[:], scalar1=-1.0, scalar2=None,
                                                  op0=ALU.add), reads=["lbT"], writes=["nomlT"])
            S.emit()

        with contextlib.ExitStack() as st:
            A = Alloc(nc, st, "a1")
            S = Sched(nc)
            S.excl.update(["pT", "psV", "psQ", "psN", "psS0", "psS1", "psQ", "psO1"])
            w_fq = A.sb("w_fq", [128, 8, 512], BF16)
            w_fk = A.sb("w_fk", [128, 8, 512], BF16)
            w_fv = A.sb("w_fv", [128, 8, 512], BF16)
            w_ff = A.sb("w_ff", [128, 8, 8], BF16)
            KT = A.sb("KT", [65, 8, SEQ], BF16)
            Vc = A.sb("Vc", [128, NT, 8, 64], BF16)
            cP = A.sb("cP", [128, NT, 8], F32)
            biasG = A.sb("biasG", [128, NT, 8], F32)
            Aaug = A.sb("Aaug", [128, 4, 8], BF16)
            hT = A.sb("hT", [128, 8, 512], BF16)
            sqr = Ring([(A.sb(f"sq{i}", [128, 512], BF16), f"sq{i}") for i in range(2)])
            rsr = Ring([(A.sb(f"rs{i}", [128, 512], F32), f"rs{i}") for i in range(2)])
            rinv_sb = A.sb("rinv_sb", [64, 512], F32)
            zb = A.sb("zb", [128, 8], F32)
            cref = A.sb("cref", [128, 8], F32)
            xring = Ring([(A.sb(f"x{i}", [128, 1024], F32), f"x{i}") for i in range(3)])
            W = NS()
            W.ssq = Ring([(A.sb(f"ssq{i}", [128, 4], F32), f"ssq{i}") for i in range(2)])
            W.hb = Ring([(A.sb(f"hb{i}", [128, 1024], BF16), f"hb{i}") for i in range(2)])
            QTs = [(A.sb(f"QT{i}", [65, 512], BF16), f"QT{i}") for i in range(8)]
            PTr = Ring([(A.sb(f"PT{i}", [128, 512], BF16), f"PT{i}") for i in range(3)])
            pTt = A.ps("pT", [128, 8, 128], BF16)
            W.pT = Ring([(pTt, "pT")])
            psV = A.ps("psV", [128, 512], F32)
            psQ = A.ps("psQ", [128, 512], F32)
            psN = A.ps("psN", [128, 512], F32)
            psSr = Ring([(A.ps(f"psS{i}", [128, 512], F32), f"psS{i}") for i in range(2)])
            psOr = Ring([(A.ps(f"psO{i}", [128, 512], F32), f"psO{i}") for i in range(2)])
            print("A1 sbuf bytes/partition", A.bytes + A0.bytes + A_yb.bytes)

            S.op("pool", lambda e: e.memset(KT[64:65, :, :], 1.0), writes=["KTones"])
            load_w(S, w_fv, "w_fv", w_in, 0, 8, FV, 512)
            for k in range(8):
                S.op("pool", lambda e, k=k: e.dma_start(out=w_ff[:, k, :], in_=w_in[k * 128:(k + 1) * 128, FFC:FFC + 8]),
                     writes=["w_ff"], dma="w_ff", nodep=True)
            load_w(S, w_fq, "w_fq", w_in, 0, 8, FQ, 512)
            load_w(S, w_fk, "w_fk", w_in, 0, 8, FK, 512)

            for G in range(NG):
                for i in range(4):
                    ti = 4 * G + i
                    xt, xk = xring.next()
                    S.op("sp", lambda e, xt=xt, ti=ti: e.dma_start(out=xt[:], in_=x[ti * 128:(ti + 1) * 128, :]),
                         writes=[xk], dma=xk)
                    norm_T(S, C, xt[:], xk, C.gmixB, hT[:, :, i * 128:(i + 1) * 128], ("hT", i), W, scale_eng="dve")
                    for k in range(8):
                        S.op("pe", lambda e, k=k, i=i: e.matmul(psV[:], lhsT=hT[:, k, i * 128:(i + 1) * 128], rhs=w_fv[:, k, :],
                                                               start=(k == 0), stop=(k == 7)),
                             reads=[("hT", i), "w_fv"], writes=["psV"])
                    S.op("dve", lambda e, ti=ti: e.tensor_copy(out=Vc[:, ti, :, :],
                                                               in_=psV[:].rearrange("p (h d) -> p h d", d=64)),
                         reads=["psV"], writes=[("Vc", ti)])
                    psF = psQ[:, 0:8]
                    psC = psQ[:, 8:16]
                    for k in range(8):
                        S.op("pe", lambda e, k=k, i=i, psF=psF: e.matmul(psF, lhsT=hT[:, k, i * 128:(i + 1) * 128], rhs=w_ff[:, k, :],
                                                                        start=(k == 0), stop=(k == 7)),
                             reads=[("hT", i), "w_ff"], writes=["psQ"])
                    S.op("dve", lambda e, psF=psF: e.tensor_tensor(out=zb[:], in0=psF, in1=C.fbias[:], op=ALU.add),
                         reads=["psQ"], writes=["zb"])
                    S.op("act", lambda e: e.activation(out=zb[:], in_=zb[:], func=AF.Exp, scale=-1.0),
                         reads=["zb"], writes=["zb"])
                    S.op("act", lambda e: e.activation(out=zb[:], in_=zb[:], func=AF.Ln, bias=C.one_t[:, 0:1], scale=1.0),
                         reads=["zb"], writes=["zb"])
                    S.op("pe", lambda e, ti=ti, psC=psC: e.matmul(psC, lhsT=C.U_f[:], rhs=zb[:], start=True, stop=(ti == 0)),
                         reads=["zb"], writes=["psQ"])
                    if ti > 0:
                        S.op("pe", lambda e, ti=ti, psC=psC: e.matmul(psC, lhsT=C.E127[:], rhs=cP[:, ti - 1, :], start=False, stop=True),
                             reads=[("cP", ti - 1)], writes=["psQ"])
                    S.op("dve", lambda e, ti=ti, psC=psC: e.tensor_copy(out=cP[:, ti, :], in_=psC),
                         reads=["psQ"], writes=[("cP", ti)])
                psR = psQ[:, 16:24]
                S.op("pe", lambda e, G=G, psR=psR: e.matmul(psR, lhsT=C.E127[:], rhs=cP[:, 4 * G + 1, :], start=True, stop=True),
                     reads=[("cP", 4 * G + 1)], writes=["psQ"])
                S.op("dve", lambda e, psR=psR: e.tensor_copy(out=cref[:], in_=psR), reads=["psQ"], writes=["cref"])
                for i in range(4):
                    S.op("dve", lambda e, i=i, G=G: e.tensor_tensor(out=Aaug[:, i, :], in0=cref[:], in1=cP[:, 4 * G + i, :],
                                                                   op=ALU.subtract),
                         reads=["cref", ("cP", 4 * G + i)], writes=[("Aaug", i)])
                nj = 4 * G + 4
                S.op("dve", lambda e, nj=nj: e.tensor_tensor(out=biasG[:, 0:nj, :], in0=cP[:, 0:nj, :],
                                                             in1=cref[:].unsqueeze(1).to_broadcast([128, nj, 8]),
                                                             op=ALU.subtract),
                     reads=["cref"] + [("cP", t) for t in range(nj)], writes=["biasG"])
                for h in range(8):
                    QT, qk = QTs[h]
                    pa, pak = (psQ, "psQ") if h % 2 == 0 else (psN, "psN")
                    for i in range(4):
                        S.op("pe", lambda e, i=i, h=h, pa=pa: e.matmul(pa[64:65, i * 128:(i + 1) * 128], lhsT=Aaug[:, i, h:h + 1],
                                                                      rhs=C.ident[:], start=(i == 0), stop=(i == 3)),
                             reads=[("Aaug", i)], writes=[pak], c=110.0)
                    S.op("dve", lambda e, QT=QT, pa=pa: e.tensor_copy(out=QT[64:65, :], in_=pa[64:65, :]), reads=[pak], writes=[(qk, "r")], c=600.0)
                for p in range(4):
                    he, ho = 2 * p, 2 * p + 1
                    for k in range(8):
                        S.op("pe", lambda e, k=k, p=p: e.matmul(psQ[:], lhsT=w_fq[:, k, p * 128:(p + 1) * 128], rhs=hT[:, k, :],
                                                               start=(k == 0), stop=(k == 7)),
                             reads=[("hT", 0), ("hT", 1), ("hT", 2), ("hT", 3), "w_fq"], writes=["psQ"])
                    for k in range(8):
                        S.op("pe", lambda e, k=k, p=p: e.matmul(psV[:], lhsT=w_fk[:, k, p * 128:(p + 1) * 128], rhs=hT[:, k, :],
                                                               start=(k == 0), stop=(k == 7)),
                             reads=[("hT", 0), ("hT", 1), ("hT", 2), ("hT", 3), "w_fk"], writes=["psV"])
                    for (src, skey, gain, dsts) in (
                            (psQ, "psQ", C.qg, [(QTs[he][0][0:64, :], (QTs[he][1], "q")), (QTs[ho][0][0:64, :], (QTs[ho][1], "q"))]),
                            (psV, "psV", C.kg, [(KT[0:64, he, G * 512:(G + 1) * 512], ("KT", he, G)),
                                                (KT[0:64, ho, G * 512:(G + 1) * 512], ("KT", ho, G))])):
                        sq, sqk = sqr.next()
                        rs, rsk = rsr.next()
                        S.op("act", lambda e, src=src, sq=sq: e.activation(out=sq[:], in_=src[:], func=AF.Square),
                             reads=[skey], writes=[sqk], c=650.0)
                        S.op("pe", lambda e, sq=sq: e.matmul(psN[:], lhsT=C.bd_bf[:], rhs=sq[:], start=True, stop=True),
                             reads=[sqk], writes=["psN"])
                        S.op("act", lambda e, rs=rs: e.activation(out=rs[:], in_=psN[:], func=AF.Ln, scale=1.0 / 64.0,
                                                           bias=C.eps_t[:, 0:1]), reads=["psN"], writes=[rsk], c=650.0)
                        S.op("act", lambda e, rs=rs: e.activation(out=rs[:], in_=rs[:], func=AF.Exp, scale=-0.5),
                             reads=[rsk], writes=[rsk], c=650.0)
                        for hh, (dst, dkey) in enumerate(dsts):
                            ps_ = slice(hh * 64, (hh + 1) * 64)
                            S.op("dve", lambda e, src=src, gain=gain, dst=dst, rs=rs, ps_=ps_: e.scalar_tensor_tensor(
                                out=dst, in0=src[ps_, :], scalar=gain[ps_, 0:1], in1=rs[ps_, :], op0=ALU.mult, op1=ALU.mult),
                                reads=[skey, rsk], writes=[dkey], c=620.0)
                for h in range(8):
                    QT, qk0 = QTs[h]
                    qk = (qk0, "q")
                    qkr = (qk0, "r")
                    psO, ok = psOr.next()
                    pend = None

                    def pv(j, PT, ptk, psO=psO, ok=ok, h=h, G=G, nj=nj):
                        c0 = max(0, j - 4 * G) * 128
                        S.op("pe", lambda e: e.matmul(psO[0:64, c0:512], lhsT=Vc[:, j, h, :], rhs=PT[:, c0:512],
                                                      start=(j == 0), stop=(j == nj - 1)),
                             reads=[ptk, ("Vc", j)], writes=[ok])
                        S.op("pe", lambda e: e.matmul(psO[64:128, c0:512], lhsT=C.ones_bf[:, 0:64], rhs=PT[:, c0:512],
                                                      start=(j == 0), stop=(j == nj - 1)),
                             reads=[ptk], writes=[ok])

                    for j in range(nj):
                        r = max(0, j - 4 * G)
                        c0 = r * 128
                        psS, sk_ = psSr.next()
                        PT, ptk = PTr.next()
                        S.op("pe", lambda e, j=j, c0=c0, psS=psS, h=h, QT=QT: e.matmul(
                            psS[:, c0:512], lhsT=KT[0:65, h, j * 128:(j + 1) * 128], rhs=QT[0:65, c0:512], start=True, stop=True),
                            reads=[("KT", h, j // 4), "KTones", qk, qkr], writes=[sk_])
                        S.op("act", lambda e, j=j, c0=c0, psS=psS, PT=PT, h=h: e.activation(
                            out=PT[:, c0:512], in_=psS[:, c0:512], func=AF.Exp, bias=biasG[:, j, h:h + 1], scale=1.0),
                            reads=[sk_, "biasG"], writes=[ptk])
                        if j >= 4 * G:
                            S.op("pool", lambda e, c0=c0, PT=PT: e.tensor_tensor(out=PT[:, c0:c0 + 128], in0=PT[:, c0:c0 + 128],
                                                                                in1=C.mask_bf[:], op=ALU.mult),
                                 reads=[ptk], writes=[ptk])
                        if pend is not None:
                            pv(*pend)
                        pend = (j, PT, ptk)
                    pv(*pend)
                    S.op("dve", lambda e, psO=psO: e.reciprocal(out=rinv_sb[:], in_=psO[64:128, :]),
                         reads=[ok], writes=["rinv_sb"])
                    S.op("dve", lambda e, psO=psO, h=h, G=G: e.tensor_tensor(
                        out=ybT[(h % 2) * 64:(h % 2) * 64 + 64, h // 2, G * 512:(G + 1) * 512],
                        in0=psO[0:64, :], in1=rinv_sb[:], op=ALU.mult),
                        reads=[ok, "rinv_sb"], writes=[("ybT", G, h)])
            if debug:
                S.op("sp", lambda e: e.dma_start(out=dbg_yb, in_=ybT[:]), reads=[("ybT", G_, h_) for G_ in range(NG) for h_ in range(8)], dma="dbgyb")
            S.emit()

        with contextlib.ExitStack() as st:
            A = Alloc(nc, st, "a2")
            S = Sched(nc)
            S.excl.update(["pT"] + [f"b{i}" for i in range(7)])
            w_hq = A.sb("w_hq", [128, 8, 512], BF16)
            w_hf = A.sb("w_hf", [128, 8, 512], BF16)
            w_hi = A.sb("w_hi", [128, 8, 512], BF16)
            w_hg = A.sb("w_hg", [128, 8, 512], BF16)
            w_ga = A.sb("w_ga", [128, 8, 1024], BF16)
            w_gb = A.sb("w_gb", [128, 8, 1024], BF16)
            wA = A.sb("wA", [128, 4, 1024], BF16)
            wB = A.sb("wB", [128, 4, 1024], BF16)
            wO = A.sb("wO", [128, 8, 1024], BF16)
            hT = A.sb("hT", [128, 8, 512], BF16)
            v_tok = A.sb("v_tok", [128, 4, 512], BF16)
            tmp = [A.sb(f"t{i}", [128, 512], F32) for i in range(5)]
            qT = A.sb("qT", [128, 512], BF16)
            ktT = A.sb("ktT", [128, 512], BF16)
            khT = A.sb("khT", [128, 512], BF16)
            khat = A.sb("khat", [128, 4, 128], BF16)
            sg = A.sb("sg", [128, 4, 512], BF16)
            yaT = A.sb("yaT", [128, 4, 512], BF16)
            mT = A.sb("mT", [128, 8, 512], BF16)
            Sf = A.sb("Sf", [128, 4, 128], F32)
            Sb = A.sb("Sb", [128, 4, 128], BF16)
            PsT = A.sb("PsT", [128, 128], BF16)
            eL = A.sb("eL", [128, 4], F32)
            sq = A.sb("sq2", [128, 512], BF16)
            xring = Ring([(A.sb(f"x{i}", [128, 1024], F32), f"x{i}") for i in range(5)])
            W = NS()
            W.ssq = Ring([(A.sb(f"ssq{i}", [128, 4], F32), f"ssq{i}") for i in range(2)])
            W.hb = Ring([(A.sb("hb0", [128, 1024], BF16), "hb0")])
            pTt = A.ps("pT", [128, 8, 128], BF16)
            W.pT = Ring([(pTt, "pT")])
            bk = [A.ps(f"b{i}", [128, 512], F32) for i in range(7)]
            print("A2 sbuf bytes/partition", A.bytes + A0.bytes + A_yb.bytes)

            S.op("pool", lambda e: e.memset(Sf[:], 0.0), writes=[("Sf", h) for h in range(4)])
            S.op("pool", lambda e: e.memset(Sb[:], 0.0), writes=[("Sb", h) for h in range(4)])
            load_w(S, w_hi, "w_hi", w_in, 0, 8, HI, 512)
            load_w(S, w_hq, "w_hq", w_in, 0, 8, HQ, 512)
            load_w(S, w_hf, "w_hf", w_in, 0, 8, HF, 512)
            load_w(S, w_hg, "w_hg", w_in, 0, 8, HGC, 512)
            load_w(S, w_ga, "w_ga", w_in, 0, 8, GA, 1024)
            load_w(S, w_gb, "w_gb", w_in, 0, 8, GB, 1024)
            load_w(S, wA, "wA", w_ba, 0, 4, 0, 1024)
            load_w(S, wB, "wB", w_bb, 0, 4, 0, 1024)
            load_w(S, wO, "wO", w_out, 0, 8, 0, 1024)

            for G in range(NG):
                xts = []
                for i in range(4):
                    ti = 4 * G + i
                    xt, xk = xring.next()
                    xts.append((xt, xk))
                    S.op("sp", lambda e, xt=xt, ti=ti: e.dma_start(out=xt[:], in_=x[ti * 128:(ti + 1) * 128, :]),
                         writes=[xk], dma=xk)
                    norm_T(S, C, xt[:], xk, C.gmixB, hT[:, :, i * 128:(i + 1) * 128], ("hT", i), W, scale_eng="dve")
                    for k in range(8):
                        S.op("pe", lambda e, k=k, i=i: e.matmul(bk[0][:], lhsT=hT[:, k, i * 128:(i + 1) * 128], rhs=w_hi[:, k, :],
                                                               start=(k == 0), stop=(k == 7)),
                             reads=[("hT", i), "w_hi"], writes=["b0"])
                    S.op("dve", lambda e, i=i: e.tensor_copy(out=v_tok[:, i, :], in_=bk[0][:]), reads=["b0"], writes=[("v_tok", i)], c=600.0)
                hTall = [("hT", 0), ("hT", 1), ("hT", 2), ("hT", 3)]
                for h in range(4):
                    pZq, pZf, pZg, pSm, pOh, pN = bk[1], bk[2], bk[3], bk[4], bk[5], bk[6]
                    for (ps_, w_, wk_, key_) in ((pZf, w_hf, "w_hf", "b2"), (pZq, w_hq, "w_hq", "b1"), (pZg, w_hg, "w_hg", "b3")):
                        for k in range(8):
                            S.op("pe", lambda e, k=k, h=h, ps_=ps_, w_=w_: e.matmul(
                                ps_[:], lhsT=w_[:, k, h * 128:(h + 1) * 128], rhs=hT[:, k, :], start=(k == 0), stop=(k == 7)),
                                reads=hTall + [wk_], writes=[key_])
                    t0, t1, t2, t3, t4 = tmp
                    sigmoid_act(S, pZf[:], "b2", t0[:], "t0", C)
                    S.op("act", lambda e, h=h: e.activation(out=t1[:], in_=t0[:], func=AF.Ln, scale=C.omlT[:, h:h + 1],
                                                            bias=C.lbT[:, h:h + 1]), reads=["t0"], writes=["t1"])
                    S.op("dve", lambda e, h=h: e.tensor_scalar(out=t2[:], in0=t0[:], scalar1=C.nomlT[:, h:h + 1],
                                                               scalar2=C.omlT[:, h:h + 1], op0=ALU.mult, op1=ALU.add),
                         reads=["t0"], writes=["t2"])
                    for i in range(4):
                        S.op("dve", lambda e, i=i: e.tensor_tensor_scan(out=t3[:, i * 128:(i + 1) * 128], data0=C.ones_f[:],
                                                                       data1=t1[:, i * 128:(i + 1) * 128], initial=0.0,
                                                                       op0=ALU.mult, op1=ALU.add),
                             reads=["t1"], writes=["t3"])
                    S.op("act", lambda e: e.activation(out=eL[:], in_=t3[:].rearrange("p (i c) -> p i c", c=128)[:, :, 127],
                                                       func=AF.Exp), reads=["t3"], writes=["eL"])
                    S.op("act", lambda e: e.activation(out=t1[:], in_=t3[:], func=AF.Exp), reads=["t3"], writes=["t1"])
                    S.op("act", lambda e: e.activation(out=t0[:], in_=t3[:], func=AF.Exp, scale=-1.0),
                         reads=["t3"], writes=["t0"])
                    sigmoid_act(S, pZq[:], "b1", t4[:], "t4", C)
                    S.op("dve", lambda e: e.tensor_tensor(out=t4[:], in0=pZq[:], in1=t4[:], op=ALU.mult),
                         reads=["b1", "t4"], writes=["t4"])
                    S.op("dve", lambda e: e.tensor_tensor(out=qT[:], in0=t4[:], in1=t1[:], op=ALU.mult),
                         reads=["t4", "t1"], writes=["qT"])
                    S.op("dve", lambda e: e.tensor_tensor(out=ktT[:], in0=t2[:], in1=t0[:], op=ALU.mult),
                         reads=["t2", "t0"], writes=["ktT"])
                    for i in range(4):
                        S.op("dve", lambda e, i=i: e.tensor_scalar(out=khT[:, i * 128:(i + 1) * 128], in0=ktT[:, i * 128:(i + 1) * 128],
                                                                  scalar1=eL[:, i:i + 1], scalar2=None, op0=ALU.mult),
                             reads=["ktT", "eL"], writes=["khT"], c=250.0)
                    sigmoid_act(S, pZg[:], "b3", t4[:], "t4", C)
                    S.op("dve", lambda e, h=h: e.tensor_tensor(out=sg[:, h, :], in0=pZg[:], in1=t4[:], op=ALU.mult),
                         reads=["b3", "t4"], writes=[("sg", h)])
                    for i in range(4):
                        S.op("pe", lambda e, i=i: e.transpose(out=pTt[:, i, :], in_=khT[:, i * 128:(i + 1) * 128], identity=C.ident[:]),
                             reads=["khT"], writes=["pT"])
                    S.op("dve", lambda e: e.tensor_copy(out=khat[:], in_=pTt[:, 0:4, :]), reads=["pT"], writes=["khat"], c=450.0)
                    for i in range(4):
                        cs = slice(i * 128, (i + 1) * 128)
                        pSc = pSm[:, 0:128]
                        pSt = pN[:, 0:128]
                        S.op("pe", lambda e, cs=cs, pSc=pSc: e.matmul(pSc, lhsT=ktT[:, cs], rhs=qT[:, cs], start=True, stop=True),
                             reads=["ktT", "qT"], writes=["b4"])
                        S.op("dve", lambda e, pSc=pSc: e.tensor_tensor(out=PsT[:], in0=pSc, in1=C.mask_bf[:], op=ALU.mult),
                             reads=["b4"], writes=["PsT"])
                        S.op("pe", lambda e, cs=cs, i=i, h=h: e.matmul(pOh[:, cs], lhsT=v_tok[:, i, h * 128:(h + 1) * 128], rhs=PsT[:],
                                                                      start=True, stop=False),
                             reads=[("v_tok", i), "PsT"], writes=["b5"])
                        S.op("pe", lambda e, cs=cs, h=h: e.matmul(pOh[:, cs], lhsT=Sb[:, h, :], rhs=qT[:, cs], start=False, stop=True),
                             reads=[("Sb", h), "qT"], writes=["b5"])
                        S.op("pe", lambda e, i=i, h=h, pSt=pSt: e.matmul(pSt, lhsT=khat[:, i, :], rhs=v_tok[:, i, h * 128:(h + 1) * 128],
                                                                        start=True, stop=True),
                             reads=["khat", ("v_tok", i)], writes=["b6"])
                        S.op("dve", lambda e, i=i, h=h, pSt=pSt: e.scalar_tensor_tensor(
                            out=Sf[:, h, :], in0=Sf[:, h, :], scalar=eL[:, i:i + 1], in1=pSt, op0=ALU.mult, op1=ALU.add),
                            reads=[("Sf", h), "eL", "b6"], writes=[("Sf", h)])
                        S.op("dve", lambda e, h=h: e.tensor_copy(out=Sb[:, h, :], in_=Sf[:, h, :]), reads=[("Sf", h)], writes=[("Sb", h)], c=250.0)
                    S.op("act", lambda e: e.activation(out=sq[:], in_=pOh[:], func=AF.Square), reads=["b5"], writes=["sq"])
                    S.op("pe", lambda e: e.matmul(pN[:], lhsT=C.ones_bf[:], rhs=sq[:], start=True, stop=True),
                         reads=["sq"], writes=["b6"])
                    S.op("act", lambda e: e.activation(out=t2[:], in_=pN[:], func=AF.Ln, scale=1.0 / 128.0, bias=C.eps_t[:, 0:1]),
                         reads=["b6"], writes=["t2"])
                    S.op("act", lambda e: e.activation(out=t2[:], in_=t2[:], func=AF.Exp, scale=-0.5), reads=["t2"], writes=["t2"])
                    S.op("dve", lambda e: e.scalar_tensor_tensor(out=t2[:], in0=pOh[:], scalar=C.gon[:, 0:1], in1=t2[:],
                                                                 op0=ALU.mult, op1=ALU.mult), reads=["b5", "t2"], writes=["t2"])
                    S.op("dve", lambda e, h=h: e.tensor_tensor(out=yaT[:, h, :], in0=t2[:], in1=sg[:, h, :], op=ALU.mult),
                         reads=["t2", ("sg", h)], writes=[("yaT", h)])
                if debug:
                    S.op("sp", lambda e, G=G: e.dma_start(out=dbg_ya[G], in_=yaT[:]), reads=[("yaT", h) for h in range(4)], dma="dbgya")
                t0, t1 = tmp[0], tmp[1]
                for dc in range(8):
                    ds_ = slice(dc * 128, (dc + 1) * 128)
                    pGa, pGb, pA, pB = bk[1], bk[2], bk[3], bk[4]
                    for k in range(8):
                        S.op("pe", lambda e, k=k, ds_=ds_: e.matmul(pGa[:], lhsT=w_ga[:, k, ds_], rhs=hT[:, k, :], start=(k == 0), stop=(k == 7)),
                             reads=hTall + ["w_ga"], writes=["b1"])
                    for k in range(8):
                        S.op("pe", lambda e, k=k, ds_=ds_: e.matmul(pGb[:], lhsT=w_gb[:, k, ds_], rhs=hT[:, k, :], start=(k == 0), stop=(k == 7)),
                             reads=hTall + ["w_gb"], writes=["b2"])
                    for k in range(4):
                        S.op("pe", lambda e, k=k, ds_=ds_: e.matmul(pA[:], lhsT=wA[:, k, ds_], rhs=yaT[:, k, :], start=(k == 0), stop=(k == 3)),
                             reads=[("yaT", k), "wA"], writes=["b3"])
                    for k in range(4):
                        S.op("pe", lambda e, k=k, ds_=ds_, G=G: e.matmul(pB[:], lhsT=wB[:, k, ds_], rhs=ybT[:, k, G * 512:(G + 1) * 512],
                                                                        start=(k == 0), stop=(k == 3)),
                             reads=["wB"], writes=["b4"])
                    sigmoid_act(S, pGa[:], "b1", t0[:], "t0", C)
                    sigmoid_act(S, pGb[:], "b2", t1[:], "t1", C)
                    S.op("dve", lambda e: e.tensor_tensor(out=t0[:], in0=pA[:], in1=t0[:], op=ALU.mult), reads=["b3", "t0"], writes=["t0"])
                    S.op("dve", lambda e: e.tensor_tensor(out=t1[:], in0=pB[:], in1=t1[:], op=ALU.mult), reads=["b4", "t1"], writes=["t1"])
                    S.op("dve", lambda e, dc=dc: e.tensor_tensor(out=mT[:, dc, :], in0=t0[:], in1=t1[:], op=ALU.add),
                         reads=["t0", "t1"], writes=[("mT", dc)])
                for i in range(4):
                    ti = 4 * G + i
                    xt, xk = xts[i]
                    for half in range(2):
                        pX = bk[5 + half]
                        for k in range(8):
                            S.op("pe", lambda e, k=k, i=i, half=half, pX=pX: e.matmul(
                                pX[:], lhsT=mT[:, k, i * 128:(i + 1) * 128], rhs=wO[:, k, half * 512:(half + 1) * 512],
                                start=(k == 0), stop=(k == 7)),
                                reads=[("mT", k), "wO"], writes=[f"b{5 + half}"])
                        S.op("dve", lambda e, xt=xt, half=half, pX=pX: e.tensor_tensor(
                            out=xt[:, half * 512:(half + 1) * 512], in0=pX[:], in1=xt[:, half * 512:(half + 1) * 512], op=ALU.add),
                            reads=[f"b{5 + half}", xk], writes=[xk])
                    S.op("sp", lambda e, xt=xt, ti=ti: e.dma_start(out=x1d[ti * 128:(ti + 1) * 128, :], in_=xt[:]),
                         reads=[xk], dma="st_" + xk)
            S.emit()

        st_yb.close()
        TB = 4
        NB = TB * 128
        with contextlib.ExitStack() as st:
            A = Alloc(nc, st, "b")
            S = Sched(nc)
            S.excl.update(["pT"] + [f"b{i}" for i in range(7)])
            wG = A.sb("wG", [128, 8, DFF], BF16)
            wU = A.sb("wU", [128, 8, DFF], BF16)
            wD = A.sb("wD", [128, NFF, 1024], BF16)
            wPG = A.sb("wPG", [128, 8, 1024], BF16)
            wPP = A.sb("wPP", [128, 2, 1024], BF16)
            gffnB = A.sb("gffnB", [128, 8, 128], BF16)
            gpleB = A.sb("gpleB", [128, 8, 128], BF16)
            gT2 = A.sb("gT2", [128, 8], F32)
            gT3 = A.sb("gT3", [128, 8], F32)
            h2T = A.sb("h2T", [128, 8, NB], BF16)
            actT = A.sb("actT", [128, NFF, NB], BF16)
            tr = Ring([(A.sb(f"t{i}", [128, 512], F32), f"t{i}") for i in range(2)])
            pbr = Ring([(A.sb(f"pb{i}", [128, 256], BF16), f"pb{i}") for i in range(2)])
            ppT = A.sb("ppT", [128, 2, 128], BF16)
            xaring = Ring([(A.sb("xa0", [128, 1024], F32), "xa0")])
            xring = Ring([(A.sb(f"x{i}", [128, 1024], F32), f"x{i}") for i in range(2)])
            h3T = h2T[:, :, 384:512]
            W = NS()
            W.ssq = Ring([(A.sb(f"ssq{i}", [128, 4], F32), f"ssq{i}") for i in range(2)])
            W.hb = Ring([(A.sb("hb0", [128, 1024], BF16), "hb0")])
            pTt = A.ps("pT", [128, 8, 128], BF16)
            W.pT = Ring([(pTt, "pT")])
            bk = [A.ps(f"b{i}", [128, 512], F32) for i in range(7)]
            print("B sbuf bytes/partition", A.bytes + A0.bytes)

            S.op("sp", lambda e: e.dma_start(out=gT2[:], in_=norm_ffn_g.rearrange("o (k p) -> p (o k)", p=128),
                                             allow_slow_non_contiguous=True), writes=["gT2"], dma="gT2")
            S.op("sp", lambda e: e.dma_start(out=gT3[:], in_=norm_ple_g.rearrange("o (k p) -> p (o k)", p=128),
                                             allow_slow_non_contiguous=True), writes=["gT3"], dma="gT3")
            S.op("dve", lambda e: e.tensor_copy(out=gffnB[:], in_=gT2[:].unsqueeze(2).to_broadcast([128, 8, 128])),
                 reads=["gT2"], writes=["gffnB"])
            S.op("dve", lambda e: e.tensor_copy(out=gpleB[:], in_=gT3[:].unsqueeze(2).to_broadcast([128, 8, 128])),
                 reads=["gT3"], writes=["gpleB"])
            for blk in range(4):
                for (dst_, key_, src_) in ((wG, "wG", w_fg), (wU, "wU", w_fu)):
                    for k in range(8):
                        S.op("pool", lambda e, k=k, blk=blk, dst_=dst_, src_=src_: e.dma_start(
                            out=dst_[:, k, blk * 704:(blk + 1) * 704], in_=src_[k * 128:(k + 1) * 128, blk * 704:(blk + 1) * 704]),
                            writes=[(key_, blk)], dma=f"{key_}{blk}", nodep=True)
            load_w(S, wD, "wD", w_fd, 0, NFF, 0, 1024)
            load_w(S, wPG, "wPG", w_pg, 0, 8, 0, 1024)
            load_w(S, wPP, "wPP", w_pp, 0, 2, 0, 1024)

            for g in range(NT // TB):
                for i in range(TB):
                    ti = g * TB + i
                    xt, xk = xaring.next()
                    S.op("sp", lambda e, xt=xt, ti=ti: e.dma_start(out=xt[:], in_=x1d[ti * 128:(ti + 1) * 128, :]),
                         writes=[xk], dma=xk)
                    norm_T(S, C, xt[:], xk, gffnB, h2T[:, :, i * 128:(i + 1) * 128], ("h2T", i), W)
                h2all = [("h2T", i) for i in range(TB)]
                for f in range(NFF):
                    fs = slice(f * 128, (f + 1) * 128)
                    pG = bk[0 + (f % 2)]
                    pU = bk[2 + (f % 2)]
                    kG = f"b{0 + (f % 2)}"
                    kU = f"b{2 + (f % 2)}"
                    for k in range(8):
                        S.op("pe", lambda e, k=k, fs=fs, pG=pG: e.matmul(pG[:, 0:NB], lhsT=wG[:, k, fs], rhs=h2T[:, k, :],
                                                                        start=(k == 0), stop=(k == 7)),
                             reads=h2all + [("wG", b_) for b_ in sorted({(f * 128) // 704, (f * 128 + 127) // 704})], writes=[kG])
                    for k in range(8):
                        S.op("pe", lambda e, k=k, fs=fs, pU=pU: e.matmul(pU[:, 0:NB], lhsT=wU[:, k, fs], rhs=h2T[:, k, :],
                                                                        start=(k == 0), stop=(k == 7)),
                             reads=h2all + [("wU", b_) for b_ in sorted({(f * 128) // 704, (f * 128 + 127) // 704})], writes=[kU])
                    t, tk = tr.next()
                    sigmoid_act(S, pG[:, 0:NB], kG, t[:, 0:NB], tk, C)
                    S.op("dve", lambda e, t=t, pG=pG: e.tensor_tensor(out=t[:, 0:NB], in0=pG[:, 0:NB], in1=t[:, 0:NB], op=ALU.mult),
                         reads=[kG, tk], writes=[tk])
                    S.op("dve", lambda e, t=t, pU=pU, f=f: e.tensor_tensor(out=actT[:, f, :], in0=pU[:, 0:NB], in1=t[:, 0:NB], op=ALU.mult),
                         reads=[kU, tk], writes=[("actT", f)])
                for i in range(TB):
                    ti = g * TB + i
                    xt, xk = xring.next()
                    S.op("sp", lambda e, xt=xt, ti=ti: e.dma_start(out=xt[:], in_=x1d[ti * 128:(ti + 1) * 128, :]),
                         writes=[xk], dma=xk)
                    for half in range(2):
                        pD = bk[4 + half]
                        kD = f"b{4 + half}"
                        for f in range(NFF):
                            S.op("pe", lambda e, f=f, i=i, half=half, pD=pD: e.matmul(
                                pD[:], lhsT=actT[:, f, i * 128:(i + 1) * 128], rhs=wD[:, f, half * 512:(half + 1) * 512],
                                start=(f == 0), stop=(f == NFF - 1)),
                                reads=[("actT", f), "wD"], writes=[kD])
                        S.op("dve", lambda e, xt=xt, half=half, pD=pD: e.tensor_tensor(
                            out=xt[:, half * 512:(half + 1) * 512], in0=pD[:], in1=xt[:, half * 512:(half + 1) * 512], op=ALU.add),
                            reads=[kD, xk], writes=[xk])
                    norm_T(S, C, xt[:], xk, gpleB, h3T, ("h2T", 3), W)
                    pb, pbk = pbr.next()
                    S.op("pool", lambda e, pb=pb, ti=ti: e.dma_start(out=pb[:], in_=p_in[ti * 128:(ti + 1) * 128, :]),
                         writes=[pbk], dma=pbk)
                    for c in range(2):
                        S.op("pe", lambda e, c=c, pb=pb: e.transpose(out=pTt[:, c, :], in_=pb[:, c * 128:(c + 1) * 128], identity=C.ident[:]),
                             reads=[pbk], writes=["pT"])
                    S.op("act", lambda e: e.copy(out=ppT[:], in_=pTt[:, 0:2, :]), reads=["pT"], writes=["ppT"])
                    for half in range(2):
                        hs = slice(half * 512, (half + 1) * 512)
                        pPG = bk[4 + half]
                        kPG = f"b{4 + half}"
                        pPP = bk[6]
                        for k in range(8):
                            S.op("pe", lambda e, k=k, hs=hs, pPG=pPG: e.matmul(pPG[:], lhsT=h3T[:, k, :], rhs=wPG[:, k, hs],
                                                                              start=(k == 0), stop=(k == 7)),
                                 reads=[("h2T", 3), "wPG"], writes=[kPG])
                        for k in range(2):
                            S.op("pe", lambda e, k=k, hs=hs: e.matmul(pPP[:], lhsT=ppT[:, k, :], rhs=wPP[:, k, hs],
                                                                     start=(k == 0), stop=(k == 1)),
                                 reads=["ppT", "wPP"], writes=["b6"])
                        t, tk = tr.next()
                        sigmoid_act(S, pPG[:], kPG, t[:], tk, C)
                        S.op("dve", lambda e, t=t: e.tensor_tensor(out=t[:], in0=pPP[:], in1=t[:], op=ALU.mult),
                             reads=["b6", tk], writes=[tk])
                        S.op("dve", lambda e, t=t, xt=xt, hs=hs: e.tensor_tensor(out=xt[:, hs], in0=xt[:, hs], in1=t[:], op=ALU.add),
                             reads=[tk, xk], writes=[xk])
                    S.op("sp", lambda e, xt=xt, ti=ti: e.dma_start(out=out[ti * 128:(ti + 1) * 128, :], in_=xt[:]),
                         reads=[xk], dma="ost_" + xk)
            S.emit()
    return nc


_IN_NAMES = ["x", "p", "norm_mix_g", "w_in", "hg_lb_logits", "hg_onorm_g", "fox_f_bias", "fox_q_norm_g",
             "fox_k_norm_g", "w_branch_a", "w_branch_b", "w_out", "norm_ffn_g", "w_ffn_gate", "w_ffn_up",
             "w_ffn_down", "norm_ple_g", "w_ple_gate", "w_ple_proj"]


def make_in_maps(inputs, n_cores, SEQ):
    f = lambda a: np.ascontiguousarray(np.asarray(a, dtype=np.float32))
    shared = {}
    for k in _IN_NAMES:
        if k in ("x", "p"):
            continue
        a = f(inputs[k])
        if k == "hg_lb_logits":
            shared[k] = a.reshape(2, 512)
        else:
            shared[k] = a.reshape(a.shape[-2] if a.ndim >= 2 else 1, a.shape[-1])
    xs = f(inputs["x"])
    ps = f(inputs["p"])[0]
    maps = []
    for c in range(n_cores):
        m = dict(shared)
        m["x"] = np.ascontiguousarray(xs[c])
        m["p"] = np.ascontiguousarray(ps[c])
        maps.append(m)
    return maps


def kernel(**inputs):
    x = np.asarray(inputs["x"])
    B, SEQ, _ = x.shape
    nc = build_program(SEQ)
    in_maps = make_in_maps(inputs, B, SEQ)
    res = run_bass_kernel_spmd(nc, in_maps, core_ids=list(range(B)))
    outs = [np.asarray(r["out"], dtype=np.float32) for r in res.results]
    return np.stack(outs, axis=0)
```
